# Optimizing a Trainium2 kernel written in Bass

```python
import math
import jax, jax.numpy as jnp
from jax import lax
import numpy as np


D_MODEL = 1024
BATCH = 32
SEQ = 2048
DEPTH = 2

PLE_DIM = 256
EPS = 1e-6
ROPE_THETA = 10000.0

CONV_WIDTH = D_MODEL // 2
CONV_K = 3
ATT_HEADS = 8
ATT_HEAD_DIM = 64
ATT_WIDTH = ATT_HEADS * ATT_HEAD_DIM
IDX_HEADS = 8
IDX_DIM = 64
IDX_ROPE_DIM = 32
TOPK_MAX = 256
Q_BLOCK = 128
DN_HEADS = 8
DN_DK = 128
DN_DV = 128
DN_CONV_K = 4
DN_CHUNK = 64
D_FF = -(-8 * D_MODEL // (3 * 256)) * 256

EVEN_SPLITS = (CONV_WIDTH, CONV_WIDTH, CONV_WIDTH, ATT_WIDTH, ATT_WIDTH, ATT_WIDTH, IDX_HEADS * IDX_DIM, IDX_DIM, IDX_HEADS)
EVEN_IN = sum(EVEN_SPLITS)
EVEN_MIX = CONV_WIDTH + ATT_WIDTH
DN_QKV = 2 * DN_HEADS * DN_DK + DN_HEADS * DN_DV
ODD_SPLITS = (DN_QKV, DN_HEADS * DN_DV, DN_HEADS, DN_HEADS)
ODD_IN = sum(ODD_SPLITS)
ODD_MIX = DN_HEADS * DN_DV
N_EVEN = (DEPTH + 1) // 2
N_ODD = DEPTH // 2

kernel_name = 'hybrid_conv_dsa_deltanet_block'


def split_cols(z, sizes):
    return jnp.split(z, [int(c) for c in np.cumsum(sizes)[:-1]], axis=-1)


def rmsnorm(x, g):
    xf = x.astype(jnp.float32)
    y = xf * lax.rsqrt(jnp.mean(xf * xf, axis=-1, keepdims=True) + EPS)
    return (y * g.astype(jnp.float32)).astype(x.dtype)


def layernorm(x, g, b):
    xf = x.astype(jnp.float32)
    mu = jnp.mean(xf, axis=-1, keepdims=True)
    var = jnp.mean(jnp.square(xf - mu), axis=-1, keepdims=True)
    y = (xf - mu) * lax.rsqrt(var + EPS)
    return (y * g.astype(jnp.float32) + b.astype(jnp.float32)).astype(x.dtype)


def l2norm(x):
    return x * lax.rsqrt(jnp.sum(x * x, axis=-1, keepdims=True) + EPS)


def rope_tables(positions, dim):
    inv_freq = 1.0 / (ROPE_THETA ** (jnp.arange(0, dim, 2, dtype=jnp.float32) / dim))
    ang = positions.astype(jnp.float32)[..., None] * inv_freq
    return jnp.cos(ang), jnp.sin(ang)


def apply_rope(x, cos, sin):
    x1, x2 = jnp.split(x.astype(jnp.float32), 2, axis=-1)
    c = cos[:, :, None, :]
    s = sin[:, :, None, :]
    return jnp.concatenate([x1 * c - x2 * s, x1 * s + x2 * c], axis=-1).astype(x.dtype)


def partial_rope(x, cos, sin):
    return jnp.concatenate([apply_rope(x[..., :IDX_ROPE_DIM], cos, sin), x[..., IDX_ROPE_DIM:]], axis=-1)


def causal_dwconv(u, w):
    k, s = w.shape[0], u.shape[1]
    up = jnp.pad(u, ((0, 0), (k - 1, 0), (0, 0)))
    y = up[:, 0:s] * w[0]
    for j in range(1, k):
        y = y + up[:, j:j + s] * w[j]
    return y


def dsa_attention(q, k, v, qi, ki, wi):
    b_, s_, _, dh = q.shape
    topk = min(TOPK_MAX, s_ // 4)
    nb = s_ // Q_BLOCK
    key_pos = jnp.arange(s_)
    wi = wi * (IDX_HEADS ** -0.5 * IDX_DIM ** -0.5)

    def to_blocks(t):
        return jnp.swapaxes(t.reshape(b_, nb, Q_BLOCK, *t.shape[2:]), 0, 1)

    def block(args):
        qb, qib, wib, start = args
        qpos = start + jnp.arange(Q_BLOCK)
        causal = key_pos[None, :] <= qpos[:, None]
        rel = jax.nn.relu(jnp.einsum('bqhd,bkd->bqhk', qib, ki).astype(jnp.float32))
        score = jnp.einsum('bqhk,bqh->bqk', rel, wib.astype(jnp.float32))
        score = jnp.where(causal[None], score, -jnp.inf)
        _, idx = lax.top_k(score, topk)
        valid = idx <= qpos[None, :, None]
        kg = jax.vmap(lambda kk, ii: kk[ii])(k, idx)
        vg = jax.vmap(lambda vv, ii: vv[ii])(v, idx)
        logits = jnp.einsum('bqhd,bqnhd->bhqn', qb, kg).astype(jnp.float32) * (dh ** -0.5)
        logits = jnp.where(valid[:, None], logits, -jnp.inf)
        prob = jax.nn.softmax(logits, axis=-1)
        return jnp.einsum('bhqn,bqnhd->bqhd', prob.astype(vg.dtype), vg)

    starts = jnp.arange(nb, dtype=jnp.int32) * Q_BLOCK
    out = lax.map(block, (to_blocks(q), to_blocks(qi), to_blocks(wi), starts))
    return jnp.swapaxes(out, 0, 1).reshape(b_, s_, q.shape[2], dh)


def conv_dsa_mixer(h, cos_a, sin_a, cos_i, sin_i, w_in, conv_w, q_norm_g, k_norm_g, ik_ln_g, ik_ln_b, w_out):
    b_, s_, _ = h.shape
    zb, zc, zh, q, k, v, qi, ki, wi = split_cols(h @ w_in, EVEN_SPLITS)
    y_a = zb * causal_dwconv(zc * zh, conv_w)
    heads = lambda t: t.reshape(b_, s_, ATT_HEADS, ATT_HEAD_DIM)
    q = apply_rope(rmsnorm(heads(q), q_norm_g), cos_a, sin_a)
    k = apply_rope(rmsnorm(heads(k), k_norm_g), cos_a, sin_a)
    v = heads(v)
    qi = partial_rope(qi.reshape(b_, s_, IDX_HEADS, IDX_DIM), cos_i, sin_i)
    ki = partial_rope(layernorm(ki, ik_ln_g, ik_ln_b)[:, :, None, :], cos_i, sin_i)[:, :, 0, :]
    y_b = dsa_attention(q, k, v, qi, ki, wi).reshape(b_, s_, ATT_WIDTH)
    return jnp.concatenate([y_a, y_b], axis=-1) @ w_out


def gated_delta_rule(q, k, v, g, beta):
    b_, s_, h_, dk = q.shape
    dv = v.shape[-1]
    c = DN_CHUNK
    n = s_ // c
    q = l2norm(q) * (dk ** -0.5)
    k = l2norm(k)

    def chunks(t):
        t = jnp.moveaxis(t, 2, 1)
        return t.reshape(b_, h_, n, c, *t.shape[3:])

    q, k, v, g, beta = chunks(q), chunks(k), chunks(v), chunks(g), chunks(beta)
    g = jnp.cumsum(g, axis=-1)
    tril = jnp.tril(jnp.ones((c, c), bool))
    strict = jnp.tril(jnp.ones((c, c), bool), -1)
    decay = jnp.exp(jnp.where(tril, g[..., :, None] - g[..., None, :], -jnp.inf))
    kb = k * beta[..., None]
    vb = v * beta[..., None]
    m = jnp.where(strict, jnp.einsum('bhncd,bhnsd->bhncs', kb, k) * decay, 0.0)
    eye = jnp.eye(c, dtype=q.dtype)
    t_inv = lax.linalg.triangular_solve(eye + m, jnp.broadcast_to(eye, m.shape), left_side=True, lower=True, unit_diagonal=True)
    u = t_inv @ vb
    w = t_inv @ (kb * jnp.exp(g)[..., None])
    qk = jnp.where(tril, jnp.einsum('bhncd,bhnsd->bhncs', q, k) * decay, 0.0)

    def step(state, xs):
        q_c, k_c, u_c, w_c, g_c, qk_c = xs
        v_new = u_c - w_c @ state
        o = (q_c * jnp.exp(g_c)[..., None]) @ state + qk_c @ v_new
        g_last = g_c[..., -1]
        k_dec = k_c * jnp.exp(g_last[..., None] - g_c)[..., None]
        state = state * jnp.exp(g_last)[..., None, None] + jnp.einsum('bhcd,bhce->bhde', k_dec, v_new)
        return state, o

    xs = tuple(jnp.moveaxis(t, 2, 0) for t in (q, k, u, w, g, qk))
    state0 = jnp.zeros((b_, h_, dk, dv), q.dtype)
    _, o = lax.scan(step, state0, xs)
    o = jnp.moveaxis(o, 0, 2).reshape(b_, h_, s_, dv)
    return jnp.moveaxis(o, 1, 2)


def gated_deltanet_mixer(h, w_in, conv_w, a_log, dt_bias, o_norm_g, w_out):
    b_, s_, _ = h.shape
    qkv, z, a, bt = split_cols(h @ w_in, ODD_SPLITS)
    qkv = jax.nn.silu(causal_dwconv(qkv, conv_w))
    q, k, v = split_cols(qkv, (DN_HEADS * DN_DK, DN_HEADS * DN_DK, DN_HEADS * DN_DV))
    f32 = jnp.float32
    q = q.reshape(b_, s_, DN_HEADS, DN_DK).astype(f32)
    k = k.reshape(b_, s_, DN_HEADS, DN_DK).astype(f32)
    v = v.reshape(b_, s_, DN_HEADS, DN_DV).astype(f32)
    g = -jnp.exp(a_log.astype(f32)) * jax.nn.softplus(a.astype(f32) + dt_bias.astype(f32))
    beta = jax.nn.sigmoid(bt.astype(f32))
    o = gated_delta_rule(q, k, v, g, beta)
    o = rmsnorm(o, o_norm_g) * jax.nn.silu(z.reshape(b_, s_, DN_HEADS, DN_DV).astype(f32))
    return o.reshape(b_, s_, ODD_MIX).astype(h.dtype) @ w_out


def swiglu(h, w_gate, w_up, w_down):
    return (jax.nn.silu(h @ w_gate) * (h @ w_up)) @ w_down


def setup_inputs(seed: int = 0) -> dict:
    key = jax.random.key(seed)
    k = jax.random.split(key, 25)
    f32 = jnp.float32
    nrm = lambda kk, shape, scale: jax.random.normal(kk, shape, f32) * scale
    gain = lambda kk, shape: 1.0 + 0.02 * jax.random.normal(kk, shape, f32)
    x = nrm(k[0], (BATCH, SEQ, D_MODEL), 1.0)
    p = nrm(k[1], (DEPTH, BATCH, SEQ, PLE_DIM), 1.0)
    offset = jax.random.randint(k[2], (BATCH, 1), 0, 4096, dtype=jnp.int32)
    positions = offset + jnp.arange(SEQ, dtype=jnp.int32)[None, :]
    norm_mix_g = gain(k[3], (DEPTH, D_MODEL))
    norm_ffn_g = gain(k[4], (DEPTH, D_MODEL))
    ev_w_in = nrm(k[5], (N_EVEN, D_MODEL, EVEN_IN), D_MODEL ** -0.5)
    ev_conv_w = nrm(k[6], (N_EVEN, CONV_K, CONV_WIDTH), CONV_K ** -0.5)
    ev_q_norm_g = gain(k[7], (N_EVEN, ATT_HEAD_DIM))
    ev_k_norm_g = gain(k[8], (N_EVEN, ATT_HEAD_DIM))
    ev_ik_ln_g = gain(k[9], (N_EVEN, IDX_DIM))
    ev_ik_ln_b = nrm(k[10], (N_EVEN, IDX_DIM), 0.02)
    ev_w_out = nrm(k[11], (N_EVEN, EVEN_MIX, D_MODEL), 0.5 * EVEN_MIX ** -0.5)
    od_w_in = nrm(k[12], (N_ODD, D_MODEL, ODD_IN), D_MODEL ** -0.5)
    od_conv_w = nrm(k[13], (N_ODD, DN_CONV_K, DN_QKV), 0.5)
    dt = jnp.exp(jax.random.uniform(k[14], (N_ODD, DN_HEADS), f32, math.log(1e-3), math.log(1e-1)))
    od_dt_bias = dt + jnp.log(-jnp.expm1(-dt))
    od_a_log = jnp.log(jax.random.uniform(k[15], (N_ODD, DN_HEADS), f32, 1.0, 16.0))
    od_o_norm_g = gain(k[16], (N_ODD, DN_DV))
    od_w_out = nrm(k[17], (N_ODD, ODD_MIX, D_MODEL), 0.5 * ODD_MIX ** -0.5)
    ffn_w_gate = nrm(k[18], (DEPTH, D_MODEL, D_FF), D_MODEL ** -0.5)
    ffn_w_up = nrm(k[19], (DEPTH, D_MODEL, D_FF), D_MODEL ** -0.5)
    ffn_w_down = nrm(k[20], (DEPTH, D_FF, D_MODEL), 0.5 * D_FF ** -0.5)
    ple_w_proj = nrm(k[21], (DEPTH, PLE_DIM, D_MODEL), PLE_DIM ** -0.5)
    ple_post_norm_g = gain(k[22], (DEPTH, D_MODEL))
    ple_norm_g = gain(k[23], (DEPTH, D_MODEL))
    ple_w_gate = nrm(k[24], (DEPTH, D_MODEL, D_MODEL), D_MODEL ** -0.5)
    return {'x': x, 'p': p, 'positions': positions, 'norm_mix_g': norm_mix_g, 'norm_ffn_g': norm_ffn_g,
            'ev_w_in': ev_w_in, 'ev_conv_w': ev_conv_w, 'ev_q_norm_g': ev_q_norm_g, 'ev_k_norm_g': ev_k_norm_g,
            'ev_ik_ln_g': ev_ik_ln_g, 'ev_ik_ln_b': ev_ik_ln_b, 'ev_w_out': ev_w_out,
            'od_w_in': od_w_in, 'od_conv_w': od_conv_w, 'od_a_log': od_a_log, 'od_dt_bias': od_dt_bias,
            'od_o_norm_g': od_o_norm_g, 'od_w_out': od_w_out,
            'ffn_w_gate': ffn_w_gate, 'ffn_w_up': ffn_w_up, 'ffn_w_down': ffn_w_down,
            'ple_w_proj': ple_w_proj, 'ple_post_norm_g': ple_post_norm_g, 'ple_norm_g': ple_norm_g, 'ple_w_gate': ple_w_gate}


def reference(x, p, positions, norm_mix_g, norm_ffn_g, ev_w_in, ev_conv_w, ev_q_norm_g, ev_k_norm_g, ev_ik_ln_g, ev_ik_ln_b, ev_w_out, od_w_in, od_conv_w, od_a_log, od_dt_bias, od_o_norm_g, od_w_out, ffn_w_gate, ffn_w_up, ffn_w_down, ple_w_proj, ple_post_norm_g, ple_norm_g, ple_w_gate):
    cos_a, sin_a = rope_tables(positions, ATT_HEAD_DIM)
    cos_i, sin_i = rope_tables(positions, IDX_ROPE_DIM)
    h = x
    for i in range(DEPTH):
        j = i // 2
        hn = rmsnorm(h, norm_mix_g[i])
        if i % 2 == 0:
            h = h + conv_dsa_mixer(hn, cos_a, sin_a, cos_i, sin_i, ev_w_in[j], ev_conv_w[j], ev_q_norm_g[j],
                                   ev_k_norm_g[j], ev_ik_ln_g[j], ev_ik_ln_b[j], ev_w_out[j])
        else:
            h = h + gated_deltanet_mixer(hn, od_w_in[j], od_conv_w[j], od_a_log[j], od_dt_bias[j],
                                         od_o_norm_g[j], od_w_out[j])
        h = h + swiglu(rmsnorm(h, norm_ffn_g[i]), ffn_w_gate[i], ffn_w_up[i], ffn_w_down[i])
        e = rmsnorm(p[i] @ ple_w_proj[i], ple_post_norm_g[i])
        h = h + e * jax.nn.sigmoid(rmsnorm(h, ple_norm_g[i]) @ ple_w_gate[i])
    return h
```

```python
import contextlib
import math
import os
import numpy as np
import concourse.bass as bass
import concourse.mybir as mybir
from concourse.bass_utils import run_bass_kernel_spmd

DT = mybir.dt
F32, BF16, I32 = DT.float32, DT.bfloat16, DT.int32
ALU = mybir.AluOpType
AF = mybir.ActivationFunctionType
AX = mybir.AxisListType

ENGS = ("pe", "act", "dve", "pool", "sp")
N_DMA_SEMS = 40


class _Op:
    __slots__ = ("eng", "fn", "reads", "writes", "pos", "dma", "waits", "signal",
                 "obs", "dsem", "dval", "gidx", "tick")


class Prog:
    def __init__(self, nc):
        self.nc = nc
        self.ops = []
        self.streams = {e: [] for e in ENGS}
        self.last_w = {}
        self.readers = {}
        self.n_dma = 0
        self.n_dma_sw = 0
        self.dma_last = {}
        self.dma_count = {}
        self.pending_dma = []

    def op(self, eng, fn, reads=(), writes=(), dma=False, extra=()):
        mx = int(os.environ.get("KDBG_MAXOPS", "0"))
        if mx and len(self.ops) >= mx and not self._force:
            return None
        o = _Op()
        o.eng, o.fn, o.dma = eng, fn, dma
        o.reads, o.writes = tuple(reads), tuple(writes)
        o.gidx = len(self.ops)
        o.pos = len(self.streams[eng])
        o.signal = False
        o.dsem = o.dval = None
        o.tick = 0
        deps = set(extra)
        for k in o.reads:
            w = self.last_w.get(k)
            if w is not None:
                deps.add(w)
            if k.startswith("ps"):
                for r in self.readers.get(k, ()):
                    if self.ops[r].eng != eng:
                        deps.add(r)
        for k in o.writes:
            w = self.last_w.get(k)
            if w is not None:
                deps.add(w)
            for r in self.readers.get(k, ()):
                deps.add(r)
        if dma:
            if eng == "pool" and not os.environ.get("KDBG_SHARED"):
                s = self.n_dma_sw % 8
                self.n_dma_sw += 1
            else:
                s = 8 + self.n_dma % (N_DMA_SEMS - 8)
                self.n_dma += 1
            prev = self.dma_last.get(s)
            if prev is not None:
                deps.add(prev)
            self.dma_last[s] = o.gidx
            self.dma_count[s] = self.dma_count.get(s, 0) + 1
            o.dsem, o.dval = s, 16 * self.dma_count[s]
            self.pending_dma.append(o.gidx)
        deps.discard(o.gidx)
        o.waits = deps
        for k in o.writes:
            self.last_w[k] = o.gidx
            self.readers[k] = []
        for k in o.reads:
            self.readers.setdefault(k, []).append(o.gidx)
        self.ops.append(o)
        self.streams[eng].append(o)
        return o

    _force = False

    def barrier(self):
        self._force = True
        dm = list(self.pending_dma)
        self.pending_dma = []
        firsts = []
        for e in ENGS:
            o = self.op(e, lambda eng: eng.drain(), extra=dm if e == "sp" else ())
            firsts.append(o.gidx)
        for e in ENGS:
            self.op(e, lambda eng: None, extra=firsts)
        self.last_w = {}
        self.readers = {}
        self._force = False

    def _analyze(self):
        ops = self.ops
        cur = {e: ({e2: -1 for e2 in ENGS}, {}) for e in ENGS}
        for o in ops:
            eobs, dobs = cur[o.eng]
            eobs = dict(eobs)
            dobs = dict(dobs)
            need = []
            for d in o.waits:
                a = ops[d]
                if a.dma:
                    if dobs.get(a.dsem, 0) >= a.dval:
                        continue
                    need.append(a)
                else:
                    if a.eng == o.eng and o.eng == "pe":
                        continue
                    if eobs[a.eng] >= a.pos:
                        continue
                    need.append(a)
            final = []
            for a in sorted(need, key=lambda a: -a.gidx):
                if a.dma:
                    if dobs.get(a.dsem, 0) >= a.dval:
                        continue
                else:
                    if eobs[a.eng] >= a.pos:
                        continue
                final.append(a)
                a.signal = True
                aeo, ado = a.obs
                for e2 in ENGS:
                    if aeo[e2] > eobs[e2]:
                        eobs[e2] = aeo[e2]
                for s, v in ado.items():
                    if v > dobs.get(s, 0):
                        dobs[s] = v
                if a.dma:
                    dobs[a.dsem] = max(dobs.get(a.dsem, 0), a.dval)
                else:
                    eobs[a.eng] = max(eobs[a.eng], a.pos)
            o.waits = final
            o.obs = (eobs, dobs)
            cur[o.eng] = (eobs, dobs)

    def emit(self):
        nc = self.nc
        self._analyze()
        es = contextlib.ExitStack()
        esem = {e: es.enter_context(nc.semaphore("tick_" + e)) for e in ENGS}
        dsem = [es.enter_context(nc.semaphore("dma%d" % i)) for i in range(N_DMA_SEMS)]
        for e in ENGS:
            c = 0
            for o in self.streams[e]:
                if o.dma:
                    continue
                if o.signal:
                    c += 1
                o.tick = c
        hw = {"pe": "tensor", "act": "scalar", "dve": "vector", "pool": "gpsimd", "sp": "sync"}
        blk = es.enter_context(nc.Block())
        stats = {"waits": 0, "ins": 0}

        def make(e):
            def body(eng):
                for o in self.streams[e]:
                    for a in o.waits:
                        if a.dma:
                            eng.wait_ge(dsem[a.dsem], a.dval)
                        else:
                            eng.wait_ge(esem[a.eng], a.tick)
                        stats["waits"] += 1
                    ins = o.fn(eng)
                    stats["ins"] += 1
                    if ins is None:
                        if o.signal or o.dma:
                            raise RuntimeError("signalling op without instruction")
                        continue
                    if o.dma:
                        ins.then_inc(dsem[o.dsem], 16)
                    elif o.signal:
                        ins.then_inc(esem[e], 1)
            return body
        for e in ENGS:
            getattr(blk, hw[e])(make(e))
        es.close()
        self.stats = stats


D = 1024
KC = 8
DFF = 2816
FCN = 22
PLE = 256
EV_IN = 3656
OD_IN = 4112
EPS = 1e-6
NIT = 16
NEG = -1.0e30
PEN = 1.0e4


def _split(n):
    for s in (1, 2, 4, 8, 16):
        if n % s == 0 and n // s <= 2048:
            return s
    raise ValueError(n)


class Builder:
    def __init__(self, S, NSEQ, topk, layers=(0, 1), stages=None, dbg=False):
        self.S, self.NSEQ, self.topk = S, NSEQ, topk
        self.NT, self.NG = S // 128, S // 512
        self.layers = layers
        self.stages = stages
        self.dbg = dbg
        self.nc = nc = bass.Bass("TRN2", target_bir_lowering=False)
        self.p = Prog(nc)
        self.es = contextlib.ExitStack()
        self._rot = 0
        self.nrot = 5
        self._decl()
        self._alloc()

    def _decl(self):
        nc, S, NSEQ = self.nc, self.S, self.NSEQ
        di = lambda n, s, d=F32: nc.dram_tensor(n, list(s), d, kind="ExternalInput").ap()
        self.x = di("x", [NSEQ, S, D])
        self.pin = di("p", [2, NSEQ, S, PLE])
        self.pos = di("positions", [NSEQ, S], I32)
        self.w = {}
        for n, s in (("norm_mix_g", [2, D]), ("norm_ffn_g", [2, D]), ("ev_w_in", [D, EV_IN]),
                     ("ev_conv_w", [3, 512]), ("ev_q_norm_g", [1, 64]), ("ev_k_norm_g", [1, 64]),
                     ("ev_ik_ln_g", [1, 64]), ("ev_ik_ln_b", [1, 64]), ("ev_w_out", [D, D]),
                     ("od_w_in", [D, OD_IN]), ("od_conv_w", [4, 3072]), ("od_a_log", [1, 8]),
                     ("od_dt_bias", [1, 8]), ("od_o_norm_g", [1, 128]), ("od_w_out", [D, D]),
                     ("ffn_w_gate", [2, D, DFF]), ("ffn_w_up", [2, D, DFF]), ("ffn_w_down", [2, DFF, D]),
                     ("ple_w_proj", [2, PLE, D]), ("ple_post_norm_g", [2, D]), ("ple_norm_g", [2, D]),
                     ("ple_w_gate", [2, D, D])):
            self.w[n] = di(n, s)
        self.out = nc.dram_tensor("out", [NSEQ, S, D], F32, kind="ExternalOutput").ap()
        if self.dbg:
            self.dbg_out = nc.dram_tensor("dbg", [8, NSEQ, S, D], F32, kind="ExternalOutput").ap()
        ds = lambda n, s: nc.dram_tensor(n, list(s), BF16, kind="Internal").ap()
        self.wb = {
            "ev_w_in": ds("b_ev_w_in", [D, EV_IN]), "ev_w_out": ds("b_ev_w_out", [D, D]),
            "od_w_in": ds("b_od_w_in", [D, OD_IN]), "od_w_out": ds("b_od_w_out", [D, D]),
            "ffn_w_gate": ds("b_ffn_w_gate", [2, D, DFF]), "ffn_w_up": ds("b_ffn_w_up", [2, D, DFF]),
            "ffn_w_down": ds("b_ffn_w_down", [2, DFF, D]), "ple_w_proj": ds("b_ple_w_proj", [2, PLE, D]),
            "ple_w_gate": ds("b_ple_w_gate", [2, D, D]),
        }

    def _alloc(self):
        nc, S, NT = self.nc, self.S, self.NT
        sb = lambda n, s, d: self.es.enter_context(nc.sbuf_tensor(n, list(s), d))
        self.H = sb("H", [128, NT, D], F32)
        self.XT = sb("XT", [128, KC, S], BF16)
        self.YT = sb("YT", [128, KC, S], BF16)
        self.AR_N = 27136
        self.AR = sb("AR", [128, self.AR_N], BF16)
        self.WB = sb("WB", [128, 4096], BF16)
        self.GREP = sb("GREP", [128, D], F32)
        self.IDB = sb("IDB", [128, 128], BF16)
        self.CM = sb("CM", [128, 7, 128], F32)
        self.G64 = sb("G64", [128, 4, 64], F32)
        self.ONG = sb("ONG", [128, 128], F32)
        self.AD = sb("AD", [128, 2, 8], F32)
        self.CW0 = sb("CW0", [128, 3, 4], F32)
        self.CW1 = sb("CW1", [128, 4, 24], F32)
        self.INV = sb("INV", [128, 48], F32)
        self.P2 = sb("P2", [128, 2, NIT], F32)
        self.CST = sb("CST", [128, 8], F32)
        ya = self.YT[:, 0:4, :].rearrange("p k t -> p (k t)")
        self.YA = ya
        self.SIN = ya[:, 0:NT * 96].bitcast(F32).rearrange("p (t f) -> p t f", f=48)
        self.COS = ya[:, NT * 96:NT * 192].bitcast(F32).rearrange("p (t f) -> p t f", f=48)
        self.ST = sb("ST", [128, 512], F32)
        self.WI = sb("WI", [128, NT, 8], F32)
        self.JK = sb("JK", [128, 2048], BF16)
        self.ps = [self.es.enter_context(nc.psum_tensor("ps%d" % i, [128, 512], F32)) for i in range(8)]

    def pbank(self):
        self._pb = getattr(self, "_pb", 0) + 1
        return 6 + self._pb % 2

    def bank(self):
        nrot = self.nrot
        i = self._rot % nrot
        self._rot += 1
        return i

    def MM(self, out, lhsT, rhs, start=True, stop=True, r=(), w=()):
        self.p.op("pe", lambda e: e.matmul(out, lhsT=lhsT, rhs=rhs, start=start, stop=stop), r, w)

    def TR(self, out, in_, r=(), w=()):
        idb = self.IDB[:]
        self.p.op("pe", lambda e: e.transpose(out=out, in_=in_, identity=idb), list(r) + ["IDB"], w)

    def ACT(self, out, in_, func, r=(), w=(), bias=None, scale=None, accum=None):
        kw = {}
        if bias is not None:
            kw["bias"] = bias
        if scale is not None:
            kw["scale"] = scale
        if accum is not None:
            kw["accum_out"] = accum
        self.p.op("act", lambda e: e.activation(out=out, in_=in_, func=func, **kw), r, w)

    def TS(self, out, in0, s1, s2, op0, op1=None, r=(), w=(), accum=None, eng="dve"):
        kw = {}
        if op1 is not None:
            kw["op1"] = op1
        if accum is not None:
            kw["accum_out"] = accum
        self.p.op(eng, lambda e: e.tensor_scalar(out=out, in0=in0, scalar1=s1, scalar2=s2, op0=op0, **kw), r, w)

    def TT(self, out, in0, in1, op, r=(), w=(), eng="dve"):
        self.p.op(eng, lambda e: e.tensor_tensor(out=out, in0=in0, in1=in1, op=op), r, w)

    def STT(self, out, in0, scalar, in1, op0, op1, r=(), w=()):
        self.p.op("dve", lambda e: e.scalar_tensor_tensor(out=out, in0=in0, scalar=scalar, in1=in1, op0=op0, op1=op1), r, w)

    def CP(self, out, in_, r=(), w=(), eng="dve"):
        if eng == "act":
            self.p.op("act", lambda e: e.copy(out=out, in_=in_), r, w)
        else:
            self.p.op(eng, lambda e: e.tensor_copy(out=out, in_=in_), r, w)

    def MS(self, ap, val, w=(), eng="pool"):
        self.p.op(eng, lambda e: e.memset(ap, val), (), w)

    def DMA(self, out, in_, r=(), w=(), eng="sp", nc_ok=False):
        w = [w] if isinstance(w, str) else list(w)
        if nc_ok:
            self.p.op(eng, lambda e: e.dma_start(out=out, in_=in_, allow_slow_non_contiguous=True), r, w, dma=True)
        else:
            self.p.op(eng, lambda e: e.dma_start(out=out, in_=in_), r, w, dma=True)

    def negreg(self, e):
        if getattr(self, "_negreg", None) is None:
            self._negreg = e.to_reg(NEG)
        return self._negreg

    def arv(self, off, n, dt=BF16):
        ne = n * (2 if dt == F32 else 1)
        assert off + ne <= self.AR_N, (off, ne)
        v = self.AR[:, off:off + ne]
        return v.bitcast(F32) if dt == F32 else v

    def prologue(self):
        p = self.p
        for n, dst in self.wb.items():
            src = self.w[n]
            if len(src.shape) == 3:
                pairs = [(src[l], dst[l]) for l in range(src.shape[0])]
            else:
                pairs = [(src, dst)]
            for s_, d_ in pairs:
                ns = _split(s_.shape[1])
                sv = s_.rearrange("k (s n) -> (k s) n", s=ns)
                dv = d_.rearrange("k (s n) -> (k s) n", s=ns)
                rows = sv.shape[0]
                step = 1024
                for r0 in range(0, rows, step):
                    r1 = min(rows, r0 + step)
                    self.DMA(dv[r0:r1, :], sv[r0:r1, :], w=["wb_" + n], eng="pool")
        self.MS(self.IDB[:], 1.0, w=["IDB"])
        idb = self.IDB[:]
        p.op("pool", lambda e: e.affine_select(out=idb, in_=idb, pattern=[[-1, 128]], compare_op=ALU.is_equal,
                                               fill=0.0, base=0, channel_multiplier=1), ["IDB"], ["IDB"])
        cm = self.CM
        self.MS(cm[:, 0, :], 1.0, w=["CM"])
        self.MS(cm[:, 1, :], 1.0, w=["CM"])
        u2 = cm[:, 1, :]
        p.op("pool", lambda e: e.affine_select(out=u2, in_=u2, pattern=[[1, 128]], compare_op=ALU.is_ge,
                                               fill=0.0, base=0, channel_multiplier=-1), ["CM"], ["CM"])
        self.MS(cm[0:64, 1, 64:128], 0.0, w=["CM"])
        self.MS(cm[:, 2, :], 0.0, w=["CM"])
        self.MS(cm[0:64, 2, 0:64], 1.0, w=["CM"])
        self.MS(cm[64:128, 2, 64:128], 1.0, w=["CM"])
        self.MS(cm[:, 3, :], 0.0, w=["CM"])
        self.MS(cm[0:64, 3, :], 1.0, w=["CM"])
        self.MS(cm[:, 4, :], 0.0, w=["CM"])
        self.MS(cm[64:128, 4, :], 1.0, w=["CM"])
        self.MS(cm[:, 5, :], 0.0, w=["CM"])
        ps_ = cm[:, 5, :]
        p.op("pool", lambda e: e.affine_select(out=ps_, in_=ps_, pattern=[[-1, 128]], compare_op=ALU.is_ge,
                                               fill=PEN, base=-1, channel_multiplier=1), ["CM"], ["CM"])
        self.MS(cm[64:128, 5, 0:64], PEN, w=["CM"])
        self.MS(cm[:, 6, :], 0.0, w=["CM"])
        pi_ = cm[:, 6, :]
        p.op("pool", lambda e: e.affine_select(out=pi_, in_=pi_, pattern=[[1, 128]], compare_op=ALU.is_ge,
                                               fill=PEN, base=0, channel_multiplier=-1), ["CM"], ["CM"])
        self.MS(cm[0:64, 6, 64:128], PEN, w=["CM"])
        self.MS(self.CST[:, 0:1], EPS, w=["CST"])
        self.MS(self.CST[:, 1:2], 1.0, w=["CST"])
        self.MS(self.CST[:, 2:3], 0.0, w=["CST"])
        inv_a = (1.0 / (np.float32(10000.0) ** (np.arange(0, 64, 2, dtype=np.float32) / np.float32(64)))).astype(np.float32)
        inv_i = (1.0 / (np.float32(10000.0) ** (np.arange(0, 32, 2, dtype=np.float32) / np.float32(32)))).astype(np.float32)
        for i, v in enumerate(list(inv_a) + list(inv_i)):
            self.MS(self.INV[:, i:i + 1], float(v), w=["INV"])
        for i in range(NIT):
            self.MS(self.P2[:, 0, i:i + 1], 2.0 ** -(i + 2), w=["P2"])
            self.MS(self.P2[:, 1, i:i + 1], 2.0 ** -(i + 1), w=["P2"])
        w = self.w
        for i, n in enumerate(("ev_q_norm_g", "ev_k_norm_g", "ev_ik_ln_g", "ev_ik_ln_b")):
            self.DMA(self.G64[:, i, :], w[n][0:1, :].to_broadcast([128, 64]), w=["G64"])
        self.DMA(self.ONG[:], w["od_o_norm_g"][0:1, :].to_broadcast([128, 128]), w=["ONG"])
        self.DMA(self.AD[:, 0, :], w["od_a_log"][0:1, :].to_broadcast([128, 8]), w=["AD"])
        self.DMA(self.AD[:, 1, :], w["od_dt_bias"][0:1, :].to_broadcast([128, 8]), w=["AD"])
        for k_ in range(3):
            self.DMA(self.CW0[:, k_, :], w["ev_conv_w"][k_].rearrange("(c p) -> p c", p=128), w=["CW0"], nc_ok=True)
        for k_ in range(4):
            self.DMA(self.CW1[:, k_, :], w["od_conv_w"][k_].rearrange("(c p) -> p c", p=128), w=["CW1"], nc_ok=True)
        self.ACT(self.AD[:, 0, :], self.AD[:, 0, :], AF.Exp, r=["AD"], w=["AD"])
        self.TS(self.AD[:, 0, :], self.AD[:, 0, :], -1.0, None, ALU.mult, r=["AD"], w=["AD"])
        self.p.barrier()

    def load_w(self, dst, name, r0, r1, c0, c1, l=None, wkey=()):
        src = self.wb[name]
        if l is not None:
            src = src[l]
        self.DMA(dst, src[r0:r1, c0:c1].rearrange("(kc p) n -> p kc n", p=128), r=["wb_" + name], w=wkey)

    WBK = ["WB0", "WB1"]

    def rope_tables(self, b):
        S, NT = self.S, self.NT
        ar = 0
        POSI = self.ST[:, 0:NT].bitcast(I32)
        POSF = self.ST[:, 16:16 + NT]
        self.DMA(POSI, self.pos[b].rearrange("(n p) -> p n", p=128), w=["ST"], nc_ok=True)
        self.CP(POSF, POSI, r=["ST"], w=["ST"])
        n = NT * 48
        ANG = self.arv(ar, n, F32).rearrange("p (t f) -> p t f", f=48); ar += 2 * n
        KF = self.arv(ar, n, F32).rearrange("p (t f) -> p t f", f=48); ar += 2 * n
        KI = self.arv(ar, n, F32).bitcast(I32).rearrange("p (t f) -> p t f", f=48); ar += 2 * n
        R2 = self.arv(ar, n, F32).rearrange("p (t f) -> p t f", f=48); ar += 2 * n
        k = ["ropetmp"]
        self.TT(ANG, self.INV[:].unsqueeze(1).to_broadcast([128, NT, 48]),
                POSF.unsqueeze(2).to_broadcast([128, NT, 48]), ALU.mult, r=["INV", "ST"], w=k)
        self.TS(KF, ANG, 1.0 / (2 * math.pi), None, ALU.mult, r=k, w=k)
        self.CP(KI, KF, r=k, w=k)
        self.CP(KF, KI, r=k, w=k)
        C1 = 6.28125
        C2 = 2 * math.pi - C1
        self.STT(ANG, KF, -C1, ANG, ALU.mult, ALU.add, r=k, w=k)
        self.STT(ANG, KF, -C2, ANG, ALU.mult, ALU.add, r=k, w=k)

        def wrap(T):
            self.TS(KF, T, math.pi, 2 * math.pi, ALU.is_gt, ALU.mult, r=k, w=k)
            self.TT(T, T, KF, ALU.subtract, r=k, w=k)
            self.TS(KF, T, -math.pi, 2 * math.pi, ALU.is_lt, ALU.mult, r=k, w=k)
            self.TT(T, T, KF, ALU.add, r=k, w=k)
        wrap(ANG)
        self.TS(R2, ANG, math.pi / 2, None, ALU.add, r=k, w=k)
        wrap(R2)
        self.ACT(self.SIN, ANG, AF.Sin, r=k, w=["SIN"])
        self.ACT(self.COS, R2, AF.Sin, r=k, w=["COS"])

    def norm_T(self, gname, l):
        NT = self.NT
        gslot = self.GREP[:]
        self.DMA(gslot, self.w[gname][l:l + 1, :].to_broadcast([128, D]), w=["GREP"])
        SS = self.ST[:, 32:32 + NT]
        RS = self.ST[:, 48:48 + NT]
        for i in range(NT):
            self.ACT(self.JK[:, 0:D], self.H[:, i, :], AF.Square, r=["H%d" % i], w=["JK", "ST_ss%d" % i], accum=SS[:, i:i + 1])
        self.ACT(RS, SS, AF.Sqrt, r=["ST_ss%d" % i for i in range(NT)], w=["ST_rs"], scale=1.0 / D, bias=self.CST[:, 0:1])
        self.p.op("dve", lambda e: e.reciprocal(out=RS, in_=RS), ["ST_rs"], ["ST_rs"])
        for i in range(NT):
            hn = self.JK[:, D:2 * D] if i % 2 == 0 else self.JK[:, 0:D]
            hk = "JKb" if i % 2 == 0 else "JK"
            self.STT(hn, self.H[:, i, :], RS[:, i:i + 1], gslot, ALU.mult, ALU.mult, r=["H%d" % i, "ST_rs", "GREP"], w=[hk])
            b = self.bank()
            pb = self.ps[b][:].bitcast(BF16)
            for kc in range(KC):
                self.TR(pb[:, kc * 128:(kc + 1) * 128], hn[:, kc * 128:(kc + 1) * 128], r=[hk], w=["ps%d" % b])
            self.CP(self.XT[:, :, i * 128:(i + 1) * 128], pb.rearrange("p (k t) -> p k t", t=128),
                    r=["ps%d" % b], w=["XT%d" % i], eng="act")

    def all_xt(self):
        return ["XT%d" % i for i in range(self.NT)]

    def conv_part(self):
        S, NT, NG = self.S, self.NT, self.NG
        ar = 0
        U = self.arv(ar, S + 2, F32); ar += 2 * (S + 2)
        ZB = self.arv(ar, S, F32); ar += 2 * S
        ACC = self.arv(ar, S, F32); ar += 2 * S
        ZC = self.arv(ar, 1024, F32).rearrange("p (a n) -> p a n", a=2); ar += 2048
        self.MS(U[:, 0:2], 0.0, w=["U"], eng="dve")
        xts = self.all_xt()
        wv = self.WB[:, 0:3 * KC * 128].rearrange("p (j k n) -> p j k n", j=3, k=KC)
        for c in range(4):
            for j, base in enumerate((512, 1024, 0)):
                self.load_w(wv[:, j], "ev_w_in", 0, D, base + c * 128, base + (c + 1) * 128, wkey=self.WBK)
            for g in range(NG):
                tok = slice(g * 512, (g + 1) * 512)
                for j in range(3):
                    b = self.bank()
                    for kc in range(KC):
                        self.MM(self.ps[b][:], wv[:, j, kc, :], self.XT[:, kc, tok], kc == 0, kc == KC - 1,
                                r=self.WBK + xts[g * 4:(g + 1) * 4], w=["ps%d" % b])
                    if j == 0:
                        self.CP(ZC[:, g % 2, :], self.ps[b][:], r=["ps%d" % b], w=["ZC%d" % (g % 2)], eng="act")
                    elif j == 1:
                        self.TT(U[:, 2 + g * 512:2 + (g + 1) * 512], self.ps[b][:], ZC[:, g % 2, :], ALU.mult,
                                r=["ps%d" % b, "ZC%d" % (g % 2)], w=["U"])
                    else:
                        self.CP(ZB[:, tok], self.ps[b][:], r=["ps%d" % b], w=["ZB"], eng="act")
            cw = self.CW0
            self.TS(ACC, U[:, 2:S + 2], cw[:, 2, c:c + 1], None, ALU.mult, r=["U", "CW0"], w=["ACC"])
            self.STT(ACC, U[:, 1:S + 1], cw[:, 1, c:c + 1], ACC, ALU.mult, ALU.add, r=["U", "CW0", "ACC"], w=["ACC"])
            self.STT(ACC, U[:, 0:S], cw[:, 0, c:c + 1], ACC, ALU.mult, ALU.add, r=["U", "CW0", "ACC"], w=["ACC"])
            self.TT(self.YT[:, c, :], ACC, ZB, ALU.mult, r=["ACC", "ZB"], w=["YTa%d" % c])

    def rope(self, dst, src, nh, half, f0, i, tmp, rk, wk):
        c = self.COS[:, i, f0:f0 + half].unsqueeze(1).to_broadcast([128, nh, half])
        s = self.SIN[:, i, f0:f0 + half].unsqueeze(1).to_broadcast([128, nh, half])
        x1 = src[:, :, 0:half]
        x2 = src[:, :, half:2 * half]
        t1 = tmp[:, 0:nh * half].rearrange("p (h d) -> p h d", h=nh)
        t2 = tmp[:, nh * half:2 * nh * half].rearrange("p (h d) -> p h d", h=nh)
        rr = list(rk) + ["SIN", "COS"]
        k1, k2 = "rt1_%d" % self._rp, "rt2_%d" % self._rp
        self.TT(t1, x1, c, ALU.mult, r=rr, w=[k1])
        self.TT(t2, x2, s, ALU.mult, r=rr, w=[k2])
        self.TT(dst[:, :, 0:half], t1, t2, ALU.subtract, r=[k1, k2], w=wk)
        self.TT(t1, x1, s, ALU.mult, r=rr, w=[k1])
        self.TT(t2, x2, c, ALU.mult, r=rr, w=[k2])
        self.TT(dst[:, :, half:2 * half], t1, t2, ALU.add, r=[k1, k2], w=wk)

    def l0_layout(self):
        S, NT = self.S, self.NT
        ar = 0
        L = {}
        L["QT"] = self.YT[:, 4:8, :]
        L["KT"] = self.arv(ar, 4 * S).rearrange("p (h t) -> p h t", h=4); ar += 4 * S
        L["QIT"] = self.arv(ar, 4 * S).rearrange("p (h t) -> p h t", h=4); ar += 4 * S
        L["V"] = self.arv(ar, NT * 8 * 65).rearrange("p (t h d) -> p t h d", t=NT, h=8); ar += NT * 8 * 65
        L["KIT"] = self.arv(ar, S); ar += S
        L["end"] = ar
        return L

    def proj_T(self, L):
        S, NT = self.S, self.NT
        o = NT * 192
        TMPs, QBFs = [], []
        for par in range(2):
            if par == 0 or S >= 2048:
                TMPs.append(self.YA[:, o:o + 2048].bitcast(F32)); o += 2048
                QBFs.append(self.YA[:, o:o + 512]); o += 512
            else:
                e0 = L["end"] + (L["end"] % 2)
                TMPs.append(self.AR[:, e0:e0 + 2048].bitcast(F32))
                QBFs.append(self.AR[:, e0 + 2048:e0 + 2560])
        SQs = [self.JK[:, 0:1024].bitcast(F32), self.JK[:, 1024:2048].bitcast(F32)]
        self.MS(L["V"][:, :, :, 64:65], 1.0, w=["Vones"], eng="dve")
        blocks = (("q", 1536, 512), ("k", 2048, 512), ("v", 2560, 512), ("qi", 3072, 512), ("kw", 3584, 72))
        for bi, (nm, c0, ncol) in enumerate(blocks):
            wk = self.WBK
            wv = self.WB[:, 0:KC * ncol].rearrange("p (k n) -> p k n", k=KC)
            self.load_w(wv, "ev_w_in", 0, D, c0, c0 + ncol, wkey=wk)
            for i in range(NT):
                tk = slice(i * 128, (i + 1) * 128)
                b = self.bank()
                pk = "ps%d" % b
                psv = self.ps[b][:, 0:ncol]
                for kc in range(KC):
                    self.MM(psv, self.XT[:, kc, tk], wv[:, kc, :], kc == 0, kc == KC - 1, r=wk + ["XT%d" % i], w=[pk])
                pp = i % 2
                TMP, QBF, SQ = TMPs[pp], QBFs[pp], SQs[pp]
                QF = TMP[:, 0:512].rearrange("p (h d) -> p h d", h=8)
                RT = TMP[:, 512:1024]
                kJK, kQF, kJQ, kSQ = "JK%d" % pp, "QF%d" % pp, "JKq%d" % pp, "ST_q%d" % pp
                self._rp = pp
                if nm in ("q", "k"):
                    gi = 0 if nm == "q" else 1
                    ps3 = psv.rearrange("p (h d) -> p h d", h=8)
                    self.ACT(SQ, psv, AF.Square, r=[pk], w=[kJK])
                    SSQ = self.ST[:, 64 + 8 * pp:72 + 8 * pp]
                    self.p.op("dve", lambda e, SSQ=SSQ, SQ=SQ: e.tensor_reduce(out=SSQ, in_=SQ.rearrange("p (h d) -> p h d", h=8),
                                                                               axis=AX.X, op=ALU.add), [kJK], [kSQ])
                    self.ACT(SSQ, SSQ, AF.Sqrt, r=[kSQ], w=[kSQ], scale=1.0 / 64, bias=self.CST[:, 0:1])
                    self.p.op("dve", lambda e, SSQ=SSQ: e.reciprocal(out=SSQ, in_=SSQ), [kSQ], [kSQ])
                    self.TT(QF, ps3, SSQ.unsqueeze(2).to_broadcast([128, 8, 64]), ALU.mult, r=[pk, kSQ], w=[kQF])
                    self.TT(QF, QF, self.G64[:, gi, :].unsqueeze(1).to_broadcast([128, 8, 64]), ALU.mult, r=[kQF, "G64"], w=[kQF])
                    QB = QBF.rearrange("p (h d) -> p h d", h=8)
                    self.rope(QB, QF, 8, 32, 0, i, RT, [kQF], [kJQ])
                    b2 = self.bank()
                    pb = self.ps[b2][:].bitcast(BF16)
                    for hp in range(4):
                        self.TR(pb[:, hp * 128:(hp + 1) * 128], QBF[:, hp * 128:(hp + 1) * 128], r=[kJQ], w=["ps%d" % b2])
                    dst = L["QT" if nm == "q" else "KT"]
                    self.CP(dst[:, :, tk], pb[:, 0:512].rearrange("p (h t) -> p h t", h=4), r=["ps%d" % b2],
                            w=["%s%d" % ("QT" if nm == "q" else "KT", i)], eng="act")
                elif nm == "v":
                    self.CP(L["V"][:, i, :, 0:64], psv.rearrange("p (h d) -> p h d", h=8), r=[pk], w=["V%d" % i], eng="act")
                elif nm == "qi":
                    ps3 = psv.rearrange("p (h d) -> p h d", h=8)
                    QB = QBF.rearrange("p (h d) -> p h d", h=8)
                    self.rope(QB, ps3, 8, 16, 32, i, RT, [pk], [kJQ])
                    self.CP(QB[:, :, 32:64], ps3[:, :, 32:64], r=[pk], w=[kJQ], eng="act")
                    b2 = self.bank()
                    pb = self.ps[b2][:].bitcast(BF16)
                    for hp in range(4):
                        self.TR(pb[:, hp * 128:(hp + 1) * 128], QBF[:, hp * 128:(hp + 1) * 128], r=[kJQ], w=["ps%d" % b2])
                    self.CP(L["QIT"][:, :, tk], pb[:, 0:512].rearrange("p (h t) -> p h t", h=4), r=["ps%d" % b2], w=["QIT%d" % i], eng="act")
                else:
                    BN = self.ST[:, 80 + 16 * pp:86 + 16 * pp]
                    MV = self.ST[:, 88 + 16 * pp:90 + 16 * pp]
                    kip = psv[:, 0:64]
                    self.p.op("dve", lambda e, BN=BN, kip=kip: e.bn_stats(out=BN, in_=kip), [pk], ["ST_bn%d" % pp])
                    self.p.op("dve", lambda e, BN=BN, MV=MV: e.bn_aggr(out=MV, in_=BN), ["ST_bn%d" % pp], ["ST_mv%d" % pp])
                    RSD = self.ST[:, 90 + 16 * pp:91 + 16 * pp]
                    self.ACT(RSD, MV[:, 1:2], AF.Sqrt, r=["ST_mv%d" % pp], w=["ST_rsd%d" % pp], bias=self.CST[:, 0:1])
                    self.p.op("dve", lambda e, RSD=RSD: e.reciprocal(out=RSD, in_=RSD), ["ST_rsd%d" % pp], ["ST_rsd%d" % pp])
                    KF_ = TMP[:, 0:64]
                    self.TS(KF_, kip, MV[:, 0:1], RSD, ALU.subtract, ALU.mult, r=[pk, "ST_mv%d" % pp, "ST_rsd%d" % pp], w=[kQF])
                    self.TT(KF_, KF_, self.G64[:, 2, :], ALU.mult, r=[kQF, "G64"], w=[kQF])
                    self.TT(KF_, KF_, self.G64[:, 3, :], ALU.add, r=[kQF, "G64"], w=[kQF])
                    KB_ = QBF[:, 0:128]
                    self.rope(KB_[:, 0:64].unsqueeze(1), KF_.unsqueeze(1), 1, 16, 32, i, RT, [kQF], [kJQ])
                    self.CP(KB_[:, 32:64], KF_[:, 32:64], r=[kQF], w=[kJQ], eng="act")
                    self.CP(KB_[:, 64:128], KB_[:, 0:64], r=[kJQ], w=[kJQ], eng="dve")
                    b2 = self.bank()
                    pb = self.ps[b2][:].bitcast(BF16)
                    self.TR(pb[:, 0:128], KB_, r=[kJQ], w=["ps%d" % b2])
                    self.CP(L["KIT"][:, tk], pb[:, 0:128], r=["ps%d" % b2], w=["KIT%d" % i], eng="act")
                    self.TS(self.WI[:, i, :], psv[:, 64:72], (8 ** -0.5) * (64 ** -0.5), None, ALU.mult, r=[pk], w=["WI%d" % i])

    def attention(self, L):
        S, NT, topk = self.S, self.NT, self.topk
        QT, KT, QIT, V, KIT = L["QT"], L["KT"], L["QIT"], L["V"], L["KIT"]
        xt = self.XT[:].rearrange("p k t -> p (k t)")
        o = 0
        SC, MSK, MT = [], [], []
        for par in range(2):
            SC.append(xt[:, o:o + 2 * S].bitcast(F32)); o += 2 * S
        for par in range(2):
            MSK.append(xt[:, o:o + S]); o += S
        for par in range(2):
            MT.append(xt[:, o:o + S].rearrange("p (k q) -> p k q", q=128)); o += S
        o2 = 0
        RL = self.YA[:, o2:o2 + 2048].bitcast(F32).rearrange("p (a n) -> p a n", a=2); o2 += 2048
        PT = self.YA[:, o2:o2 + 2048].rearrange("p (a n) -> p a n", a=4); o2 += 2048
        YB = self.JK[:, 0:512]
        st = self.ST
        RD = st[:, 150:158]
        OB = (5, 6)
        meng = "pool" if os.environ.get("KDBG_POOLMASK", "1") == "1" else "dve"

        def score(j):
            par = j % 2
            sc, sk = SC[par], "SC%d" % par
            qk = slice(j * 128, (j + 1) * 128)
            nkb = j + 1
            n = nkb * 128
            nch = (nkb + 3) // 4
            MX, MN = st[:, 100 + par:101 + par], st[:, 102 + par:103 + par]
            for ch in range(nch):
                k0 = ch * 512
                kw = min(512, n - k0)
                kkeys = ["KIT%d" % t for t in range(ch * 4, min(nkb, ch * 4 + 4))]
                for h in range(8):
                    hp, hh = h // 2, h % 2
                    pr = slice(hh * 64, (hh + 1) * 64)
                    b = self.bank()
                    pk = "ps%d" % b
                    self.MM(self.ps[b][:, 0:kw], QIT[pr, hp, qk], KIT[pr, k0:k0 + kw], r=["QIT%d" % j] + kkeys, w=[pk])
                    rs = h % 2
                    self.ACT(RL[:, rs, 0:kw], self.ps[b][:, 0:kw], AF.Relu, r=[pk], w=["RL%d" % rs])
                    if h == 0:
                        self.TS(sc[:, k0:k0 + kw], RL[:, rs, 0:kw], self.WI[:, j, 0:1], None, ALU.mult,
                                r=["RL%d" % rs, "WI%d" % j], w=[sk])
                    else:
                        self.STT(sc[:, k0:k0 + kw], RL[:, rs, 0:kw], self.WI[:, j, h:h + 1], sc[:, k0:k0 + kw], ALU.mult, ALU.add,
                                 r=["RL%d" % rs, "WI%d" % j, sk], w=[sk])
                    yield
            if n > topk:
                self.p.op("dve", lambda e, MX=MX, s_=sc[:, 0:n]: e.tensor_reduce(out=MX, in_=s_, axis=AX.X, op=ALU.max), [sk], ["bs_mx%d" % par])
                yield
                self.p.op("dve", lambda e, MN=MN, s_=sc[:, 0:n]: e.tensor_reduce(out=MN, in_=s_, axis=AX.X, op=ALU.min), [sk], ["bs_mn%d" % par])
                yield
            dg = sc[:, j * 128:(j + 1) * 128]
            self.p.op("pool", lambda e, dg=dg: e.affine_select(out=dg, in_=dg, pattern=[[-1, 128]], compare_op=ALU.is_ge,
                                                               fill=self.negreg(e), base=0, channel_multiplier=1), [sk], [sk])
            yield

        def attn(j):
            par = j % 2
            sc, sk = SC[par], "SC%d" % par
            msk, mk = MSK[par], "MSK%d" % par
            mt, mtk = MT[par], "MT%d" % par
            qk = slice(j * 128, (j + 1) * 128)
            nkb = j + 1
            n = nkb * 128
            nch = (nkb + 3) // 4
            MX, MN = st[:, 100 + par:101 + par], st[:, 102 + par:103 + par]
            RG, TH, CNT, DD, THF = (st[:, 104 + i:105 + i] for i in range(5))
            STEP = st[:, 112:112 + NIT]
            STEP2 = st[:, 128:128 + NIT]
            if n > topk:
                self.TT(RG, MX, MN, ALU.subtract, r=["bs_mx%d" % par, "bs_mn%d" % par], w=["bs_rg"])
                self.TS(STEP, self.P2[:, 0, :], RG, None, ALU.mult, r=["P2", "bs_rg"], w=["bs_step"])
                self.TS(STEP2, self.P2[:, 1, :], RG, None, ALU.mult, r=["P2", "bs_rg"], w=["bs_step2"])
                self.STT(TH, MX, 1.0, MN, ALU.mult, ALU.add, r=["bs_mx%d" % par, "bs_mn%d" % par], w=["bs_th"])
                self.TS(TH, TH, 0.5, None, ALU.mult, r=["bs_th"], w=["bs_th"])
                yield
                for it in range(NIT):
                    self.TS(msk[:, 0:n], sc[:, 0:n], TH, None, ALU.is_ge, ALU.add, r=[sk, "bs_th"], w=[mk, "bs_cnt"], accum=CNT)
                    self.TS(DD, CNT, float(topk) - 0.5, STEP2[:, it:it + 1], ALU.is_ge, ALU.mult, r=["bs_cnt", "bs_step2"], w=["bs_dd"])
                    self.STT(TH, DD, STEP[:, it:it + 1], TH, ALU.subtract, ALU.add, r=["bs_dd", "bs_step", "bs_th"], w=["bs_th"])
                    yield
                self.STT(THF, RG, -(2.0 ** -(NIT + 1)), TH, ALU.mult, ALU.add, r=["bs_rg", "bs_th"], w=["bs_thf"])
                self.TS(msk[:, 0:n], sc[:, 0:n], THF, None, ALU.is_ge, r=[sk, "bs_thf"], w=[mk])
            else:
                self.TS(msk[:, 0:n], sc[:, 0:n], 0.5 * NEG, None, ALU.is_ge, r=[sk], w=[mk])
            yield
            for g0 in range(0, nkb, 8):
                g1 = min(nkb, g0 + 8)
                b = self.bank()
                pb = self.ps[b][:].bitcast(BF16)
                for kb in range(g0, g1):
                    self.TR(pb[:, (kb - g0) * 128:(kb - g0 + 1) * 128], msk[:, kb * 128:(kb + 1) * 128], r=[mk], w=["ps%d" % b])
                self.CP(mt[:, g0:g1, :], pb[:, 0:(g1 - g0) * 128].rearrange("p (k q) -> p k q", q=128), r=["ps%d" % b], w=[mtk], eng="act")
                yield

        def attc(j):
            par = j % 2
            mt, mtk = MT[par], "MT%d" % par
            qk = slice(j * 128, (j + 1) * 128)
            nkb = j + 1
            nch = (nkb + 3) // 4
            items = [(h, ch) for h in range(8) for ch in range(nch)]

            def front(k):
                h, ch = items[k]
                hp, hh = h // 2, h % 2
                pr = slice(hh * 64, (hh + 1) * 64)
                kb0 = ch * 4
                kb1 = min(nkb, kb0 + 4)
                kw = (kb1 - kb0) * 128
                b = self.bank()
                pk = "ps%d" % b
                for kb in range(kb0, kb1):
                    self.MM(self.ps[b][:, (kb - kb0) * 128:(kb - kb0 + 1) * 128], KT[pr, hp, kb * 128:(kb + 1) * 128], QT[pr, hp, qk],
                            r=["KT%d" % kb, "QT%d" % j], w=[pk])
                ps_ = k % 4
                self.ACT(PT[:, ps_, 0:kw], self.ps[b][:, 0:kw], AF.Exp, r=[pk], w=["PT%d" % ps_], scale=0.125)
                self.TT(PT[:, ps_, 0:kw], PT[:, ps_, 0:kw], mt[:, kb0:kb1, :].rearrange("p k q -> p (k q)"), ALU.mult,
                        r=["PT%d" % ps_, mtk], w=["PT%d" % ps_], eng=meng)

            def back(k):
                h, ch = items[k]
                ob = OB[h // 4]
                ov = self.ps[ob][:, 0:260].rearrange("p (h d) -> p h d", h=4)[:, h % 4, :]
                kb0 = ch * 4
                kb1 = min(nkb, kb0 + 4)
                ps_ = k % 4
                for kb in range(kb0, kb1):
                    self.MM(ov, PT[:, ps_, (kb - kb0) * 128:(kb - kb0 + 1) * 128], V[:, kb, h, :], kb == 0, kb == nkb - 1,
                            r=["PT%d" % ps_, "V%d" % kb, "Vones"], w=["ps%d" % ob])
            front(0)
            if len(items) > 1:
                front(1)
            yield
            for k in range(len(items)):
                if k + 2 < len(items):
                    front(k + 2)
                back(k)
                yield
            for half in range(2):
                ob = OB[half]
                o3 = self.ps[ob][:, 0:260].rearrange("p (h d) -> p h d", h=4)
                rd = RD[:, half * 4:half * 4 + 4]
                self.p.op("dve", lambda e, rd=rd, o3=o3: e.reciprocal(out=rd, in_=o3[:, :, 64]), ["ps%d" % ob], ["ST_rd%d" % half])
                self.TT(YB[:, half * 256:(half + 1) * 256].rearrange("p (h d) -> p h d", h=4), o3[:, :, 0:64],
                        rd.unsqueeze(2).to_broadcast([128, 4, 64]), ALU.mult, r=["ps%d" % ob, "ST_rd%d" % half], w=["YB"])
            b = self.bank()
            pb = self.ps[b][:].bitcast(BF16)
            for c in range(4):
                self.TR(pb[:, c * 128:(c + 1) * 128], YB[:, c * 128:(c + 1) * 128], r=["YB"], w=["ps%d" % b])
            self.CP(self.YT[:, 4:8, qk], pb[:, 0:512].rearrange("p (c t) -> p c t", c=4), r=["ps%d" % b], w=["YTb%d" % j, "QT%d" % j], eng="act")
            yield

        def merge(gens):
            live = list(gens)
            while live:
                nxt = []
                for g_ in live:
                    try:
                        next(g_)
                        nxt.append(g_)
                    except StopIteration:
                        pass
                live = nxt
        merge([score(0)])
        merge([attn(0)] + ([score(1)] if NT > 1 else []))
        for j in range(NT):
            gl = [attc(j)]
            if j + 1 < NT:
                gl.append(attn(j + 1))
            if j + 2 < NT:
                gl.append(score(j + 2))
            merge(gl)

    def out_proj(self, name, k0, nk, src, rk):
        NT = self.NT
        for cb in range(2):
            wv = self.WB[:, 0:nk * 512].rearrange("p (k n) -> p k n", k=nk)
            self.load_w(wv, name, k0 * 128, (k0 + nk) * 128, cb * 512, (cb + 1) * 512, wkey=self.WBK)
            for i in range(NT):
                b = self.bank()
                for k in range(nk):
                    self.MM(self.ps[b][:], src(k, i), wv[:, k, :], k == 0, k == nk - 1, r=self.WBK + rk(i), w=["ps%d" % b])
                hv = self.H[:, i, cb * 512:(cb + 1) * 512]
                self.TT(hv, self.ps[b][:], hv, ALU.add, r=["ps%d" % b, "H%d" % i], w=["H%d" % i])

    def layer0_mixer(self, b):
        self.norm_T("norm_mix_g", 0)
        self.p.barrier()
        self.conv_part()
        tks = lambda i: slice(i * 128, (i + 1) * 128)
        self.out_proj("ev_w_out", 0, 4, lambda k, i: self.YT[:, k, tks(i)], lambda i: ["YTa%d" % c for c in range(4)])
        self.p.barrier()
        self.rope_tables(b)
        self.p.barrier()
        L = self.l0_layout()
        self.proj_T(L)
        self.p.barrier()
        self.attention(L)
        self.out_proj("ev_w_out", 4, 4, lambda k, i: self.YT[:, 4 + k, tks(i)], lambda i: ["YTb%d" % i])
        self.p.barrier()

    def ffn(self, l):
        S, NT, NG = self.S, self.NT, self.NG
        self.norm_T("norm_ffn_g", l)
        ar = 0
        AT = self.arv(ar, FCN * 512).rearrange("p (f t) -> p f t", f=FCN); ar += FCN * 512
        WD = []
        for s in range(2):
            WD.append(self.arv(ar, FCN * 256).rearrange("p (f n) -> p f n", f=FCN)); ar += FCN * 256
        SG = self.arv(ar, 512, F32); ar += 1024
        nwd = 0
        nfb = 0
        for g in range(NG):
            tok = slice(g * 512, (g + 1) * 512)
            xk = ["XT%d" % i for i in range(g * 4, g * 4 + 4)]
            for f in range(FCN):
                slot = nfb % 2
                nfb += 1
                wk = self.WBK[slot]
                wv = self.WB[:, slot * 2048:(slot + 1) * 2048].rearrange("p (j k n) -> p j k n", j=2, k=KC)
                self.load_w(wv[:, 0], "ffn_w_gate", 0, D, f * 128, (f + 1) * 128, l=l, wkey=[wk])
                self.load_w(wv[:, 1], "ffn_w_up", 0, D, f * 128, (f + 1) * 128, l=l, wkey=[wk])
                bg = self.bank()
                bu = self.bank()
                for kc in range(KC):
                    self.MM(self.ps[bg][:], wv[:, 0, kc, :], self.XT[:, kc, tok], kc == 0, kc == KC - 1, r=[wk] + xk, w=["ps%d" % bg])
                for kc in range(KC):
                    self.MM(self.ps[bu][:], wv[:, 1, kc, :], self.XT[:, kc, tok], kc == 0, kc == KC - 1, r=[wk] + xk, w=["ps%d" % bu])
                self.ACT(SG, self.ps[bg][:], AF.Silu, r=["ps%d" % bg], w=["SG"])
                self.TT(AT[:, f, :], SG, self.ps[bu][:], ALU.mult, r=["SG", "ps%d" % bu], w=["AT%d" % f])
            atk = ["AT%d" % f for f in range(FCN)]
            for cb in range(4):
                slot = nwd % 2
                nwd += 1
                wk = "WD%d" % slot
                src = self.wb["ffn_w_down"][l][:, cb * 256:(cb + 1) * 256].rearrange("(f p) n -> p f n", p=128)
                self.DMA(WD[slot], src, r=["wb_ffn_w_down"], w=[wk])
                for ti in range(4):
                    i = g * 4 + ti
                    b = self.bank()
                    for f in range(FCN):
                        self.MM(self.ps[b][:, 0:256], AT[:, f, ti * 128:(ti + 1) * 128], WD[slot][:, f, :], f == 0, f == FCN - 1,
                                r=[wk] + atk, w=["ps%d" % b])
                    hv = self.H[:, i, cb * 256:(cb + 1) * 256]
                    self.TT(hv, self.ps[b][:, 0:256], hv, ALU.add, r=["ps%d" % b, "H%d" % i], w=["H%d" % i])
        self.p.barrier()

    def ple(self, l, b):
        S, NT = self.S, self.NT
        self.norm_T("ple_norm_g", l)
        ar = 0
        gpost = self.arv(ar, D, F32); ar += 2 * D
        self.DMA(gpost, self.w["ple_post_norm_g"][l:l + 1, :].to_broadcast([128, D]), w=["GPOST"])
        WP = self.arv(ar, 2 * D).rearrange("p (k n) -> p k n", k=2); ar += 2 * D
        self.load_w(WP, "ple_w_proj", 0, PLE, 0, D, l=l, wkey=["WP"])
        PTT = self.arv(ar, 2 * S).rearrange("p (k t) -> p k t", k=2); ar += 2 * S
        PF = []
        PB = []
        for s in range(2):
            PF.append(self.arv(ar, 256, F32)); ar += 512
            PB.append(self.arv(ar, 256)); ar += 256
        SGM = self.arv(ar, 512, F32); ar += 1024
        EE = self.arv(ar, 512, F32); ar += 1024
        RSE = self.ST[:, 176:176 + NT]
        SS2 = self.ST[:, 200:200 + 2 * NT].rearrange("p (t c) -> p t c", c=2)
        eb = (5, 6)
        for i in range(NT):
            s = i % 2
            tk = slice(i * 128, (i + 1) * 128)
            self.DMA(PF[s], self.pin[l, b, i * 128:(i + 1) * 128, :], w=["PF%d" % s])
            self.CP(PB[s], PF[s], r=["PF%d" % s], w=["PB%d" % s], eng="pool")
            bt = self.bank()
            pb = self.ps[bt][:].bitcast(BF16)
            for k in range(2):
                self.TR(pb[:, k * 128:(k + 1) * 128], PB[s][:, k * 128:(k + 1) * 128], r=["PB%d" % s], w=["ps%d" % bt])
            self.CP(PTT[:, :, tk], pb[:, 0:256].rearrange("p (k t) -> p k t", k=2), r=["ps%d" % bt], w=["PTT%d" % i], eng="act")
            for cb in range(2):
                for k in range(2):
                    self.MM(self.ps[eb[cb]][:], PTT[:, k, tk], WP[:, k, cb * 512:(cb + 1) * 512], k == 0, k == 1,
                            r=["PTT%d" % i, "WP"], w=["ps%d" % eb[cb]])
                self.ACT(self.JK[:, 0:1024].bitcast(F32), self.ps[eb[cb]][:], AF.Square, r=["ps%d" % eb[cb]], w=["JK", "ST_sse%d_%d" % (i, cb)],
                         accum=SS2[:, i, cb:cb + 1])
        allss = ["ST_sse%d_%d" % (i, cb) for i in range(NT) for cb in range(2)]
        self.TT(RSE, SS2[:, :, 0], SS2[:, :, 1], ALU.add, r=allss, w=["ST_rse"])
        self.ACT(RSE, RSE, AF.Sqrt, r=["ST_rse"], w=["ST_rse"], scale=1.0 / D, bias=self.CST[:, 0:1])
        self.p.op("dve", lambda e: e.reciprocal(out=RSE, in_=RSE), ["ST_rse"], ["ST_rse"])
        for cb in range(2):
            cs = slice(cb * 512, (cb + 1) * 512)
            wg = self.WB[:].rearrange("p (k n) -> p k n", k=KC)
            self.load_w(wg, "ple_w_gate", 0, D, cb * 512, (cb + 1) * 512, l=l, wkey=self.WBK)
            for i in range(NT):
                tk = slice(i * 128, (i + 1) * 128)
                be = self.bank()
                for k in range(2):
                    self.MM(self.ps[be][:], PTT[:, k, tk], WP[:, k, cs], k == 0, k == 1, r=["PTT%d" % i, "WP"], w=["ps%d" % be])
                self.STT(EE, self.ps[be][:], RSE[:, i:i + 1], gpost[:, cs], ALU.mult, ALU.mult, r=["ps%d" % be, "ST_rse", "GPOST"], w=["EE"])
                bg = self.bank()
                for kc in range(KC):
                    self.MM(self.ps[bg][:], self.XT[:, kc, tk], wg[:, kc, :], kc == 0, kc == KC - 1, r=self.WBK + ["XT%d" % i], w=["ps%d" % bg])
                self.ACT(SGM, self.ps[bg][:], AF.Sigmoid, r=["ps%d" % bg], w=["SGM"])
                self.TT(EE, EE, SGM, ALU.mult, r=["EE", "SGM"], w=["EE"])
                hv = self.H[:, i, cs]
                self.TT(hv, hv, EE, ALU.add, r=["EE", "H%d" % i], w=["H%d" % i])
        self.p.barrier()

    def deltanet(self, b):
        S, NT, NG = self.S, self.NT, self.NG
        CM = self.CM
        ONES, U2, BD, HALF0, HALF1, PEN_S, PEN_IT = (CM[:, i, :] for i in range(7))
        self.norm_T("norm_mix_g", 1)
        ar = 0
        NH = NT * 8

        def f32v(n):
            nonlocal ar
            v = self.arv(ar, n, F32)
            ar += 2 * n
            return v

        def bfv(n):
            nonlocal ar
            v = self.arv(ar, n)
            ar += n + (n % 2)
            return v
        th = lambda v: v.rearrange("p (t h) -> p t h", h=8)
        A, BETA, GT, EG, BEG, EKD, T1, T2 = (th(f32v(NH)) for _ in range(8))
        EGL = f32v(2 * NH).rearrange("p (a t h) -> p a t h", a=2, h=8)
        RAW = f32v(NT * 16).rearrange("p (t n) -> p t n", n=16)
        wab = self.WB[:, 0:KC * 16].rearrange("p (k n) -> p k n", k=KC)
        self.load_w(wab, "od_w_in", 0, D, 4096, 4112, wkey=self.WBK)
        for i in range(NT):
            bk = self.bank()
            for kc in range(KC):
                self.MM(self.ps[bk][:, 0:16], self.XT[:, kc, i * 128:(i + 1) * 128], wab[:, kc, :], kc == 0, kc == KC - 1,
                        r=self.WBK + ["XT%d" % i], w=["ps%d" % bk])
            self.CP(RAW[:, i, :], self.ps[bk][:, 0:16], r=["ps%d" % bk], w=["RAW"], eng="act")
        k = ["dn_small"]
        self.TT(T1, RAW[:, :, 0:8], self.AD[:, 1, :].unsqueeze(1).to_broadcast([128, NT, 8]), ALU.add, r=["RAW", "AD"], w=k)
        self.TS(T2, T1, -1.0, None, ALU.mult, r=k, w=k)
        self.TT(T2, T1, T2, ALU.min, r=k, w=k)
        self.ACT(T2, T2, AF.Exp, r=k, w=k)
        self.ACT(T2, T2, AF.Ln, r=k, w=k, bias=self.CST[:, 1:2])
        self.STT(T1, T1, 0.0, T2, ALU.max, ALU.add, r=k, w=k)
        self.TT(A, T1, self.AD[:, 0, :].unsqueeze(1).to_broadcast([128, NT, 8]), ALU.mult, r=k + ["AD"], w=k)
        self.ACT(BETA, RAW[:, :, 8:16], AF.Sigmoid, r=["RAW"], w=k)
        fl = lambda v: v.rearrange("p t h -> p (t h)")
        Af = fl(A)
        bk = self.bank()
        self.MM(self.ps[bk][:, 0:NH], U2, Af, r=k + ["CM"], w=["ps%d" % bk])
        self.CP(fl(GT), self.ps[bk][:, 0:NH], r=["ps%d" % bk], w=k, eng="act")
        bk = self.bank()
        self.MM(self.ps[bk][:, 0:NH], BD, Af, r=k + ["CM"], w=["ps%d" % bk])
        self.TT(fl(T1), self.ps[bk][:, 0:NH], fl(GT), ALU.subtract, r=["ps%d" % bk] + k, w=k)
        self.ACT(EKD, T1, AF.Exp, r=k, w=k)
        self.ACT(EG, GT, AF.Exp, r=k, w=k)
        self.TT(BEG, BETA, EG, ALU.mult, r=k, w=k)
        for half, HM in enumerate((HALF0, HALF1)):
            bk = self.bank()
            self.MM(self.ps[bk][:, 0:NH], HM, Af, r=k + ["CM"], w=["ps%d" % bk])
            self.ACT(EGL[:, half].rearrange("p t h -> p (t h)"), self.ps[bk][:, 0:NH], AF.Exp, r=["ps%d" % bk], w=k)
        self.nrot = 6
        yt = self.YT[:].rearrange("p k t -> p (k t)")
        yo = 0

        def ytv(n, dt=BF16):
            nonlocal yo
            ne = n * (2 if dt == F32 else 1)
            v = yt[:, yo:yo + ne]
            yo += ne
            assert yo <= 8 * S
            return v.bitcast(F32) if dt == F32 else v
        Sh = S // 2
        NGh = Sh // 512
        QN = [ytv(S), ytv(S)]
        KN = [ytv(S), ytv(S)]
        VT = [ytv(S), ytv(S)]
        ZS = [ytv(S), ytv(S)]
        PRE = [f32v(Sh + 4), f32v(Sh + 4)]
        ACC = [f32v(Sh), f32v(Sh)]
        g4 = lambda v: v.rearrange("p (t c) -> p t c", c=128)
        TA, TB, TC = (f32v(512) for _ in range(3))
        AREP, DM, DMT = TC, TB, TC
        MM_, NN_, MP, NP, RR, VBt, KBGt = (bfv(512) for _ in range(7))
        QG, KD, WT, QKT, UU, YHg = ([bfv(512), bfv(512)] for _ in range(6))
        WO = bfv(1024)
        SF = f32v(128)
        SB_ = bfv(128)
        VN = bfv(128)
        OT = f32v(128)
        OB_ = bfv(128)
        SILg = self.JK[:, 0:1024].bitcast(F32)
        SQ = self.JK[:, 1024:2048].bitcast(F32)
        wv = self.WB[:].rearrange("p (j k n) -> p j k n", j=4, k=KC)
        xts = self.all_xt()
        IDB4 = self.IDB[:].unsqueeze(1).to_broadcast([128, 4, 128])
        NGRP = NT // 4
        c4 = lambda t: slice(t * 128, (t + 1) * 128)
        TC4 = ["TC%d" % t for t in range(4)]
        TB4 = ["TB%d" % t for t in range(4)]

        def Pgen(h):
            hp = h % 2
            qk_, kk_, vk_, zk_ = "QN%d" % hp, "KN%d" % hp, "VT%d" % hp, "ZS%d" % hp
            for j in range(4):
                self.load_w(wv[:, j], "od_w_in", 0, D, j * 1024 + h * 128, j * 1024 + (h + 1) * 128, wkey=self.WBK)
            for g in range(NG):
                tok = slice(g * 512, (g + 1) * 512)
                bk = self.pbank()
                for kc in range(KC):
                    self.MM(self.ps[bk][:], wv[:, 3, kc, :], self.XT[:, kc, tok], kc == 0, kc == KC - 1, r=self.WBK + xts[g * 4:g * 4 + 4], w=["ps%d" % bk])
                    if kc % 2 == 1 and kc < KC - 1:
                        yield
                self.ACT(ZS[hp][:, tok], self.ps[bk][:], AF.Silu, r=["ps%d" % bk], w=[zk_])
                yield
            unit = 0
            for j, nm in enumerate(("q", "k", "v")):
                for hf in range(2):
                    par = unit % 2
                    unit += 1
                    pre, acc = PRE[par], ACC[par]
                    pk_, ak_ = "PRE%d" % par, "ACC%d" % par
                    if hf == 0:
                        self.MS(pre[:, 0:3], 0.0, w=[pk_], eng="dve")
                    else:
                        self.CP(pre[:, 0:3], PRE[1 - par][:, Sh:Sh + 3], r=["PRE%d" % (1 - par)], w=[pk_], eng="dve")
                    for g in range(NGh):
                        gg = hf * NGh + g
                        tok = slice(gg * 512, (gg + 1) * 512)
                        bk = self.pbank()
                        for kc in range(KC):
                            self.MM(self.ps[bk][:], wv[:, j, kc, :], self.XT[:, kc, tok], kc == 0, kc == KC - 1, r=self.WBK + xts[gg * 4:gg * 4 + 4], w=["ps%d" % bk])
                            if kc % 2 == 1 and kc < KC - 1:
                                yield
                        self.CP(pre[:, 3 + g * 512:3 + (g + 1) * 512], self.ps[bk][:], r=["ps%d" % bk], w=[pk_], eng="act")
                        yield
                    cc = j * 8 + h
                    cw = self.CW1
                    self.TS(acc, pre[:, 3:Sh + 3], cw[:, 3, cc:cc + 1], None, ALU.mult, r=[pk_, "CW1"], w=[ak_])
                    for t in range(3):
                        self.STT(acc, pre[:, t:Sh + t], cw[:, t, cc:cc + 1], acc, ALU.mult, ALU.add, r=[pk_, "CW1", ak_], w=[ak_])
                    yield
                    htok = slice(hf * Sh, (hf + 1) * Sh)
                    if nm == "v":
                        self.ACT(VT[hp][:, htok], acc, AF.Silu, r=[ak_], w=[vk_])
                        yield
                        continue
                    dst, dk_ = (QN[hp], qk_) if nm == "q" else (KN[hp], kk_)
                    for g in range(NGh):
                        gg = hf * NGh + g
                        tok = slice(gg * 512, (gg + 1) * 512)
                        self.ACT(SILg, acc[:, g * 512:(g + 1) * 512], AF.Silu, r=[ak_], w=["JK"])
                        self.ACT(SQ, SILg, AF.Square, r=["JK"], w=["JKb"])
                        bk = self.pbank()
                        self.MM(self.ps[bk][:], ONES, SQ, r=["JKb", "CM"], w=["ps%d" % bk])
                        self.ACT(SQ, self.ps[bk][:], AF.Sqrt, r=["ps%d" % bk], w=["JKb"], bias=self.CST[:, 0:1])
                        self.p.op("dve", lambda e, SQ=SQ: e.reciprocal(out=SQ, in_=SQ), ["JKb"], ["JKb"])
                        if nm == "q":
                            self.STT(dst[:, tok], SILg, 128 ** -0.5, SQ, ALU.mult, ALU.mult, r=["JK", "JKb"], w=[dk_])
                        else:
                            self.TT(dst[:, tok], SILg, SQ, ALU.mult, r=["JK", "JKb"], w=[dk_])
                        yield

        def prep(g, h):
            hp = h % 2
            qn, kn, vt = QN[hp], KN[hp], VT[hp]
            qk_, kk_, vk_ = "QN%d" % hp, "KN%d" % hp, "VT%d" % hp
            i0 = 4 * g
            pg = g % 2
            tl = list(enumerate(range(i0, i0 + 4)))
            tok4 = slice(i0 * 128, (i0 + 4) * 128)
            sm = ["dn_small"]
            for t, i in tl:
                self.TS(AREP[:, c4(t)], ONES, A[:, i, h:h + 1], None, ALU.mult, r=["CM"] + sm, w=["TC%d" % t])
            bT = self.bank()
            kT = "ps%d" % bT
            pbT = self.ps[bT][:].bitcast(BF16)
            for t, i in tl:
                self.TR(pbT[:, c4(t)], vt[:, c4(i)], r=[vk_], w=[kT])
                self.TR(pbT[:, 512 + t * 128:512 + (t + 1) * 128], kn[:, c4(i)], r=[kk_], w=[kT])
            yield
            bG = self.bank()
            kG = "ps%d" % bG
            for t, i in tl:
                self.MM(self.ps[bG][:, c4(t)], AREP[:, c4(t)], U2, r=["TC%d" % t, "CM"], w=[kG])
            for t, i in tl:
                kv = pbT[:, 512 + t * 128:512 + (t + 1) * 128]
                self.TS(VBt[:, c4(t)], pbT[:, c4(t)], BETA[:, i, h:h + 1], None, ALU.mult, r=[kT] + sm, w=["VBt%d" % t])
                self.TS(KBGt[:, c4(t)], kv, BEG[:, i, h:h + 1], None, ALU.mult, r=[kT] + sm, w=["KBGt%d" % t])
                self.TS(KD[pg][:, c4(t)], kv, EKD[:, i, h:h + 1], None, ALU.mult, r=[kT] + sm, w=["KD%d_%d" % (pg, t)])
            yield
            self.ACT(TA, self.ps[bG][:], AF.Exp, r=[kG], w=["TA"])
            for t, i in tl:
                self.STT(TB[:, c4(t)], self.ps[bG][:, c4(t)], GT[:, i, h:h + 1], PEN_S, ALU.subtract, ALU.add, r=[kG, "CM", "TA"] + sm, w=["TB%d" % t])
                self.STT(TC[:, c4(t)], self.ps[bG][:, c4(t)], GT[:, i, h:h + 1], PEN_IT, ALU.subtract, ALU.subtract, r=[kG, "CM", "TA"] + sm, w=["TC%d" % t])
            yield
            self.TT(QG[pg], qn[:, tok4], TA, ALU.mult, r=[qk_, "TA"], w=["QG%d" % pg])
            self.ACT(DM, TB, AF.Exp, r=TB4, w=TB4, scale=-1.0)
            self.ACT(DMT, TC, AF.Exp, r=TC4, w=TC4)
            bK = self.bank()
            bQ = self.bank()
            for t, i in tl:
                self.MM(self.ps[bK][:, c4(t)], kn[:, c4(i)], kn[:, c4(i)], r=[kk_], w=["ps%d" % bK])
            for t, i in tl:
                self.MM(self.ps[bQ][:, c4(t)], kn[:, c4(i)], qn[:, c4(i)], r=[kk_, qk_], w=["ps%d" % bQ])
            yield
            for t, i in tl:
                self.STT(MM_[:, c4(t)], self.ps[bK][:, c4(t)], BETA[:, i, h:h + 1], DM[:, c4(t)], ALU.mult, ALU.mult, r=["ps%d" % bK] + TB4 + sm, w=["M%d" % t])
            self.TT(QKT[pg], self.ps[bQ][:], DMT, ALU.mult, r=["ps%d" % bQ] + TC4, w=["QKT%d" % pg])
            yield
            bN = self.bank()
            pbN = self.ps[bN][:].bitcast(BF16)
            for t, i in tl:
                self.TR(pbN[:, c4(t)], MM_[:, c4(t)], r=["M%d" % t], w=["ps%d" % bN])
            self.CP(NN_, pbN[:, 0:512], r=["ps%d" % bN], w=["N"], eng="act")
            self.TT(g4(RR), IDB4, g4(NN_), ALU.subtract, r=["IDB", "N"], w=["R"])
            yield
            cm_, cn_, km, kn_ = MM_, NN_, "M", "N"
            M4 = ["M%d" % t for t in range(4)]
            for lev in range(1, 6):
                last = lev == 5
                if lev % 2 == 1:
                    nm_, nn_, km2, kn2 = MP, NP, "MP", "NP"
                else:
                    nm_, nn_, km2, kn2 = MM_, NN_, "M", "N"
                b1 = self.bank()
                for t, i in tl:
                    self.MM(self.ps[b1][:, c4(t)], cn_[:, c4(t)], cm_[:, c4(t)], r=[km, kn_] + M4, w=["ps%d" % b1])
                if not last:
                    b2 = self.bank()
                    for t, i in tl:
                        self.MM(self.ps[b2][:, c4(t)], cm_[:, c4(t)], cn_[:, c4(t)], r=[km, kn_] + M4, w=["ps%d" % b2])
                self.CP(nm_, self.ps[b1][:], r=["ps%d" % b1], w=[km2] + (M4 if km2 == "M" else []), eng="act")
                if not last:
                    self.CP(nn_, self.ps[b2][:], r=["ps%d" % b2], w=[kn2], eng="dve")
                yield
                b3 = self.bank()
                for t, i in tl:
                    self.MM(self.ps[b3][:, c4(t)], nm_[:, c4(t)], RR[:, c4(t)], r=[km2, "R"], w=["ps%d" % b3])
                self.TT(RR, self.ps[b3][:], RR, ALU.add, r=["ps%d" % b3, "R"], w=["R"])
                yield
                cm_, cn_, km, kn_ = nm_, nn_, km2, kn2
            bU = self.bank()
            bW = self.bank()
            for t, i in tl:
                self.MM(self.ps[bU][:, c4(t)], RR[:, c4(t)], VBt[:, c4(t)], r=["R", "VBt%d" % t], w=["ps%d" % bU])
            for t, i in tl:
                self.MM(self.ps[bW][:, c4(t)], KBGt[:, c4(t)], RR[:, c4(t)], r=["R", "KBGt%d" % t], w=["ps%d" % bW])
            self.CP(UU[pg], self.ps[bU][:], r=["ps%d" % bU], w=["UU%d" % pg], eng="act")
            self.CP(WT[pg], self.ps[bW][:], r=["ps%d" % bW], w=["WT%d" % pg], eng="dve")
            yield

        def rec(g, h):
            hp = h % 2
            i0 = 4 * g
            pg = g % 2
            for t, i in enumerate(range(i0, i0 + 4)):
                tk = c4(i)
                for half in range(2):
                    pr = slice(half * 64, (half + 1) * 64)
                    b1 = self.bank()
                    self.MM(self.ps[b1][:, 0:128], WT[pg][:, c4(t)], SB_, r=["WT%d" % pg, "SB"], w=["ps%d" % b1])
                    self.TT(VN[pr, :], UU[pg][pr, c4(t)], self.ps[b1][pr, 0:128], ALU.subtract, r=["UU%d" % pg, "ps%d" % b1], w=["VN"])
                    b2 = self.bank()
                    self.MM(self.ps[b2][:, 0:128], QG[pg][:, c4(t)], SB_, True, False, r=["QG%d" % pg, "SB"], w=["ps%d" % b2])
                    self.MM(self.ps[b2][:, 0:128], QKT[pg][pr, c4(t)], VN[pr, :], False, True, r=["QKT%d" % pg, "VN"], w=["ps%d" % b2])
                    self.CP(OT[pr, :], self.ps[b2][pr, 0:128], r=["ps%d" % b2], w=["OT"], eng="act")
                    b3 = self.bank()
                    self.MM(self.ps[b3][:, 0:128], KD[pg][pr, c4(t)], VN[pr, :], r=["KD%d_%d" % (pg, t), "VN"], w=["ps%d" % b3])
                    self.STT(SF, SF, EGL[:, half, i, h:h + 1], self.ps[b3][:, 0:128], ALU.mult, ALU.add, r=["SF", "dn_small", "ps%d" % b3], w=["SF"])
                    self.CP(SB_, SF, r=["SF"], w=["SB"], eng="act")
                    yield
                SSO = self.ST[:, 170:171]
                self.ACT(SQ[:, 0:128], OT, AF.Square, r=["OT"], w=["JKb", "ST_sso"], accum=SSO)
                self.ACT(SSO, SSO, AF.Sqrt, r=["ST_sso"], w=["ST_sso"], scale=1.0 / 128, bias=self.CST[:, 0:1])
                self.p.op("dve", lambda e, SSO=SSO: e.reciprocal(out=SSO, in_=SSO), ["ST_sso"], ["ST_sso"])
                self.STT(OB_, OT, SSO, self.ONG[:], ALU.mult, ALU.mult, r=["OT", "ST_sso", "ONG"], w=["OB"])
                bk = self.bank()
                pb = self.ps[bk][:].bitcast(BF16)
                self.TR(pb[:, 0:128], OB_, r=["OB"], w=["ps%d" % bk])
                self.TT(YHg[pg][:, c4(t)], pb[:, 0:128], ZS[hp][:, tk], ALU.mult, r=["ps%d" % bk, "ZS%d" % hp], w=["YH%d_%d" % (pg, t)])
                yield
                for cb in range(2):
                    bo = self.bank()
                    self.MM(self.ps[bo][:], YHg[pg][:, c4(t)], WO[:, cb * 512:(cb + 1) * 512], r=["YH%d_%d" % (pg, t), "WO"], w=["ps%d" % bo])
                    hv = self.H[:, i, cb * 512:(cb + 1) * 512]
                    self.TT(hv, self.ps[bo][:], hv, ALU.add, r=["ps%d" % bo, "H%d" % i], w=["H%d" % i])
                yield

        def alt(ga, gb):
            da = db = False
            while not (da and db):
                if not da:
                    try:
                        next(ga)
                    except StopIteration:
                        da = True
                if not db:
                    try:
                        next(gb)
                    except StopIteration:
                        db = True
                yield

        def Tgen(h):
            self.DMA(WO, self.wb["od_w_out"][h * 128:(h + 1) * 128, :], r=["wb_od_w_out"], w=["WO"])
            self.MS(SF, 0.0, w=["SF"], eng="dve")
            self.MS(SB_, 0.0, w=["SB"], eng="dve")
            for _ in prep(0, h):
                yield
            for g in range(NGRP):
                a_ = prep(g + 1, h) if g + 1 < NGRP else iter(())
                for _ in alt(a_, rec(g, h)):
                    yield

        for _ in Pgen(0):
            pass
        for h in range(8):
            nxt = Pgen(h + 1) if h + 1 < 8 else iter(())
            for _ in alt(Tgen(h), nxt):
                pass
        self.nrot = 5
        self.p.barrier()

    def dump(self, slot, b):
        if self.dbg:
            self.DMA(self.dbg_out[slot, b].rearrange("(n p) d -> p n d", p=128), self.H[:], r=["H%d" % i for i in range(self.NT)], w=["dbg"])

    def build(self):
        NT = self.NT
        self.prologue()
        for b in range(self.NSEQ):
            hk = ["H%d" % i for i in range(NT)]
            step = max(1, NT // 4)
            for i0 in range(0, NT, step):
                self.DMA(self.H[:, i0:i0 + step, :], self.x[b, i0 * 128:(i0 + step) * 128, :].rearrange("(n p) d -> p n d", p=128),
                         w=hk[i0:i0 + step])
            for l in self.layers:
                if l == 0:
                    self.layer0_mixer(b)
                else:
                    self.deltanet(b)
                self.dump(l * 3 + 0, b)
                self.ffn(l)
                self.dump(l * 3 + 1, b)
                self.ple(l, b)
                self.dump(l * 3 + 2, b)
            for i0 in range(0, NT, step):
                self.DMA(self.out[b, i0 * 128:(i0 + step) * 128, :].rearrange("(n p) d -> p n d", p=128), self.H[:, i0:i0 + step, :],
                         r=hk[i0:i0 + step], w=["out"])
            self.p.barrier()
        self.p.emit()
        self.es.close()
        return self.nc


_CACHE = {}


def _get_nc(S, NSEQ, topk):
    key = (S, NSEQ, topk)
    if key not in _CACHE:
        _CACHE[key] = Builder(S, NSEQ, topk).build()
    return _CACHE[key]


def make_in_maps(inputs, ncores):
    x = np.ascontiguousarray(inputs["x"], dtype=np.float32)
    B = x.shape[0]
    nseq = B // ncores
    in_maps = []
    for c in range(ncores):
        sl = slice(c * nseq, (c + 1) * nseq)
        m = {"x": x[sl], "p": np.ascontiguousarray(inputs["p"][:, sl], dtype=np.float32),
             "positions": np.ascontiguousarray(inputs["positions"][sl], dtype=np.int32)}
        for n in ("norm_mix_g", "norm_ffn_g", "ffn_w_gate", "ffn_w_up", "ffn_w_down", "ple_w_proj",
                  "ple_post_norm_g", "ple_norm_g", "ple_w_gate"):
            m[n] = np.ascontiguousarray(inputs[n], dtype=np.float32)
        for n in ("ev_w_in", "ev_w_out", "od_w_in", "od_w_out", "ev_conv_w", "od_conv_w"):
            m[n] = np.ascontiguousarray(inputs[n][0], dtype=np.float32)
        for n in ("ev_q_norm_g", "ev_k_norm_g", "ev_ik_ln_g", "ev_ik_ln_b", "od_a_log", "od_dt_bias", "od_o_norm_g"):
            m[n] = np.ascontiguousarray(inputs[n], dtype=np.float32)
        in_maps.append(m)
    return in_maps


def kernel(**inputs):
    B, S, _ = inputs["x"].shape
    ncores = 8
    nseq = B // ncores
    topk = min(256, S // 4)
    nc = _get_nc(S, nseq, topk)
    in_maps = make_in_maps(inputs, ncores)
    res = run_bass_kernel_spmd(nc, in_maps, core_ids=list(range(ncores)))
    return np.concatenate([np.asarray(r["out"]) for r in res.results], axis=0).astype(np.float32)
```

```python
import contextlib
import math
import os
import numpy as np
import concourse.bass as bass
import concourse.mybir as mybir
from concourse.bass_utils import run_bass_kernel_spmd

DT = mybir.dt
F32, BF16, I32 = DT.float32, DT.bfloat16, DT.int32
ALU = mybir.AluOpType
AF = mybir.ActivationFunctionType
AX = mybir.AxisListType

ENGS = ("pe", "act", "dve", "pool", "sp")
N_DMA_SEMS = 40


class _Op:
    __slots__ = ("eng", "fn", "reads", "writes", "pos", "dma", "waits", "signal",
                 "obs", "dsem", "dval", "gidx", "tick")


class Prog:
    def __init__(self, nc):
        self.nc = nc
        self.ops = []
        self.streams = {e: [] for e in ENGS}
        self.last_w = {}
        self.readers = {}
        self.n_dma = 0
        self.n_dma_sw = 0
        self.dma_last = {}
        self.dma_count = {}
        self.pending_dma = []

    def op(self, eng, fn, reads=(), writes=(), dma=False, extra=()):
        mx = int(os.environ.get("KDBG_MAXOPS", "0"))
        if mx and len(self.ops) >= mx and not self._force:
            return None
        o = _Op()
        o.eng, o.fn, o.dma = eng, fn, dma
        o.reads, o.writes = tuple(reads), tuple(writes)
        o.gidx = len(self.ops)
        o.pos = len(self.streams[eng])
        o.signal = False
        o.dsem = o.dval = None
        o.tick = 0
        deps = set(extra)
        for k in o.reads:
            w = self.last_w.get(k)
            if w is not None:
                deps.add(w)
            if k.startswith("ps"):
                for r in self.readers.get(k, ()):
                    if self.ops[r].eng != eng:
                        deps.add(r)
        for k in o.writes:
            w = self.last_w.get(k)
            if w is not None:
                deps.add(w)
            for r in self.readers.get(k, ()):
                deps.add(r)
        if dma:
            if eng == "pool" and not os.environ.get("KDBG_SHARED"):
                s = self.n_dma_sw % 8
                self.n_dma_sw += 1
            else:
                s = 8 + self.n_dma % (N_DMA_SEMS - 8)
                self.n_dma += 1
            prev = self.dma_last.get(s)
            if prev is not None:
                deps.add(prev)
            self.dma_last[s] = o.gidx
            self.dma_count[s] = self.dma_count.get(s, 0) + 1
            o.dsem, o.dval = s, 16 * self.dma_count[s]
            self.pending_dma.append(o.gidx)
        deps.discard(o.gidx)
        o.waits = deps
        for k in o.writes:
            self.last_w[k] = o.gidx
            self.readers[k] = []
        for k in o.reads:
            self.readers.setdefault(k, []).append(o.gidx)
        self.ops.append(o)
        self.streams[eng].append(o)
        return o

    _force = False

    def barrier(self):
        self._force = True
        dm = list(self.pending_dma)
        self.pending_dma = []
        firsts = []
        for e in ENGS:
            o = self.op(e, lambda eng: eng.drain(), extra=dm if e == "sp" else ())
            firsts.append(o.gidx)
        for e in ENGS:
            self.op(e, lambda eng: None, extra=firsts)
        self.last_w = {}
        self.readers = {}
        self._force = False

    def _analyze(self):
        ops = self.ops
        cur = {e: ({e2: -1 for e2 in ENGS}, {}) for e in ENGS}
        for o in ops:
            eobs, dobs = cur[o.eng]
            eobs = dict(eobs)
            dobs = dict(dobs)
            need = []
            for d in o.waits:
                a = ops[d]
                if a.dma:
                    if dobs.get(a.dsem, 0) >= a.dval:
                        continue
                    need.append(a)
                else:
                    if a.eng == o.eng and o.eng == "pe":
                        continue
                    if eobs[a.eng] >= a.pos:
                        continue
                    need.append(a)
            final = []
            for a in sorted(need, key=lambda a: -a.gidx):
                if a.dma:
                    if dobs.get(a.dsem, 0) >= a.dval:
                        continue
                else:
                    if eobs[a.eng] >= a.pos:
                        continue
                final.append(a)
                a.signal = True
                aeo, ado = a.obs
                for e2 in ENGS:
                    if aeo[e2] > eobs[e2]:
                        eobs[e2] = aeo[e2]
                for s, v in ado.items():
                    if v > dobs.get(s, 0):
                        dobs[s] = v
                if a.dma:
                    dobs[a.dsem] = max(dobs.get(a.dsem, 0), a.dval)
                else:
                    eobs[a.eng] = max(eobs[a.eng], a.pos)
            o.waits = final
            o.obs = (eobs, dobs)
            cur[o.eng] = (eobs, dobs)

    def emit(self):
        nc = self.nc
        self._analyze()
        es = contextlib.ExitStack()
        esem = {e: es.enter_context(nc.semaphore("tick_" + e)) for e in ENGS}
        dsem = [es.enter_context(nc.semaphore("dma%d" % i)) for i in range(N_DMA_SEMS)]
        for e in ENGS:
            c = 0
            for o in self.streams[e]:
                if o.dma:
                    continue
                if o.signal:
                    c += 1
                o.tick = c
        hw = {"pe": "tensor", "act": "scalar", "dve": "vector", "pool": "gpsimd", "sp": "sync"}
        blk = es.enter_context(nc.Block())
        stats = {"waits": 0, "ins": 0}

        def make(e):
            def body(eng):
                attach = os.environ.get("KDBG_ATTACH", "1") == "1"
                for o in self.streams[e]:
                    ws = list(o.waits)
                    probe = None
                    if attach and ws and o.fn is not None and not getattr(o, "noattach", False):
                        probe = ws.pop()
                    for a in ws:
                        if a.dma:
                            eng.wait_ge(dsem[a.dsem], a.dval)
                        else:
                            eng.wait_ge(esem[a.eng], a.tick)
                        stats["waits"] += 1
                    ins = o.fn(eng)
                    stats["ins"] += 1
                    if ins is None:
                        if probe is not None:
                            a = probe
                            if a.dma:
                                eng.wait_ge(dsem[a.dsem], a.dval)
                            else:
                                eng.wait_ge(esem[a.eng], a.tick)
                            stats["waits"] += 1
                        if o.signal or o.dma:
                            raise RuntimeError("signalling op without instruction")
                        continue
                    if probe is not None:
                        a = probe
                        if a.dma:
                            ins._wait_ge(dsem[a.dsem], a.dval)
                        else:
                            ins._wait_ge(esem[a.eng], a.tick)
                    if o.dma:
                        ins.then_inc(dsem[o.dsem], 16)
                    elif o.signal:
                        ins.then_inc(esem[e], 1)
            return body
        for e in ENGS:
            getattr(blk, hw[e])(make(e))
        es.close()
        self.stats = stats


D = 1024
KC = 8
DFF = 2816
FCN = 22
PLE = 256
EV_IN = 3656
OD_IN = 4112
EPS = 1e-6
NIT = 16
NEG = -1.0e30
PEN = 1.0e4


def _split(n):
    for s in (1, 2, 4, 8, 16):
        if n % s == 0 and n // s <= 2048:
            return s
    raise ValueError(n)


class Builder:
    def __init__(self, S, NSEQ, topk, layers=(0, 1), stages=None, dbg=False):
        self.S, self.NSEQ, self.topk = S, NSEQ, topk
        self.NT, self.NG = S // 128, S // 512
        self.layers = layers
        self.stages = stages
        self.dbg = dbg
        self.nc = nc = bass.Bass("TRN2", target_bir_lowering=False)
        self.p = Prog(nc)
        self.es = contextlib.ExitStack()
        self._rot = 0
        self.nrot = 5
        self._decl()
        self._alloc()

    def _decl(self):
        nc, S, NSEQ = self.nc, self.S, self.NSEQ
        di = lambda n, s, d=F32: nc.dram_tensor(n, list(s), d, kind="ExternalInput").ap()
        self.x = di("x", [NSEQ, S, D])
        self.pin = di("p", [2, NSEQ, S, PLE])
        self.pos = di("positions", [NSEQ, S], I32)
        self.w = {}
        for n, s in (("norm_mix_g", [2, D]), ("norm_ffn_g", [2, D]), ("ev_w_in", [D, EV_IN]),
                     ("ev_conv_w", [3, 512]), ("ev_q_norm_g", [1, 64]), ("ev_k_norm_g", [1, 64]),
                     ("ev_ik_ln_g", [1, 64]), ("ev_ik_ln_b", [1, 64]), ("ev_w_out", [D, D]),
                     ("od_w_in", [D, OD_IN]), ("od_conv_w", [4, 3072]), ("od_a_log", [1, 8]),
                     ("od_dt_bias", [1, 8]), ("od_o_norm_g", [1, 128]), ("od_w_out", [D, D]),
                     ("ffn_w_gate", [2, D, DFF]), ("ffn_w_up", [2, D, DFF]), ("ffn_w_down", [2, DFF, D]),
                     ("ple_w_proj", [2, PLE, D]), ("ple_post_norm_g", [2, D]), ("ple_norm_g", [2, D]),
                     ("ple_w_gate", [2, D, D])):
            self.w[n] = di(n, s)
        self.out = nc.dram_tensor("out", [NSEQ, S, D], F32, kind="ExternalOutput").ap()
        if self.dbg:
            self.dbg_out = nc.dram_tensor("dbg", [8, NSEQ, S, D], F32, kind="ExternalOutput").ap()
        ds = lambda n, s: nc.dram_tensor(n, list(s), BF16, kind="Internal").ap()
        self.wb = {
            "ev_w_in": ds("b_ev_w_in", [D, EV_IN]), "ev_w_out": ds("b_ev_w_out", [D, D]),
            "od_w_in": ds("b_od_w_in", [D, OD_IN]), "od_w_out": ds("b_od_w_out", [D, D]),
            "ffn_w_gate": ds("b_ffn_w_gate", [2, D, DFF]), "ffn_w_up": ds("b_ffn_w_up", [2, D, DFF]),
            "ffn_w_down": ds("b_ffn_w_down", [2, DFF, D]), "ple_w_proj": ds("b_ple_w_proj", [2, PLE, D]),
            "ple_w_gate": ds("b_ple_w_gate", [2, D, D]),
        }

    def _alloc(self):
        nc, S, NT = self.nc, self.S, self.NT
        sb = lambda n, s, d: self.es.enter_context(nc.sbuf_tensor(n, list(s), d))
        self.H = sb("H", [128, NT, D], F32)
        self.XT = sb("XT", [128, KC, S], BF16)
        self.YT = sb("YT", [128, KC, S], BF16)
        self.AR_N = 27136
        self.AR = sb("AR", [128, self.AR_N], BF16)
        self.WB = sb("WB", [128, 4096], BF16)
        self.GREP = sb("GREP", [128, D], F32)
        self.IDB = sb("IDB", [128, 128], BF16)
        self.CM = sb("CM", [128, 7, 128], F32)
        self.G64 = sb("G64", [128, 4, 64], F32)
        self.ONG = sb("ONG", [128, 128], F32)
        self.AD = sb("AD", [128, 2, 8], F32)
        self.CW0 = sb("CW0", [128, 3, 4], F32)
        self.CW1 = sb("CW1", [128, 4, 24], F32)
        self.INV = sb("INV", [128, 48], F32)
        self.P2 = sb("P2", [128, 2, NIT], F32)
        self.CST = sb("CST", [128, 8], F32)
        ya = self.YT[:, 0:4, :].rearrange("p k t -> p (k t)")
        self.YA = ya
        self.SIN = ya[:, 0:NT * 96].bitcast(F32).rearrange("p (t f) -> p t f", f=48)
        self.COS = ya[:, NT * 96:NT * 192].bitcast(F32).rearrange("p (t f) -> p t f", f=48)
        self.ST = sb("ST", [128, 512], F32)
        self.WI = sb("WI", [128, NT, 8], F32)
        self.JK = sb("JK", [128, 2048], BF16)
        self.ps = [self.es.enter_context(nc.psum_tensor("ps%d" % i, [128, 512], F32)) for i in range(8)]

    def pbank(self):
        self._pb = getattr(self, "_pb", 0) + 1
        return 6 + self._pb % 2

    def bank(self):
        nrot = self.nrot
        i = self._rot % nrot
        self._rot += 1
        return i

    def MM(self, out, lhsT, rhs, start=True, stop=True, r=(), w=()):
        self.p.op("pe", lambda e: e.matmul(out, lhsT=lhsT, rhs=rhs, start=start, stop=stop), r, w)

    def TR(self, out, in_, r=(), w=()):
        idb = self.IDB[:]
        self.p.op("pe", lambda e: e.transpose(out=out, in_=in_, identity=idb), list(r) + ["IDB"], w)

    def ACT(self, out, in_, func, r=(), w=(), bias=None, scale=None, accum=None):
        kw = {}
        if bias is not None:
            kw["bias"] = bias
        if scale is not None:
            kw["scale"] = scale
        if accum is not None:
            kw["accum_out"] = accum
        self.p.op("act", lambda e: e.activation(out=out, in_=in_, func=func, **kw), r, w)

    def TS(self, out, in0, s1, s2, op0, op1=None, r=(), w=(), accum=None, eng="dve"):
        kw = {}
        if op1 is not None:
            kw["op1"] = op1
        if accum is not None:
            kw["accum_out"] = accum
        self.p.op(eng, lambda e: e.tensor_scalar(out=out, in0=in0, scalar1=s1, scalar2=s2, op0=op0, **kw), r, w)

    def TT(self, out, in0, in1, op, r=(), w=(), eng="dve"):
        self.p.op(eng, lambda e: e.tensor_tensor(out=out, in0=in0, in1=in1, op=op), r, w)

    def STT(self, out, in0, scalar, in1, op0, op1, r=(), w=()):
        self.p.op("dve", lambda e: e.scalar_tensor_tensor(out=out, in0=in0, scalar=scalar, in1=in1, op0=op0, op1=op1), r, w)

    def CP(self, out, in_, r=(), w=(), eng="dve"):
        if eng == "act":
            self.p.op("act", lambda e: e.copy(out=out, in_=in_), r, w)
        else:
            self.p.op(eng, lambda e: e.tensor_copy(out=out, in_=in_), r, w)

    def MS(self, ap, val, w=(), eng="pool"):
        self.p.op(eng, lambda e: e.memset(ap, val), (), w)

    def DMA(self, out, in_, r=(), w=(), eng="sp", nc_ok=False):
        w = [w] if isinstance(w, str) else list(w)
        if nc_ok:
            self.p.op(eng, lambda e: e.dma_start(out=out, in_=in_, allow_slow_non_contiguous=True), r, w, dma=True)
        else:
            self.p.op(eng, lambda e: e.dma_start(out=out, in_=in_), r, w, dma=True)

    def negreg(self, e):
        if getattr(self, "_negreg", None) is None:
            self._negreg = e.to_reg(NEG)
        return self._negreg

    def arv(self, off, n, dt=BF16):
        ne = n * (2 if dt == F32 else 1)
        assert off + ne <= self.AR_N, (off, ne)
        v = self.AR[:, off:off + ne]
        return v.bitcast(F32) if dt == F32 else v

    def prologue(self):
        p = self.p
        for n, dst in self.wb.items():
            src = self.w[n]
            if len(src.shape) == 3:
                pairs = [(src[l], dst[l]) for l in range(src.shape[0])]
            else:
                pairs = [(src, dst)]
            for s_, d_ in pairs:
                ns = _split(s_.shape[1])
                sv = s_.rearrange("k (s n) -> (k s) n", s=ns)
                dv = d_.rearrange("k (s n) -> (k s) n", s=ns)
                rows = sv.shape[0]
                step = 1024
                for r0 in range(0, rows, step):
                    r1 = min(rows, r0 + step)
                    self.DMA(dv[r0:r1, :], sv[r0:r1, :], w=["wb_" + n], eng="pool")
        self.MS(self.IDB[:], 1.0, w=["IDB"])
        idb = self.IDB[:]
        p.op("pool", lambda e: e.affine_select(out=idb, in_=idb, pattern=[[-1, 128]], compare_op=ALU.is_equal,
                                               fill=0.0, base=0, channel_multiplier=1), ["IDB"], ["IDB"])
        cm = self.CM
        self.MS(cm[:, 0, :], 1.0, w=["CM"])
        self.MS(cm[:, 1, :], 1.0, w=["CM"])
        u2 = cm[:, 1, :]
        p.op("pool", lambda e: e.affine_select(out=u2, in_=u2, pattern=[[1, 128]], compare_op=ALU.is_ge,
                                               fill=0.0, base=0, channel_multiplier=-1), ["CM"], ["CM"])
        self.MS(cm[0:64, 1, 64:128], 0.0, w=["CM"])
        self.MS(cm[:, 2, :], 0.0, w=["CM"])
        self.MS(cm[0:64, 2, 0:64], 1.0, w=["CM"])
        self.MS(cm[64:128, 2, 64:128], 1.0, w=["CM"])
        self.MS(cm[:, 3, :], 0.0, w=["CM"])
        self.MS(cm[0:64, 3, :], 1.0, w=["CM"])
        self.MS(cm[:, 4, :], 0.0, w=["CM"])
        self.MS(cm[64:128, 4, :], 1.0, w=["CM"])
        self.MS(cm[:, 5, :], 0.0, w=["CM"])
        ps_ = cm[:, 5, :]
        p.op("pool", lambda e: e.affine_select(out=ps_, in_=ps_, pattern=[[-1, 128]], compare_op=ALU.is_ge,
                                               fill=PEN, base=-1, channel_multiplier=1), ["CM"], ["CM"])
        self.MS(cm[64:128, 5, 0:64], PEN, w=["CM"])
        self.MS(cm[:, 6, :], 0.0, w=["CM"])
        pi_ = cm[:, 6, :]
        p.op("pool", lambda e: e.affine_select(out=pi_, in_=pi_, pattern=[[1, 128]], compare_op=ALU.is_ge,
                                               fill=PEN, base=0, channel_multiplier=-1), ["CM"], ["CM"])
        self.MS(cm[0:64, 6, 64:128], PEN, w=["CM"])
        self.MS(self.CST[:, 0:1], EPS, w=["CST"])
        self.MS(self.CST[:, 1:2], 1.0, w=["CST"])
        self.MS(self.CST[:, 2:3], 0.0, w=["CST"])
        inv_a = (1.0 / (np.float32(10000.0) ** (np.arange(0, 64, 2, dtype=np.float32) / np.float32(64)))).astype(np.float32)
        inv_i = (1.0 / (np.float32(10000.0) ** (np.arange(0, 32, 2, dtype=np.float32) / np.float32(32)))).astype(np.float32)
        for i, v in enumerate(list(inv_a) + list(inv_i)):
            self.MS(self.INV[:, i:i + 1], float(v), w=["INV"])
        for i in range(NIT):
            self.MS(self.P2[:, 0, i:i + 1], 2.0 ** -(i + 2), w=["P2"])
            self.MS(self.P2[:, 1, i:i + 1], 2.0 ** -(i + 1), w=["P2"])
        w = self.w
        for i, n in enumerate(("ev_q_norm_g", "ev_k_norm_g", "ev_ik_ln_g", "ev_ik_ln_b")):
            self.DMA(self.G64[:, i, :], w[n][0:1, :].to_broadcast([128, 64]), w=["G64"])
        self.DMA(self.ONG[:], w["od_o_norm_g"][0:1, :].to_broadcast([128, 128]), w=["ONG"])
        self.DMA(self.AD[:, 0, :], w["od_a_log"][0:1, :].to_broadcast([128, 8]), w=["AD"])
        self.DMA(self.AD[:, 1, :], w["od_dt_bias"][0:1, :].to_broadcast([128, 8]), w=["AD"])
        for k_ in range(3):
            self.DMA(self.CW0[:, k_, :], w["ev_conv_w"][k_].rearrange("(c p) -> p c", p=128), w=["CW0"], nc_ok=True)
        for k_ in range(4):
            self.DMA(self.CW1[:, k_, :], w["od_conv_w"][k_].rearrange("(c p) -> p c", p=128), w=["CW1"], nc_ok=True)
        self.ACT(self.AD[:, 0, :], self.AD[:, 0, :], AF.Exp, r=["AD"], w=["AD"])
        self.TS(self.AD[:, 0, :], self.AD[:, 0, :], -1.0, None, ALU.mult, r=["AD"], w=["AD"])
        self.p.barrier()

    def load_w(self, dst, name, r0, r1, c0, c1, l=None, wkey=()):
        src = self.wb[name]
        if l is not None:
            src = src[l]
        self.DMA(dst, src[r0:r1, c0:c1].rearrange("(kc p) n -> p kc n", p=128), r=["wb_" + name], w=wkey)

    WBK = ["WB0", "WB1"]

    def rope_tables(self, b):
        S, NT = self.S, self.NT
        ar = 0
        POSI = self.ST[:, 0:NT].bitcast(I32)
        POSF = self.ST[:, 16:16 + NT]
        self.DMA(POSI, self.pos[b].rearrange("(n p) -> p n", p=128), w=["ST"], nc_ok=True)
        self.CP(POSF, POSI, r=["ST"], w=["ST"])
        n = NT * 48
        ANG = self.arv(ar, n, F32).rearrange("p (t f) -> p t f", f=48); ar += 2 * n
        KF = self.arv(ar, n, F32).rearrange("p (t f) -> p t f", f=48); ar += 2 * n
        KI = self.arv(ar, n, F32).bitcast(I32).rearrange("p (t f) -> p t f", f=48); ar += 2 * n
        R2 = self.arv(ar, n, F32).rearrange("p (t f) -> p t f", f=48); ar += 2 * n
        k = ["ropetmp"]
        self.TT(ANG, self.INV[:].unsqueeze(1).to_broadcast([128, NT, 48]),
                POSF.unsqueeze(2).to_broadcast([128, NT, 48]), ALU.mult, r=["INV", "ST"], w=k)
        self.TS(KF, ANG, 1.0 / (2 * math.pi), None, ALU.mult, r=k, w=k)
        self.CP(KI, KF, r=k, w=k)
        self.CP(KF, KI, r=k, w=k)
        C1 = 6.28125
        C2 = 2 * math.pi - C1
        self.STT(ANG, KF, -C1, ANG, ALU.mult, ALU.add, r=k, w=k)
        self.STT(ANG, KF, -C2, ANG, ALU.mult, ALU.add, r=k, w=k)

        def wrap(T):
            self.TS(KF, T, math.pi, 2 * math.pi, ALU.is_gt, ALU.mult, r=k, w=k)
            self.TT(T, T, KF, ALU.subtract, r=k, w=k)
            self.TS(KF, T, -math.pi, 2 * math.pi, ALU.is_lt, ALU.mult, r=k, w=k)
            self.TT(T, T, KF, ALU.add, r=k, w=k)
        wrap(ANG)
        self.TS(R2, ANG, math.pi / 2, None, ALU.add, r=k, w=k)
        wrap(R2)
        self.ACT(self.SIN, ANG, AF.Sin, r=k, w=["SIN"])
        self.ACT(self.COS, R2, AF.Sin, r=k, w=["COS"])

    def norm_T(self, gname, l):
        NT = self.NT
        gslot = self.GREP[:]
        self.DMA(gslot, self.w[gname][l:l + 1, :].to_broadcast([128, D]), w=["GREP"])
        SS = self.ST[:, 32:32 + NT]
        RS = self.ST[:, 48:48 + NT]
        for i in range(NT):
            self.ACT(self.JK[:, 0:D], self.H[:, i, :], AF.Square, r=["H%d" % i], w=["JK", "ST_ss%d" % i], accum=SS[:, i:i + 1])
        self.ACT(RS, SS, AF.Sqrt, r=["ST_ss%d" % i for i in range(NT)], w=["ST_rs"], scale=1.0 / D, bias=self.CST[:, 0:1])
        self.p.op("dve", lambda e: e.reciprocal(out=RS, in_=RS), ["ST_rs"], ["ST_rs"])
        for i in range(NT):
            hn = self.JK[:, D:2 * D] if i % 2 == 0 else self.JK[:, 0:D]
            hk = "JKb" if i % 2 == 0 else "JK"
            self.STT(hn, self.H[:, i, :], RS[:, i:i + 1], gslot, ALU.mult, ALU.mult, r=["H%d" % i, "ST_rs", "GREP"], w=[hk])
            b = self.bank()
            pb = self.ps[b][:].bitcast(BF16)
            for kc in range(KC):
                self.TR(pb[:, kc * 128:(kc + 1) * 128], hn[:, kc * 128:(kc + 1) * 128], r=[hk], w=["ps%d" % b])
            self.CP(self.XT[:, :, i * 128:(i + 1) * 128], pb.rearrange("p (k t) -> p k t", t=128),
                    r=["ps%d" % b], w=["XT%d" % i], eng="act")

    def all_xt(self):
        return ["XT%d" % i for i in range(self.NT)]

    def conv_part(self):
        S, NT, NG = self.S, self.NT, self.NG
        ar = 0
        U = self.arv(ar, S + 2, F32); ar += 2 * (S + 2)
        ZB = self.arv(ar, S, F32); ar += 2 * S
        ACC = self.arv(ar, S, F32); ar += 2 * S
        ZC = self.arv(ar, 1024, F32).rearrange("p (a n) -> p a n", a=2); ar += 2048
        self.MS(U[:, 0:2], 0.0, w=["U"], eng="dve")
        xts = self.all_xt()
        wv = self.WB[:, 0:3 * KC * 128].rearrange("p (j k n) -> p j k n", j=3, k=KC)
        for c in range(4):
            for j, base in enumerate((512, 1024, 0)):
                self.load_w(wv[:, j], "ev_w_in", 0, D, base + c * 128, base + (c + 1) * 128, wkey=self.WBK)
            for g in range(NG):
                tok = slice(g * 512, (g + 1) * 512)
                for j in range(3):
                    b = self.bank()
                    for kc in range(KC):
                        self.MM(self.ps[b][:], wv[:, j, kc, :], self.XT[:, kc, tok], kc == 0, kc == KC - 1,
                                r=self.WBK + xts[g * 4:(g + 1) * 4], w=["ps%d" % b])
                    if j == 0:
                        self.CP(ZC[:, g % 2, :], self.ps[b][:], r=["ps%d" % b], w=["ZC%d" % (g % 2)], eng="act")
                    elif j == 1:
                        self.TT(U[:, 2 + g * 512:2 + (g + 1) * 512], self.ps[b][:], ZC[:, g % 2, :], ALU.mult,
                                r=["ps%d" % b, "ZC%d" % (g % 2)], w=["U"])
                    else:
                        self.CP(ZB[:, tok], self.ps[b][:], r=["ps%d" % b], w=["ZB"], eng="act")
            cw = self.CW0
            self.TS(ACC, U[:, 2:S + 2], cw[:, 2, c:c + 1], None, ALU.mult, r=["U", "CW0"], w=["ACC"])
            self.STT(ACC, U[:, 1:S + 1], cw[:, 1, c:c + 1], ACC, ALU.mult, ALU.add, r=["U", "CW0", "ACC"], w=["ACC"])
            self.STT(ACC, U[:, 0:S], cw[:, 0, c:c + 1], ACC, ALU.mult, ALU.add, r=["U", "CW0", "ACC"], w=["ACC"])
            self.TT(self.YT[:, c, :], ACC, ZB, ALU.mult, r=["ACC", "ZB"], w=["YTa%d" % c])

    def rope(self, dst, src, nh, half, f0, i, tmp, rk, wk):
        c = self.COS[:, i, f0:f0 + half].unsqueeze(1).to_broadcast([128, nh, half])
        s = self.SIN[:, i, f0:f0 + half].unsqueeze(1).to_broadcast([128, nh, half])
        x1 = src[:, :, 0:half]
        x2 = src[:, :, half:2 * half]
        t1 = tmp[:, 0:nh * half].rearrange("p (h d) -> p h d", h=nh)
        t2 = tmp[:, nh * half:2 * nh * half].rearrange("p (h d) -> p h d", h=nh)
        rr = list(rk) + ["SIN", "COS"]
        k1, k2 = "rt1_%d" % self._rp, "rt2_%d" % self._rp
        self.TT(t1, x1, c, ALU.mult, r=rr, w=[k1])
        self.TT(t2, x2, s, ALU.mult, r=rr, w=[k2])
        self.TT(dst[:, :, 0:half], t1, t2, ALU.subtract, r=[k1, k2], w=wk)
        self.TT(t1, x1, s, ALU.mult, r=rr, w=[k1])
        self.TT(t2, x2, c, ALU.mult, r=rr, w=[k2])
        self.TT(dst[:, :, half:2 * half], t1, t2, ALU.add, r=[k1, k2], w=wk)

    def l0_layout(self):
        S, NT = self.S, self.NT
        ar = 0
        L = {}
        L["QT"] = self.YT[:, 4:8, :]
        L["KT"] = self.arv(ar, 4 * S).rearrange("p (h t) -> p h t", h=4); ar += 4 * S
        L["QIT"] = self.arv(ar, 4 * S).rearrange("p (h t) -> p h t", h=4); ar += 4 * S
        L["V"] = self.arv(ar, NT * 8 * 65).rearrange("p (t h d) -> p t h d", t=NT, h=8); ar += NT * 8 * 65
        L["KIT"] = self.arv(ar, S); ar += S
        L["end"] = ar
        return L

    def proj_T(self, L):
        S, NT = self.S, self.NT
        o = NT * 192
        TMPs, QBFs = [], []
        for par in range(2):
            if par == 0 or S >= 2048:
                TMPs.append(self.YA[:, o:o + 2048].bitcast(F32)); o += 2048
                QBFs.append(self.YA[:, o:o + 512]); o += 512
            else:
                e0 = L["end"] + (L["end"] % 2)
                TMPs.append(self.AR[:, e0:e0 + 2048].bitcast(F32))
                QBFs.append(self.AR[:, e0 + 2048:e0 + 2560])
        SQs = [self.JK[:, 0:1024].bitcast(F32), self.JK[:, 1024:2048].bitcast(F32)]
        self.MS(L["V"][:, :, :, 64:65], 1.0, w=["Vones"], eng="dve")
        blocks = (("q", 1536, 512), ("k", 2048, 512), ("v", 2560, 512), ("qi", 3072, 512), ("kw", 3584, 72))
        for bi, (nm, c0, ncol) in enumerate(blocks):
            wk = self.WBK
            wv = self.WB[:, 0:KC * ncol].rearrange("p (k n) -> p k n", k=KC)
            self.load_w(wv, "ev_w_in", 0, D, c0, c0 + ncol, wkey=wk)
            for i in range(NT):
                tk = slice(i * 128, (i + 1) * 128)
                b = self.bank()
                pk = "ps%d" % b
                psv = self.ps[b][:, 0:ncol]
                for kc in range(KC):
                    self.MM(psv, self.XT[:, kc, tk], wv[:, kc, :], kc == 0, kc == KC - 1, r=wk + ["XT%d" % i], w=[pk])
                pp = i % 2
                TMP, QBF, SQ = TMPs[pp], QBFs[pp], SQs[pp]
                QF = TMP[:, 0:512].rearrange("p (h d) -> p h d", h=8)
                RT = TMP[:, 512:1024]
                kJK, kQF, kJQ, kSQ = "JK%d" % pp, "QF%d" % pp, "JKq%d" % pp, "ST_q%d" % pp
                self._rp = pp
                if nm in ("q", "k"):
                    gi = 0 if nm == "q" else 1
                    ps3 = psv.rearrange("p (h d) -> p h d", h=8)
                    self.ACT(SQ, psv, AF.Square, r=[pk], w=[kJK])
                    SSQ = self.ST[:, 64 + 8 * pp:72 + 8 * pp]
                    self.p.op("dve", lambda e, SSQ=SSQ, SQ=SQ: e.tensor_reduce(out=SSQ, in_=SQ.rearrange("p (h d) -> p h d", h=8),
                                                                               axis=AX.X, op=ALU.add), [kJK], [kSQ])
                    self.ACT(SSQ, SSQ, AF.Sqrt, r=[kSQ], w=[kSQ], scale=1.0 / 64, bias=self.CST[:, 0:1])
                    self.p.op("dve", lambda e, SSQ=SSQ: e.reciprocal(out=SSQ, in_=SSQ), [kSQ], [kSQ])
                    self.TT(QF, ps3, SSQ.unsqueeze(2).to_broadcast([128, 8, 64]), ALU.mult, r=[pk, kSQ], w=[kQF])
                    self.TT(QF, QF, self.G64[:, gi, :].unsqueeze(1).to_broadcast([128, 8, 64]), ALU.mult, r=[kQF, "G64"], w=[kQF])
                    QB = QBF.rearrange("p (h d) -> p h d", h=8)
                    self.rope(QB, QF, 8, 32, 0, i, RT, [kQF], [kJQ])
                    b2 = self.bank()
                    pb = self.ps[b2][:].bitcast(BF16)
                    for hp in range(4):
                        self.TR(pb[:, hp * 128:(hp + 1) * 128], QBF[:, hp * 128:(hp + 1) * 128], r=[kJQ], w=["ps%d" % b2])
                    dst = L["QT" if nm == "q" else "KT"]
                    self.CP(dst[:, :, tk], pb[:, 0:512].rearrange("p (h t) -> p h t", h=4), r=["ps%d" % b2],
                            w=["%s%d" % ("QT" if nm == "q" else "KT", i)], eng="act")
                elif nm == "v":
                    self.CP(L["V"][:, i, :, 0:64], psv.rearrange("p (h d) -> p h d", h=8), r=[pk], w=["V%d" % i], eng="act")
                elif nm == "qi":
                    ps3 = psv.rearrange("p (h d) -> p h d", h=8)
                    QB = QBF.rearrange("p (h d) -> p h d", h=8)
                    self.rope(QB, ps3, 8, 16, 32, i, RT, [pk], [kJQ])
                    self.CP(QB[:, :, 32:64], ps3[:, :, 32:64], r=[pk], w=[kJQ], eng="act")
                    b2 = self.bank()
                    pb = self.ps[b2][:].bitcast(BF16)
                    for hp in range(4):
                        self.TR(pb[:, hp * 128:(hp + 1) * 128], QBF[:, hp * 128:(hp + 1) * 128], r=[kJQ], w=["ps%d" % b2])
                    self.CP(L["QIT"][:, :, tk], pb[:, 0:512].rearrange("p (h t) -> p h t", h=4), r=["ps%d" % b2], w=["QIT%d" % i], eng="act")
                else:
                    BN = self.ST[:, 80 + 16 * pp:86 + 16 * pp]
                    MV = self.ST[:, 88 + 16 * pp:90 + 16 * pp]
                    kip = psv[:, 0:64]
                    self.p.op("dve", lambda e, BN=BN, kip=kip: e.bn_stats(out=BN, in_=kip), [pk], ["ST_bn%d" % pp])
                    self.p.op("dve", lambda e, BN=BN, MV=MV: e.bn_aggr(out=MV, in_=BN), ["ST_bn%d" % pp], ["ST_mv%d" % pp])
                    RSD = self.ST[:, 90 + 16 * pp:91 + 16 * pp]
                    self.ACT(RSD, MV[:, 1:2], AF.Sqrt, r=["ST_mv%d" % pp], w=["ST_rsd%d" % pp], bias=self.CST[:, 0:1])
                    self.p.op("dve", lambda e, RSD=RSD: e.reciprocal(out=RSD, in_=RSD), ["ST_rsd%d" % pp], ["ST_rsd%d" % pp])
                    KF_ = TMP[:, 0:64]
                    self.TS(KF_, kip, MV[:, 0:1], RSD, ALU.subtract, ALU.mult, r=[pk, "ST_mv%d" % pp, "ST_rsd%d" % pp], w=[kQF])
                    self.TT(KF_, KF_, self.G64[:, 2, :], ALU.mult, r=[kQF, "G64"], w=[kQF])
                    self.TT(KF_, KF_, self.G64[:, 3, :], ALU.add, r=[kQF, "G64"], w=[kQF])
                    KB_ = QBF[:, 0:128]
                    self.rope(KB_[:, 0:64].unsqueeze(1), KF_.unsqueeze(1), 1, 16, 32, i, RT, [kQF], [kJQ])
                    self.CP(KB_[:, 32:64], KF_[:, 32:64], r=[kQF], w=[kJQ], eng="act")
                    self.CP(KB_[:, 64:128], KB_[:, 0:64], r=[kJQ], w=[kJQ], eng="dve")
                    b2 = self.bank()
                    pb = self.ps[b2][:].bitcast(BF16)
                    self.TR(pb[:, 0:128], KB_, r=[kJQ], w=["ps%d" % b2])
                    self.CP(L["KIT"][:, tk], pb[:, 0:128], r=["ps%d" % b2], w=["KIT%d" % i], eng="act")
                    self.TS(self.WI[:, i, :], psv[:, 64:72], (8 ** -0.5) * (64 ** -0.5), None, ALU.mult, r=[pk], w=["WI%d" % i])

    def attention(self, L):
        S, NT, topk = self.S, self.NT, self.topk
        QT, KT, QIT, V, KIT = L["QT"], L["KT"], L["QIT"], L["V"], L["KIT"]
        xt = self.XT[:].rearrange("p k t -> p (k t)")
        o = 0
        SC, MSK, MT = [], [], []
        for par in range(2):
            SC.append(xt[:, o:o + 2 * S].bitcast(F32)); o += 2 * S
        for par in range(2):
            MSK.append(xt[:, o:o + S]); o += S
        for par in range(2):
            MT.append(xt[:, o:o + S].rearrange("p (k q) -> p k q", q=128)); o += S
        o2 = 0
        RL = self.YA[:, o2:o2 + 2048].bitcast(F32).rearrange("p (a n) -> p a n", a=2); o2 += 2048
        PT = self.YA[:, o2:o2 + 2048].rearrange("p (a n) -> p a n", a=4); o2 += 2048
        YB = self.JK[:, 0:512]
        st = self.ST
        RD = st[:, 150:158]
        OB = (5, 6)
        meng = "pool" if os.environ.get("KDBG_POOLMASK", "1") == "1" else "dve"

        def score(j):
            par = j % 2
            sc, sk = SC[par], "SC%d" % par
            qk = slice(j * 128, (j + 1) * 128)
            nkb = j + 1
            n = nkb * 128
            nch = (nkb + 3) // 4
            MX, MN = st[:, 100 + par:101 + par], st[:, 102 + par:103 + par]
            for ch in range(nch):
                k0 = ch * 512
                kw = min(512, n - k0)
                kkeys = ["KIT%d" % t for t in range(ch * 4, min(nkb, ch * 4 + 4))]
                for h in range(8):
                    hp, hh = h // 2, h % 2
                    pr = slice(hh * 64, (hh + 1) * 64)
                    b = self.bank()
                    pk = "ps%d" % b
                    self.MM(self.ps[b][:, 0:kw], QIT[pr, hp, qk], KIT[pr, k0:k0 + kw], r=["QIT%d" % j] + kkeys, w=[pk])
                    rs = h % 2
                    self.ACT(RL[:, rs, 0:kw], self.ps[b][:, 0:kw], AF.Relu, r=[pk], w=["RL%d" % rs])
                    if h == 0:
                        self.TS(sc[:, k0:k0 + kw], RL[:, rs, 0:kw], self.WI[:, j, 0:1], None, ALU.mult,
                                r=["RL%d" % rs, "WI%d" % j], w=[sk])
                    else:
                        self.STT(sc[:, k0:k0 + kw], RL[:, rs, 0:kw], self.WI[:, j, h:h + 1], sc[:, k0:k0 + kw], ALU.mult, ALU.add,
                                 r=["RL%d" % rs, "WI%d" % j, sk], w=[sk])
                    yield
            if n > topk:
                self.p.op("dve", lambda e, MX=MX, s_=sc[:, 0:n]: e.tensor_reduce(out=MX, in_=s_, axis=AX.X, op=ALU.max), [sk], ["bs_mx%d" % par])
                yield
                self.p.op("dve", lambda e, MN=MN, s_=sc[:, 0:n]: e.tensor_reduce(out=MN, in_=s_, axis=AX.X, op=ALU.min), [sk], ["bs_mn%d" % par])
                yield
            dg = sc[:, j * 128:(j + 1) * 128]
            self.p.op("pool", lambda e, dg=dg: e.affine_select(out=dg, in_=dg, pattern=[[-1, 128]], compare_op=ALU.is_ge,
                                                               fill=self.negreg(e), base=0, channel_multiplier=1), [sk], [sk])
            yield

        def attn(j):
            par = j % 2
            sc, sk = SC[par], "SC%d" % par
            msk, mk = MSK[par], "MSK%d" % par
            mt, mtk = MT[par], "MT%d" % par
            qk = slice(j * 128, (j + 1) * 128)
            nkb = j + 1
            n = nkb * 128
            nch = (nkb + 3) // 4
            MX, MN = st[:, 100 + par:101 + par], st[:, 102 + par:103 + par]
            RG, TH, CNT, DD, THF = (st[:, 104 + i:105 + i] for i in range(5))
            STEP = st[:, 112:112 + NIT]
            STEP2 = st[:, 128:128 + NIT]
            if n > topk:
                self.TT(RG, MX, MN, ALU.subtract, r=["bs_mx%d" % par, "bs_mn%d" % par], w=["bs_rg"])
                self.TS(STEP, self.P2[:, 0, :], RG, None, ALU.mult, r=["P2", "bs_rg"], w=["bs_step"])
                self.TS(STEP2, self.P2[:, 1, :], RG, None, ALU.mult, r=["P2", "bs_rg"], w=["bs_step2"])
                self.STT(TH, MX, 1.0, MN, ALU.mult, ALU.add, r=["bs_mx%d" % par, "bs_mn%d" % par], w=["bs_th"])
                self.TS(TH, TH, 0.5, None, ALU.mult, r=["bs_th"], w=["bs_th"])
                yield
                for it in range(NIT):
                    self.TS(msk[:, 0:n], sc[:, 0:n], TH, None, ALU.is_ge, ALU.add, r=[sk, "bs_th"], w=[mk, "bs_cnt"], accum=CNT)
                    self.TS(DD, CNT, float(topk) - 0.5, STEP2[:, it:it + 1], ALU.is_ge, ALU.mult, r=["bs_cnt", "bs_step2"], w=["bs_dd"])
                    self.STT(TH, DD, STEP[:, it:it + 1], TH, ALU.subtract, ALU.add, r=["bs_dd", "bs_step", "bs_th"], w=["bs_th"])
                    yield
                self.STT(THF, RG, -(2.0 ** -(NIT + 1)), TH, ALU.mult, ALU.add, r=["bs_rg", "bs_th"], w=["bs_thf"])
                self.TS(msk[:, 0:n], sc[:, 0:n], THF, None, ALU.is_ge, r=[sk, "bs_thf"], w=[mk])
            else:
                self.TS(msk[:, 0:n], sc[:, 0:n], 0.5 * NEG, None, ALU.is_ge, r=[sk], w=[mk])
            yield
            for g0 in range(0, nkb, 8):
                g1 = min(nkb, g0 + 8)
                b = self.bank()
                pb = self.ps[b][:].bitcast(BF16)
                for kb in range(g0, g1):
                    self.TR(pb[:, (kb - g0) * 128:(kb - g0 + 1) * 128], msk[:, kb * 128:(kb + 1) * 128], r=[mk], w=["ps%d" % b])
                self.CP(mt[:, g0:g1, :], pb[:, 0:(g1 - g0) * 128].rearrange("p (k q) -> p k q", q=128), r=["ps%d" % b], w=[mtk], eng="act")
                yield

        def attc(j):
            par = j % 2
            mt, mtk = MT[par], "MT%d" % par
            qk = slice(j * 128, (j + 1) * 128)
            nkb = j + 1
            nch = (nkb + 3) // 4
            items = [(h, ch) for h in range(8) for ch in range(nch)]

            def front(k):
                h, ch = items[k]
                hp, hh = h // 2, h % 2
                pr = slice(hh * 64, (hh + 1) * 64)
                kb0 = ch * 4
                kb1 = min(nkb, kb0 + 4)
                kw = (kb1 - kb0) * 128
                b = self.bank()
                pk = "ps%d" % b
                for kb in range(kb0, kb1):
                    self.MM(self.ps[b][:, (kb - kb0) * 128:(kb - kb0 + 1) * 128], KT[pr, hp, kb * 128:(kb + 1) * 128], QT[pr, hp, qk],
                            r=["KT%d" % kb, "QT%d" % j], w=[pk])
                ps_ = k % 4
                self.ACT(PT[:, ps_, 0:kw], self.ps[b][:, 0:kw], AF.Exp, r=[pk], w=["PT%d" % ps_], scale=0.125)
                self.TT(PT[:, ps_, 0:kw], PT[:, ps_, 0:kw], mt[:, kb0:kb1, :].rearrange("p k q -> p (k q)"), ALU.mult,
                        r=["PT%d" % ps_, mtk], w=["PT%d" % ps_], eng=meng)

            def back(k):
                h, ch = items[k]
                ob = OB[h // 4]
                ov = self.ps[ob][:, 0:260].rearrange("p (h d) -> p h d", h=4)[:, h % 4, :]
                kb0 = ch * 4
                kb1 = min(nkb, kb0 + 4)
                ps_ = k % 4
                for kb in range(kb0, kb1):
                    self.MM(ov, PT[:, ps_, (kb - kb0) * 128:(kb - kb0 + 1) * 128], V[:, kb, h, :], kb == 0, kb == nkb - 1,
                            r=["PT%d" % ps_, "V%d" % kb, "Vones"], w=["ps%d" % ob])
            front(0)
            if len(items) > 1:
                front(1)
            yield
            for k in range(len(items)):
                if k + 2 < len(items):
                    front(k + 2)
                back(k)
                yield
            for half in range(2):
                ob = OB[half]
                o3 = self.ps[ob][:, 0:260].rearrange("p (h d) -> p h d", h=4)
                rd = RD[:, half * 4:half * 4 + 4]
                self.p.op("dve", lambda e, rd=rd, o3=o3: e.reciprocal(out=rd, in_=o3[:, :, 64]), ["ps%d" % ob], ["ST_rd%d" % half])
                self.TT(YB[:, half * 256:(half + 1) * 256].rearrange("p (h d) -> p h d", h=4), o3[:, :, 0:64],
                        rd.unsqueeze(2).to_broadcast([128, 4, 64]), ALU.mult, r=["ps%d" % ob, "ST_rd%d" % half], w=["YB"])
            b = self.bank()
            pb = self.ps[b][:].bitcast(BF16)
            for c in range(4):
                self.TR(pb[:, c * 128:(c + 1) * 128], YB[:, c * 128:(c + 1) * 128], r=["YB"], w=["ps%d" % b])
            self.CP(self.YT[:, 4:8, qk], pb[:, 0:512].rearrange("p (c t) -> p c t", c=4), r=["ps%d" % b], w=["YTb%d" % j, "QT%d" % j], eng="act")
            yield

        def merge(gens):
            live = list(gens)
            while live:
                nxt = []
                for g_ in live:
                    try:
                        next(g_)
                        nxt.append(g_)
                    except StopIteration:
                        pass
                live = nxt
        merge([score(0)])
        merge([attn(0)] + ([score(1)] if NT > 1 else []))
        for j in range(NT):
            gl = [attc(j)]
            if j + 1 < NT:
                gl.append(attn(j + 1))
            if j + 2 < NT:
                gl.append(score(j + 2))
            merge(gl)

    def out_proj(self, name, k0, nk, src, rk):
        NT = self.NT
        for cb in range(2):
            wv = self.WB[:, 0:nk * 512].rearrange("p (k n) -> p k n", k=nk)
            self.load_w(wv, name, k0 * 128, (k0 + nk) * 128, cb * 512, (cb + 1) * 512, wkey=self.WBK)
            for i in range(NT):
                b = self.bank()
                for k in range(nk):
                    self.MM(self.ps[b][:], src(k, i), wv[:, k, :], k == 0, k == nk - 1, r=self.WBK + rk(i), w=["ps%d" % b])
                hv = self.H[:, i, cb * 512:(cb + 1) * 512]
                self.TT(hv, self.ps[b][:], hv, ALU.add, r=["ps%d" % b, "H%d" % i], w=["H%d" % i])

    def layer0_mixer(self, b):
        self.norm_T("norm_mix_g", 0)
        self.p.barrier()
        self.conv_part()
        tks = lambda i: slice(i * 128, (i + 1) * 128)
        self.out_proj("ev_w_out", 0, 4, lambda k, i: self.YT[:, k, tks(i)], lambda i: ["YTa%d" % c for c in range(4)])
        self.p.barrier()
        self.rope_tables(b)
        self.p.barrier()
        L = self.l0_layout()
        self.proj_T(L)
        self.p.barrier()
        self.attention(L)
        self.out_proj("ev_w_out", 4, 4, lambda k, i: self.YT[:, 4 + k, tks(i)], lambda i: ["YTb%d" % i])
        self.p.barrier()

    def ffn(self, l):
        S, NT, NG = self.S, self.NT, self.NG
        self.norm_T("norm_ffn_g", l)
        ar = 0
        AT = self.arv(ar, FCN * 512).rearrange("p (f t) -> p f t", f=FCN); ar += FCN * 512
        WD = []
        for s in range(2):
            WD.append(self.arv(ar, FCN * 256).rearrange("p (f n) -> p f n", f=FCN)); ar += FCN * 256
        SG = self.arv(ar, 512, F32); ar += 1024
        nwd = 0
        nfb = 0
        for g in range(NG):
            tok = slice(g * 512, (g + 1) * 512)
            xk = ["XT%d" % i for i in range(g * 4, g * 4 + 4)]
            for f in range(FCN):
                slot = nfb % 2
                nfb += 1
                wk = self.WBK[slot]
                wv = self.WB[:, slot * 2048:(slot + 1) * 2048].rearrange("p (j k n) -> p j k n", j=2, k=KC)
                self.load_w(wv[:, 0], "ffn_w_gate", 0, D, f * 128, (f + 1) * 128, l=l, wkey=[wk])
                self.load_w(wv[:, 1], "ffn_w_up", 0, D, f * 128, (f + 1) * 128, l=l, wkey=[wk])
                bg = self.bank()
                bu = self.bank()
                for kc in range(KC):
                    self.MM(self.ps[bg][:], wv[:, 0, kc, :], self.XT[:, kc, tok], kc == 0, kc == KC - 1, r=[wk] + xk, w=["ps%d" % bg])
                for kc in range(KC):
                    self.MM(self.ps[bu][:], wv[:, 1, kc, :], self.XT[:, kc, tok], kc == 0, kc == KC - 1, r=[wk] + xk, w=["ps%d" % bu])
                self.ACT(SG, self.ps[bg][:], AF.Silu, r=["ps%d" % bg], w=["SG"])
                self.TT(AT[:, f, :], SG, self.ps[bu][:], ALU.mult, r=["SG", "ps%d" % bu], w=["AT%d" % f])
            atk = ["AT%d" % f for f in range(FCN)]
            for cb in range(4):
                slot = nwd % 2
                nwd += 1
                wk = "WD%d" % slot
                src = self.wb["ffn_w_down"][l][:, cb * 256:(cb + 1) * 256].rearrange("(f p) n -> p f n", p=128)
                self.DMA(WD[slot], src, r=["wb_ffn_w_down"], w=[wk])
                for ti in range(4):
                    i = g * 4 + ti
                    b = self.bank()
                    for f in range(FCN):
                        self.MM(self.ps[b][:, 0:256], AT[:, f, ti * 128:(ti + 1) * 128], WD[slot][:, f, :], f == 0, f == FCN - 1,
                                r=[wk] + atk, w=["ps%d" % b])
                    hv = self.H[:, i, cb * 256:(cb + 1) * 256]
                    self.TT(hv, self.ps[b][:, 0:256], hv, ALU.add, r=["ps%d" % b, "H%d" % i], w=["H%d" % i])
        self.p.barrier()

    def ple(self, l, b):
        S, NT = self.S, self.NT
        self.norm_T("ple_norm_g", l)
        ar = 0
        gpost = self.arv(ar, D, F32); ar += 2 * D
        self.DMA(gpost, self.w["ple_post_norm_g"][l:l + 1, :].to_broadcast([128, D]), w=["GPOST"])
        WP = self.arv(ar, 2 * D).rearrange("p (k n) -> p k n", k=2); ar += 2 * D
        self.load_w(WP, "ple_w_proj", 0, PLE, 0, D, l=l, wkey=["WP"])
        PTT = self.arv(ar, 2 * S).rearrange("p (k t) -> p k t", k=2); ar += 2 * S
        PF = []
        PB = []
        for s in range(2):
            PF.append(self.arv(ar, 256, F32)); ar += 512
            PB.append(self.arv(ar, 256)); ar += 256
        SGM = self.arv(ar, 512, F32); ar += 1024
        EE = self.arv(ar, 512, F32); ar += 1024
        RSE = self.ST[:, 176:176 + NT]
        SS2 = self.ST[:, 200:200 + 2 * NT].rearrange("p (t c) -> p t c", c=2)
        eb = (5, 6)
        for i in range(NT):
            s = i % 2
            tk = slice(i * 128, (i + 1) * 128)
            self.DMA(PF[s], self.pin[l, b, i * 128:(i + 1) * 128, :], w=["PF%d" % s])
            self.CP(PB[s], PF[s], r=["PF%d" % s], w=["PB%d" % s], eng="pool")
            bt = self.bank()
            pb = self.ps[bt][:].bitcast(BF16)
            for k in range(2):
                self.TR(pb[:, k * 128:(k + 1) * 128], PB[s][:, k * 128:(k + 1) * 128], r=["PB%d" % s], w=["ps%d" % bt])
            self.CP(PTT[:, :, tk], pb[:, 0:256].rearrange("p (k t) -> p k t", k=2), r=["ps%d" % bt], w=["PTT%d" % i], eng="act")
            for cb in range(2):
                for k in range(2):
                    self.MM(self.ps[eb[cb]][:], PTT[:, k, tk], WP[:, k, cb * 512:(cb + 1) * 512], k == 0, k == 1,
                            r=["PTT%d" % i, "WP"], w=["ps%d" % eb[cb]])
                self.ACT(self.JK[:, 0:1024].bitcast(F32), self.ps[eb[cb]][:], AF.Square, r=["ps%d" % eb[cb]], w=["JK", "ST_sse%d_%d" % (i, cb)],
                         accum=SS2[:, i, cb:cb + 1])
        allss = ["ST_sse%d_%d" % (i, cb) for i in range(NT) for cb in range(2)]
        self.TT(RSE, SS2[:, :, 0], SS2[:, :, 1], ALU.add, r=allss, w=["ST_rse"])
        self.ACT(RSE, RSE, AF.Sqrt, r=["ST_rse"], w=["ST_rse"], scale=1.0 / D, bias=self.CST[:, 0:1])
        self.p.op("dve", lambda e: e.reciprocal(out=RSE, in_=RSE), ["ST_rse"], ["ST_rse"])
        for cb in range(2):
            cs = slice(cb * 512, (cb + 1) * 512)
            wg = self.WB[:].rearrange("p (k n) -> p k n", k=KC)
            self.load_w(wg, "ple_w_gate", 0, D, cb * 512, (cb + 1) * 512, l=l, wkey=self.WBK)
            for i in range(NT):
                tk = slice(i * 128, (i + 1) * 128)
                be = self.bank()
                for k in range(2):
                    self.MM(self.ps[be][:], PTT[:, k, tk], WP[:, k, cs], k == 0, k == 1, r=["PTT%d" % i, "WP"], w=["ps%d" % be])
                self.STT(EE, self.ps[be][:], RSE[:, i:i + 1], gpost[:, cs], ALU.mult, ALU.mult, r=["ps%d" % be, "ST_rse", "GPOST"], w=["EE"])
                bg = self.bank()
                for kc in range(KC):
                    self.MM(self.ps[bg][:], self.XT[:, kc, tk], wg[:, kc, :], kc == 0, kc == KC - 1, r=self.WBK + ["XT%d" % i], w=["ps%d" % bg])
                self.ACT(SGM, self.ps[bg][:], AF.Sigmoid, r=["ps%d" % bg], w=["SGM"])
                self.TT(EE, EE, SGM, ALU.mult, r=["EE", "SGM"], w=["EE"])
                hv = self.H[:, i, cs]
                self.TT(hv, hv, EE, ALU.add, r=["EE", "H%d" % i], w=["H%d" % i])
        self.p.barrier()

    def deltanet(self, b):
        S, NT, NG = self.S, self.NT, self.NG
        CM = self.CM
        ONES, U2, BD, HALF0, HALF1, PEN_S, PEN_IT = (CM[:, i, :] for i in range(7))
        self.norm_T("norm_mix_g", 1)
        ar = 0
        NH = NT * 8

        def f32v(n):
            nonlocal ar
            v = self.arv(ar, n, F32)
            ar += 2 * n
            return v

        def bfv(n):
            nonlocal ar
            v = self.arv(ar, n)
            ar += n + (n % 2)
            return v
        th = lambda v: v.rearrange("p (t h) -> p t h", h=8)
        A, BETA, GT, EG, BEG, EKD, T1, T2 = (th(f32v(NH)) for _ in range(8))
        EGL = f32v(2 * NH).rearrange("p (a t h) -> p a t h", a=2, h=8)
        RAW = f32v(NT * 16).rearrange("p (t n) -> p t n", n=16)
        wab = self.WB[:, 0:KC * 16].rearrange("p (k n) -> p k n", k=KC)
        self.load_w(wab, "od_w_in", 0, D, 4096, 4112, wkey=self.WBK)
        for i in range(NT):
            bk = self.bank()
            for kc in range(KC):
                self.MM(self.ps[bk][:, 0:16], self.XT[:, kc, i * 128:(i + 1) * 128], wab[:, kc, :], kc == 0, kc == KC - 1,
                        r=self.WBK + ["XT%d" % i], w=["ps%d" % bk])
            self.CP(RAW[:, i, :], self.ps[bk][:, 0:16], r=["ps%d" % bk], w=["RAW"], eng="act")
        k = ["dn_small"]
        self.TT(T1, RAW[:, :, 0:8], self.AD[:, 1, :].unsqueeze(1).to_broadcast([128, NT, 8]), ALU.add, r=["RAW", "AD"], w=k)
        self.TS(T2, T1, -1.0, None, ALU.mult, r=k, w=k)
        self.TT(T2, T1, T2, ALU.min, r=k, w=k)
        self.ACT(T2, T2, AF.Exp, r=k, w=k)
        self.ACT(T2, T2, AF.Ln, r=k, w=k, bias=self.CST[:, 1:2])
        self.STT(T1, T1, 0.0, T2, ALU.max, ALU.add, r=k, w=k)
        self.TT(A, T1, self.AD[:, 0, :].unsqueeze(1).to_broadcast([128, NT, 8]), ALU.mult, r=k + ["AD"], w=k)
        self.ACT(BETA, RAW[:, :, 8:16], AF.Sigmoid, r=["RAW"], w=k)
        fl = lambda v: v.rearrange("p t h -> p (t h)")
        Af = fl(A)
        bk = self.bank()
        self.MM(self.ps[bk][:, 0:NH], U2, Af, r=k + ["CM"], w=["ps%d" % bk])
        self.CP(fl(GT), self.ps[bk][:, 0:NH], r=["ps%d" % bk], w=k, eng="act")
        bk = self.bank()
        self.MM(self.ps[bk][:, 0:NH], BD, Af, r=k + ["CM"], w=["ps%d" % bk])
        self.TT(fl(T1), self.ps[bk][:, 0:NH], fl(GT), ALU.subtract, r=["ps%d" % bk] + k, w=k)
        self.ACT(EKD, T1, AF.Exp, r=k, w=k)
        self.ACT(EG, GT, AF.Exp, r=k, w=k)
        self.TT(BEG, BETA, EG, ALU.mult, r=k, w=k)
        for half, HM in enumerate((HALF0, HALF1)):
            bk = self.bank()
            self.MM(self.ps[bk][:, 0:NH], HM, Af, r=k + ["CM"], w=["ps%d" % bk])
            self.ACT(EGL[:, half].rearrange("p t h -> p (t h)"), self.ps[bk][:, 0:NH], AF.Exp, r=["ps%d" % bk], w=k)
        self.nrot = 6
        yt = self.YT[:].rearrange("p k t -> p (k t)")
        yo = 0

        def ytv(n, dt=BF16):
            nonlocal yo
            ne = n * (2 if dt == F32 else 1)
            v = yt[:, yo:yo + ne]
            yo += ne
            assert yo <= 8 * S
            return v.bitcast(F32) if dt == F32 else v
        Sh = S // 2
        NGh = Sh // 512
        QN = [ytv(S), ytv(S)]
        KN = [ytv(S), ytv(S)]
        VT = [ytv(S), ytv(S)]
        ZS = [ytv(S), ytv(S)]
        PRE = [f32v(Sh + 4), f32v(Sh + 4)]
        ACC = [f32v(Sh), f32v(Sh)]
        g4 = lambda v: v.rearrange("p (t c) -> p t c", c=128)
        TA, TB, TC = (f32v(512) for _ in range(3))
        AREP, DM, DMT = TC, TB, TC
        MM_, NN_, MP, NP, RR, VBt, KBGt = (bfv(512) for _ in range(7))
        QG, KD, WT, QKT, UU, YHg = ([bfv(512), bfv(512)] for _ in range(6))
        WO = bfv(1024)
        SF = f32v(128)
        SB_ = bfv(128)
        VN = bfv(128)
        OT = f32v(128)
        OB_ = bfv(128)
        SILg = self.JK[:, 0:1024].bitcast(F32)
        SQ = self.JK[:, 1024:2048].bitcast(F32)
        wv = self.WB[:].rearrange("p (j k n) -> p j k n", j=4, k=KC)
        xts = self.all_xt()
        IDB4 = self.IDB[:].unsqueeze(1).to_broadcast([128, 4, 128])
        NGRP = NT // 4
        c4 = lambda t: slice(t * 128, (t + 1) * 128)
        TC4 = ["TC%d" % t for t in range(4)]
        TB4 = ["TB%d" % t for t in range(4)]

        def Pgen(h):
            hp = h % 2
            qk_, kk_, vk_, zk_ = "QN%d" % hp, "KN%d" % hp, "VT%d" % hp, "ZS%d" % hp
            for j in range(4):
                self.load_w(wv[:, j], "od_w_in", 0, D, j * 1024 + h * 128, j * 1024 + (h + 1) * 128, wkey=self.WBK)
            for g in range(NG):
                tok = slice(g * 512, (g + 1) * 512)
                bk = self.pbank()
                for kc in range(KC):
                    self.MM(self.ps[bk][:], wv[:, 3, kc, :], self.XT[:, kc, tok], kc == 0, kc == KC - 1, r=self.WBK + xts[g * 4:g * 4 + 4], w=["ps%d" % bk])
                    if kc % 2 == 1 and kc < KC - 1:
                        yield
                self.ACT(ZS[hp][:, tok], self.ps[bk][:], AF.Silu, r=["ps%d" % bk], w=[zk_])
                yield
            unit = 0
            for j, nm in enumerate(("q", "k", "v")):
                for hf in range(2):
                    par = unit % 2
                    unit += 1
                    pre, acc = PRE[par], ACC[par]
                    pk_, ak_ = "PRE%d" % par, "ACC%d" % par
                    if hf == 0:
                        self.MS(pre[:, 0:3], 0.0, w=[pk_], eng="dve")
                    else:
                        self.CP(pre[:, 0:3], PRE[1 - par][:, Sh:Sh + 3], r=["PRE%d" % (1 - par)], w=[pk_], eng="dve")
                    for g in range(NGh):
                        gg = hf * NGh + g
                        tok = slice(gg * 512, (gg + 1) * 512)
                        bk = self.pbank()
                        for kc in range(KC):
                            self.MM(self.ps[bk][:], wv[:, j, kc, :], self.XT[:, kc, tok], kc == 0, kc == KC - 1, r=self.WBK + xts[gg * 4:gg * 4 + 4], w=["ps%d" % bk])
                            if kc % 2 == 1 and kc < KC - 1:
                                yield
                        self.CP(pre[:, 3 + g * 512:3 + (g + 1) * 512], self.ps[bk][:], r=["ps%d" % bk], w=[pk_], eng="act")
                        yield
                    cc = j * 8 + h
                    cw = self.CW1
                    self.TS(acc, pre[:, 3:Sh + 3], cw[:, 3, cc:cc + 1], None, ALU.mult, r=[pk_, "CW1"], w=[ak_])
                    for t in range(3):
                        self.STT(acc, pre[:, t:Sh + t], cw[:, t, cc:cc + 1], acc, ALU.mult, ALU.add, r=[pk_, "CW1", ak_], w=[ak_])
                    yield
                    htok = slice(hf * Sh, (hf + 1) * Sh)
                    if nm == "v":
                        self.ACT(VT[hp][:, htok], acc, AF.Silu, r=[ak_], w=[vk_])
                        yield
                        continue
                    dst, dk_ = (QN[hp], qk_) if nm == "q" else (KN[hp], kk_)
                    for g in range(NGh):
                        gg = hf * NGh + g
                        tok = slice(gg * 512, (gg + 1) * 512)
                        self.ACT(SILg, acc[:, g * 512:(g + 1) * 512], AF.Silu, r=[ak_], w=["JK"])
                        self.ACT(SQ, SILg, AF.Square, r=["JK"], w=["JKb"])
                        bk = self.pbank()
                        self.MM(self.ps[bk][:], ONES, SQ, r=["JKb", "CM"], w=["ps%d" % bk])
                        self.ACT(SQ, self.ps[bk][:], AF.Sqrt, r=["ps%d" % bk], w=["JKb"], bias=self.CST[:, 0:1])
                        self.p.op("dve", lambda e, SQ=SQ: e.reciprocal(out=SQ, in_=SQ), ["JKb"], ["JKb"])
                        if nm == "q":
                            self.STT(dst[:, tok], SILg, 128 ** -0.5, SQ, ALU.mult, ALU.mult, r=["JK", "JKb"], w=[dk_])
                        else:
                            self.TT(dst[:, tok], SILg, SQ, ALU.mult, r=["JK", "JKb"], w=[dk_])
                        yield

        def prep(g, h):
            hp = h % 2
            qn, kn, vt = QN[hp], KN[hp], VT[hp]
            qk_, kk_, vk_ = "QN%d" % hp, "KN%d" % hp, "VT%d" % hp
            i0 = 4 * g
            pg = g % 2
            tl = list(enumerate(range(i0, i0 + 4)))
            tok4 = slice(i0 * 128, (i0 + 4) * 128)
            sm = ["dn_small"]
            for t, i in tl:
                self.TS(AREP[:, c4(t)], ONES, A[:, i, h:h + 1], None, ALU.mult, r=["CM"] + sm, w=["TC%d" % t])
            bT = self.bank()
            kT = "ps%d" % bT
            pbT = self.ps[bT][:].bitcast(BF16)
            for t, i in tl:
                self.TR(pbT[:, c4(t)], vt[:, c4(i)], r=[vk_], w=[kT])
                self.TR(pbT[:, 512 + t * 128:512 + (t + 1) * 128], kn[:, c4(i)], r=[kk_], w=[kT])
            yield
            bG = self.bank()
            kG = "ps%d" % bG
            for t, i in tl:
                self.MM(self.ps[bG][:, c4(t)], AREP[:, c4(t)], U2, r=["TC%d" % t, "CM"], w=[kG])
            for t, i in tl:
                kv = pbT[:, 512 + t * 128:512 + (t + 1) * 128]
                self.TS(VBt[:, c4(t)], pbT[:, c4(t)], BETA[:, i, h:h + 1], None, ALU.mult, r=[kT] + sm, w=["VBt%d" % t])
                self.TS(KBGt[:, c4(t)], kv, BEG[:, i, h:h + 1], None, ALU.mult, r=[kT] + sm, w=["KBGt%d" % t])
                self.TS(KD[pg][:, c4(t)], kv, EKD[:, i, h:h + 1], None, ALU.mult, r=[kT] + sm, w=["KD%d_%d" % (pg, t)])
            yield
            self.ACT(TA, self.ps[bG][:], AF.Exp, r=[kG], w=["TA"])
            for t, i in tl:
                self.STT(TB[:, c4(t)], self.ps[bG][:, c4(t)], GT[:, i, h:h + 1], PEN_S, ALU.subtract, ALU.add, r=[kG, "CM", "TA"] + sm, w=["TB%d" % t])
                self.STT(TC[:, c4(t)], self.ps[bG][:, c4(t)], GT[:, i, h:h + 1], PEN_IT, ALU.subtract, ALU.subtract, r=[kG, "CM", "TA"] + sm, w=["TC%d" % t])
            yield
            self.TT(QG[pg], qn[:, tok4], TA, ALU.mult, r=[qk_, "TA"], w=["QG%d" % pg])
            self.ACT(DM, TB, AF.Exp, r=TB4, w=TB4, scale=-1.0)
            self.ACT(DMT, TC, AF.Exp, r=TC4, w=TC4)
            bK = self.bank()
            bQ = self.bank()
            for t, i in tl:
                self.MM(self.ps[bK][:, c4(t)], kn[:, c4(i)], kn[:, c4(i)], r=[kk_], w=["ps%d" % bK])
            for t, i in tl:
                self.MM(self.ps[bQ][:, c4(t)], kn[:, c4(i)], qn[:, c4(i)], r=[kk_, qk_], w=["ps%d" % bQ])
            yield
            for t, i in tl:
                self.STT(MM_[:, c4(t)], self.ps[bK][:, c4(t)], BETA[:, i, h:h + 1], DM[:, c4(t)], ALU.mult, ALU.mult, r=["ps%d" % bK] + TB4 + sm, w=["M%d" % t])
            self.TT(QKT[pg], self.ps[bQ][:], DMT, ALU.mult, r=["ps%d" % bQ] + TC4, w=["QKT%d" % pg])
            yield
            bN = self.bank()
            pbN = self.ps[bN][:].bitcast(BF16)
            for t, i in tl:
                self.TR(pbN[:, c4(t)], MM_[:, c4(t)], r=["M%d" % t], w=["ps%d" % bN])
            self.CP(NN_, pbN[:, 0:512], r=["ps%d" % bN], w=["N"], eng="act")
            self.TT(g4(RR), IDB4, g4(NN_), ALU.subtract, r=["IDB", "N"], w=["R"])
            yield
            cm_, cn_, km, kn_ = MM_, NN_, "M", "N"
            M4 = ["M%d" % t for t in range(4)]
            for lev in range(1, 6):
                last = lev == 5
                if lev % 2 == 1:
                    nm_, nn_, km2, kn2 = MP, NP, "MP", "NP"
                else:
                    nm_, nn_, km2, kn2 = MM_, NN_, "M", "N"
                b1 = self.bank()
                for t, i in tl:
                    self.MM(self.ps[b1][:, c4(t)], cn_[:, c4(t)], cm_[:, c4(t)], r=[km, kn_] + M4, w=["ps%d" % b1])
                if not last:
                    b2 = self.bank()
                    for t, i in tl:
                        self.MM(self.ps[b2][:, c4(t)], cm_[:, c4(t)], cn_[:, c4(t)], r=[km, kn_] + M4, w=["ps%d" % b2])
                self.CP(nm_, self.ps[b1][:], r=["ps%d" % b1], w=[km2] + (M4 if km2 == "M" else []), eng="act")
                if not last:
                    self.CP(nn_, self.ps[b2][:], r=["ps%d" % b2], w=[kn2], eng="dve")
                yield
                b3 = self.bank()
                for t, i in tl:
                    self.MM(self.ps[b3][:, c4(t)], nm_[:, c4(t)], RR[:, c4(t)], r=[km2, "R"], w=["ps%d" % b3])
                self.TT(RR, self.ps[b3][:], RR, ALU.add, r=["ps%d" % b3, "R"], w=["R"])
                yield
                cm_, cn_, km, kn_ = nm_, nn_, km2, kn2
            bU = self.bank()
            bW = self.bank()
            for t, i in tl:
                self.MM(self.ps[bU][:, c4(t)], RR[:, c4(t)], VBt[:, c4(t)], r=["R", "VBt%d" % t], w=["ps%d" % bU])
            for t, i in tl:
                self.MM(self.ps[bW][:, c4(t)], KBGt[:, c4(t)], RR[:, c4(t)], r=["R", "KBGt%d" % t], w=["ps%d" % bW])
            self.CP(UU[pg], self.ps[bU][:], r=["ps%d" % bU], w=["UU%d" % pg], eng="act")
            self.CP(WT[pg], self.ps[bW][:], r=["ps%d" % bW], w=["WT%d" % pg], eng="dve")
            yield

        def rec(g, h):
            hp = h % 2
            i0 = 4 * g
            pg = g % 2
            for t, i in enumerate(range(i0, i0 + 4)):
                tk = c4(i)
                for half in range(2):
                    pr = slice(half * 64, (half + 1) * 64)
                    b1 = self.bank()
                    self.MM(self.ps[b1][:, 0:128], WT[pg][:, c4(t)], SB_, r=["WT%d" % pg, "SB"], w=["ps%d" % b1])
                    self.TT(VN[pr, :], UU[pg][pr, c4(t)], self.ps[b1][pr, 0:128], ALU.subtract, r=["UU%d" % pg, "ps%d" % b1], w=["VN"])
                    b2 = self.bank()
                    self.MM(self.ps[b2][:, 0:128], QG[pg][:, c4(t)], SB_, True, False, r=["QG%d" % pg, "SB"], w=["ps%d" % b2])
                    self.MM(self.ps[b2][:, 0:128], QKT[pg][pr, c4(t)], VN[pr, :], False, True, r=["QKT%d" % pg, "VN"], w=["ps%d" % b2])
                    self.CP(OT[pr, :], self.ps[b2][pr, 0:128], r=["ps%d" % b2], w=["OT"], eng="act")
                    b3 = self.bank()
                    self.MM(self.ps[b3][:, 0:128], KD[pg][pr, c4(t)], VN[pr, :], r=["KD%d_%d" % (pg, t), "VN"], w=["ps%d" % b3])
                    self.STT(SF, SF, EGL[:, half, i, h:h + 1], self.ps[b3][:, 0:128], ALU.mult, ALU.add, r=["SF", "dn_small", "ps%d" % b3], w=["SF"])
                    self.CP(SB_, SF, r=["SF"], w=["SB"], eng="act")
                    yield
                SSO = self.ST[:, 170:171]
                self.ACT(SQ[:, 0:128], OT, AF.Square, r=["OT"], w=["JKb", "ST_sso"], accum=SSO)
                self.ACT(SSO, SSO, AF.Sqrt, r=["ST_sso"], w=["ST_sso"], scale=1.0 / 128, bias=self.CST[:, 0:1])
                self.p.op("dve", lambda e, SSO=SSO: e.reciprocal(out=SSO, in_=SSO), ["ST_sso"], ["ST_sso"])
                self.STT(OB_, OT, SSO, self.ONG[:], ALU.mult, ALU.mult, r=["OT", "ST_sso", "ONG"], w=["OB"])
                bk = self.bank()
                pb = self.ps[bk][:].bitcast(BF16)
                self.TR(pb[:, 0:128], OB_, r=["OB"], w=["ps%d" % bk])
                self.TT(YHg[pg][:, c4(t)], pb[:, 0:128], ZS[hp][:, tk], ALU.mult, r=["ps%d" % bk, "ZS%d" % hp], w=["YH%d_%d" % (pg, t)])
                yield
                for cb in range(2):
                    bo = self.bank()
                    self.MM(self.ps[bo][:], YHg[pg][:, c4(t)], WO[:, cb * 512:(cb + 1) * 512], r=["YH%d_%d" % (pg, t), "WO"], w=["ps%d" % bo])
                    hv = self.H[:, i, cb * 512:(cb + 1) * 512]
                    self.TT(hv, self.ps[bo][:], hv, ALU.add, r=["ps%d" % bo, "H%d" % i], w=["H%d" % i])
                yield

        def alt(ga, gb):
            da = db = False
            while not (da and db):
                if not da:
                    try:
                        next(ga)
                    except StopIteration:
                        da = True
                if not db:
                    try:
                        next(gb)
                    except StopIteration:
                        db = True
                yield

        def Tgen(h):
            self.DMA(WO, self.wb["od_w_out"][h * 128:(h + 1) * 128, :], r=["wb_od_w_out"], w=["WO"])
            self.MS(SF, 0.0, w=["SF"], eng="dve")
            self.MS(SB_, 0.0, w=["SB"], eng="dve")
            for _ in prep(0, h):
                yield
            for g in range(NGRP):
                a_ = prep(g + 1, h) if g + 1 < NGRP else iter(())
                for _ in alt(a_, rec(g, h)):
                    yield

        for _ in Pgen(0):
            pass
        for h in range(8):
            nxt = Pgen(h + 1) if h + 1 < 8 else iter(())
            for _ in alt(Tgen(h), nxt):
                pass
        self.nrot = 5
        self.p.barrier()

    def dump(self, slot, b):
        if self.dbg:
            self.DMA(self.dbg_out[slot, b].rearrange("(n p) d -> p n d", p=128), self.H[:], r=["H%d" % i for i in range(self.NT)], w=["dbg"])

    def build(self):
        NT = self.NT
        self.prologue()
        for b in range(self.NSEQ):
            hk = ["H%d" % i for i in range(NT)]
            step = max(1, NT // 4)
            for i0 in range(0, NT, step):
                self.DMA(self.H[:, i0:i0 + step, :], self.x[b, i0 * 128:(i0 + step) * 128, :].rearrange("(n p) d -> p n d", p=128),
                         w=hk[i0:i0 + step])
            for l in self.layers:
                if l == 0:
                    self.layer0_mixer(b)
                else:
                    self.deltanet(b)
                self.dump(l * 3 + 0, b)
                self.ffn(l)
                self.dump(l * 3 + 1, b)
                self.ple(l, b)
                self.dump(l * 3 + 2, b)
            for i0 in range(0, NT, step):
                self.DMA(self.out[b, i0 * 128:(i0 + step) * 128, :].rearrange("(n p) d -> p n d", p=128), self.H[:, i0:i0 + step, :],
                         r=hk[i0:i0 + step], w=["out"])
            self.p.barrier()
        self.p.emit()
        self.es.close()
        return self.nc


_CACHE = {}


def _get_nc(S, NSEQ, topk):
    key = (S, NSEQ, topk)
    if key not in _CACHE:
        _CACHE[key] = Builder(S, NSEQ, topk).build()
    return _CACHE[key]


def make_in_maps(inputs, ncores):
    x = np.ascontiguousarray(inputs["x"], dtype=np.float32)
    B = x.shape[0]
    nseq = B // ncores
    in_maps = []
    for c in range(ncores):
        sl = slice(c * nseq, (c + 1) * nseq)
        m = {"x": x[sl], "p": np.ascontiguousarray(inputs["p"][:, sl], dtype=np.float32),
             "positions": np.ascontiguousarray(inputs["positions"][sl], dtype=np.int32)}
        for n in ("norm_mix_g", "norm_ffn_g", "ffn_w_gate", "ffn_w_up", "ffn_w_down", "ple_w_proj",
                  "ple_post_norm_g", "ple_norm_g", "ple_w_gate"):
            m[n] = np.ascontiguousarray(inputs[n], dtype=np.float32)
        for n in ("ev_w_in", "ev_w_out", "od_w_in", "od_w_out", "ev_conv_w", "od_conv_w"):
            m[n] = np.ascontiguousarray(inputs[n][0], dtype=np.float32)
        for n in ("ev_q_norm_g", "ev_k_norm_g", "ev_ik_ln_g", "ev_ik_ln_b", "od_a_log", "od_dt_bias", "od_o_norm_g"):
            m[n] = np.ascontiguousarray(inputs[n], dtype=np.float32)
        in_maps.append(m)
    return in_maps


def kernel(**inputs):
    B, S, _ = inputs["x"].shape
    ncores = 8
    nseq = B // ncores
    topk = min(256, S // 4)
    nc = _get_nc(S, nseq, topk)
    in_maps = make_in_maps(inputs, ncores)
    res = run_bass_kernel_spmd(nc, in_maps, core_ids=list(range(ncores)))
    return np.concatenate([np.asarray(r["out"]) for r in res.results], axis=0).astype(np.float32)
```

```python
import contextlib
import math
import os
import numpy as np
import concourse.bass as bass
import concourse.mybir as mybir
from concourse.bass_utils import run_bass_kernel_spmd

DT = mybir.dt
F32, BF16, I32 = DT.float32, DT.bfloat16, DT.int32
ALU = mybir.AluOpType
AF = mybir.ActivationFunctionType
AX = mybir.AxisListType

ENGS = ("pe", "act", "dve", "pool", "sp")
N_DMA_SEMS = 40


class _Op:
    __slots__ = ("eng", "fn", "reads", "writes", "pos", "dma", "waits", "signal",
                 "obs", "dsem", "dval", "gidx", "tick")


class Prog:
    def __init__(self, nc):
        self.nc = nc
        self.ops = []
        self.streams = {e: [] for e in ENGS}
        self.last_w = {}
        self.readers = {}
        self.n_dma = 0
        self.n_dma_sw = 0
        self.dma_last = {}
        self.dma_count = {}
        self.pending_dma = []

    def op(self, eng, fn, reads=(), writes=(), dma=False, extra=()):
        mx = int(os.environ.get("KDBG_MAXOPS", "0"))
        if mx and len(self.ops) >= mx and not self._force:
            return None
        o = _Op()
        o.eng, o.fn, o.dma = eng, fn, dma
        o.reads, o.writes = tuple(reads), tuple(writes)
        o.gidx = len(self.ops)
        o.pos = len(self.streams[eng])
        o.signal = False
        o.dsem = o.dval = None
        o.tick = 0
        deps = set(extra)
        for k in o.reads:
            w = self.last_w.get(k)
            if w is not None:
                deps.add(w)
            if k.startswith("ps"):
                for r in self.readers.get(k, ()):
                    if self.ops[r].eng != eng:
                        deps.add(r)
        for k in o.writes:
            w = self.last_w.get(k)
            if w is not None:
                deps.add(w)
            for r in self.readers.get(k, ()):
                deps.add(r)
        if dma:
            if eng == "pool" and not os.environ.get("KDBG_SHARED"):
                s = self.n_dma_sw % 8
                self.n_dma_sw += 1
            else:
                s = 8 + self.n_dma % (N_DMA_SEMS - 8)
                self.n_dma += 1
            prev = self.dma_last.get(s)
            if prev is not None:
                deps.add(prev)
            self.dma_last[s] = o.gidx
            self.dma_count[s] = self.dma_count.get(s, 0) + 1
            o.dsem, o.dval = s, 16 * self.dma_count[s]
            self.pending_dma.append(o.gidx)
        deps.discard(o.gidx)
        o.waits = deps
        for k in o.writes:
            self.last_w[k] = o.gidx
            self.readers[k] = []
        for k in o.reads:
            self.readers.setdefault(k, []).append(o.gidx)
        self.ops.append(o)
        self.streams[eng].append(o)
        return o

    _force = False

    def barrier(self):
        self._force = True
        dm = list(self.pending_dma)
        self.pending_dma = []
        firsts = []
        for e in ENGS:
            o = self.op(e, lambda eng: eng.drain(), extra=dm if e == "sp" else ())
            firsts.append(o.gidx)
        for e in ENGS:
            self.op(e, lambda eng: None, extra=firsts)
        self.last_w = {}
        self.readers = {}
        self._force = False

    def _analyze(self):
        ops = self.ops
        cur = {e: ({e2: -1 for e2 in ENGS}, {}) for e in ENGS}
        for o in ops:
            eobs, dobs = cur[o.eng]
            eobs = dict(eobs)
            dobs = dict(dobs)
            need = []
            for d in o.waits:
                a = ops[d]
                if a.dma:
                    if dobs.get(a.dsem, 0) >= a.dval:
                        continue
                    need.append(a)
                else:
                    if a.eng == o.eng and o.eng == "pe":
                        continue
                    if eobs[a.eng] >= a.pos:
                        continue
                    need.append(a)
            final = []
            for a in sorted(need, key=lambda a: -a.gidx):
                if a.dma:
                    if dobs.get(a.dsem, 0) >= a.dval:
                        continue
                else:
                    if eobs[a.eng] >= a.pos:
                        continue
                final.append(a)
                a.signal = True
                aeo, ado = a.obs
                for e2 in ENGS:
                    if aeo[e2] > eobs[e2]:
                        eobs[e2] = aeo[e2]
                for s, v in ado.items():
                    if v > dobs.get(s, 0):
                        dobs[s] = v
                if a.dma:
                    dobs[a.dsem] = max(dobs.get(a.dsem, 0), a.dval)
                else:
                    eobs[a.eng] = max(eobs[a.eng], a.pos)
            o.waits = final
            o.obs = (eobs, dobs)
            cur[o.eng] = (eobs, dobs)

    def emit(self):
        nc = self.nc
        self._analyze()
        es = contextlib.ExitStack()
        esem = {e: es.enter_context(nc.semaphore("tick_" + e)) for e in ENGS}
        dsem = [es.enter_context(nc.semaphore("dma%d" % i)) for i in range(N_DMA_SEMS)]
        for e in ENGS:
            c = 0
            for o in self.streams[e]:
                if o.dma:
                    continue
                if o.signal:
                    c += 1
                o.tick = c
        hw = {"pe": "tensor", "act": "scalar", "dve": "vector", "pool": "gpsimd", "sp": "sync"}
        blk = es.enter_context(nc.Block())
        stats = {"waits": 0, "ins": 0}

        def make(e):
            def body(eng):
                attach = os.environ.get("KDBG_ATTACH", "1") == "1"
                for o in self.streams[e]:
                    ws = list(o.waits)
                    probe = None
                    if attach and ws and o.fn is not None and not getattr(o, "noattach", False):
                        probe = ws.pop()
                    for a in ws:
                        if a.dma:
                            eng.wait_ge(dsem[a.dsem], a.dval)
                        else:
                            eng.wait_ge(esem[a.eng], a.tick)
                        stats["waits"] += 1
                    ins = o.fn(eng)
                    stats["ins"] += 1
                    if ins is None:
                        if probe is not None:
                            a = probe
                            if a.dma:
                                eng.wait_ge(dsem[a.dsem], a.dval)
                            else:
                                eng.wait_ge(esem[a.eng], a.tick)
                            stats["waits"] += 1
                        if o.signal or o.dma:
                            raise RuntimeError("signalling op without instruction")
                        continue
                    if probe is not None:
                        a = probe
                        if a.dma:
                            ins._wait_ge(dsem[a.dsem], a.dval)
                        else:
                            ins._wait_ge(esem[a.eng], a.tick)
                    if o.dma:
                        ins.then_inc(dsem[o.dsem], 16)
                    elif o.signal:
                        ins.then_inc(esem[e], 1)
            return body
        for e in ENGS:
            getattr(blk, hw[e])(make(e))
        es.close()
        self.stats = stats


D = 1024
KC = 8
DFF = 2816
FCN = 22
PLE = 256
EV_IN = 3656
OD_IN = 4112
EPS = 1e-6
NIT = 16
NEG = -1.0e30
PEN = 1.0e4


def _split(n):
    for s in (1, 2, 4, 8, 16):
        if n % s == 0 and n // s <= 2048:
            return s
    raise ValueError(n)


class Builder:
    def __init__(self, S, NSEQ, topk, layers=(0, 1), stages=None, dbg=False):
        self.S, self.NSEQ, self.topk = S, NSEQ, topk
        self.NT, self.NG = S // 128, S // 512
        self.layers = layers
        self.stages = stages
        self.dbg = dbg
        self.nc = nc = bass.Bass("TRN2", target_bir_lowering=False)
        self.p = Prog(nc)
        self.es = contextlib.ExitStack()
        self._rot = 0
        self.nrot = 5
        self._decl()
        self._alloc()

    def _decl(self):
        nc, S, NSEQ = self.nc, self.S, self.NSEQ
        di = lambda n, s, d=F32: nc.dram_tensor(n, list(s), d, kind="ExternalInput").ap()
        self.x = di("x", [NSEQ, S, D])
        self.pin = di("p", [2, NSEQ, S, PLE])
        self.pos = di("positions", [NSEQ, S], I32)
        self.w = {}
        for n, s in (("norm_mix_g", [2, D]), ("norm_ffn_g", [2, D]), ("ev_w_in", [D, EV_IN]),
                     ("ev_conv_w", [3, 512]), ("ev_q_norm_g", [1, 64]), ("ev_k_norm_g", [1, 64]),
                     ("ev_ik_ln_g", [1, 64]), ("ev_ik_ln_b", [1, 64]), ("ev_w_out", [D, D]),
                     ("od_w_in", [D, OD_IN]), ("od_conv_w", [4, 3072]), ("od_a_log", [1, 8]),
                     ("od_dt_bias", [1, 8]), ("od_o_norm_g", [1, 128]), ("od_w_out", [D, D]),
                     ("ffn_w_gate", [2, D, DFF]), ("ffn_w_up", [2, D, DFF]), ("ffn_w_down", [2, DFF, D]),
                     ("ple_w_proj", [2, PLE, D]), ("ple_post_norm_g", [2, D]), ("ple_norm_g", [2, D]),
                     ("ple_w_gate", [2, D, D])):
            self.w[n] = di(n, s)
        self.out = nc.dram_tensor("out", [NSEQ, S, D], F32, kind="ExternalOutput").ap()
        if self.dbg:
            self.dbg_out = nc.dram_tensor("dbg", [8, NSEQ, S, D], F32, kind="ExternalOutput").ap()
        ds = lambda n, s: nc.dram_tensor(n, list(s), BF16, kind="Internal").ap()
        self.wb = {
            "ev_w_in": ds("b_ev_w_in", [D, EV_IN]), "ev_w_out": ds("b_ev_w_out", [D, D]),
            "od_w_in": ds("b_od_w_in", [D, OD_IN]), "od_w_out": ds("b_od_w_out", [D, D]),
            "ffn_w_gate": ds("b_ffn_w_gate", [2, D, DFF]), "ffn_w_up": ds("b_ffn_w_up", [2, D, DFF]),
            "ffn_w_down": ds("b_ffn_w_down", [2, DFF, D]), "ple_w_proj": ds("b_ple_w_proj", [2, PLE, D]),
            "ple_w_gate": ds("b_ple_w_gate", [2, D, D]),
        }

    def _alloc(self):
        nc, S, NT = self.nc, self.S, self.NT
        sb = lambda n, s, d: self.es.enter_context(nc.sbuf_tensor(n, list(s), d))
        self.H = sb("H", [128, NT, D], F32)
        self.XT = sb("XT", [128, KC, S], BF16)
        self.YT = sb("YT", [128, KC, S], BF16)
        self.AR_N = 27136
        self.AR = sb("AR", [128, self.AR_N], BF16)
        self.WB = sb("WB", [128, 4096], BF16)
        self.GREP = sb("GREP", [128, D], F32)
        self.IDB = sb("IDB", [128, 128], BF16)
        self.CM = sb("CM", [128, 7, 128], F32)
        self.G64 = sb("G64", [128, 4, 64], F32)
        self.ONG = sb("ONG", [128, 128], F32)
        self.AD = sb("AD", [128, 2, 8], F32)
        self.CW0 = sb("CW0", [128, 3, 4], F32)
        self.CW1 = sb("CW1", [128, 4, 24], F32)
        self.INV = sb("INV", [128, 48], F32)
        self.P2 = sb("P2", [128, 2, NIT], F32)
        self.CST = sb("CST", [128, 8], F32)
        ya = self.YT[:, 0:4, :].rearrange("p k t -> p (k t)")
        self.YA = ya
        self.SIN = ya[:, 0:NT * 96].bitcast(F32).rearrange("p (t f) -> p t f", f=48)
        self.COS = ya[:, NT * 96:NT * 192].bitcast(F32).rearrange("p (t f) -> p t f", f=48)
        self.ST = sb("ST", [128, 512], F32)
        self.WI = sb("WI", [128, NT, 8], F32)
        self.JK = sb("JK", [128, 2048], BF16)
        self.ps = [self.es.enter_context(nc.psum_tensor("ps%d" % i, [128, 512], F32)) for i in range(8)]

    def pbank(self):
        self._pb = getattr(self, "_pb", 0) + 1
        return 6 + self._pb % 2

    def bank(self):
        nrot = self.nrot
        i = self._rot % nrot
        self._rot += 1
        return i

    def MM(self, out, lhsT, rhs, start=True, stop=True, r=(), w=()):
        self.p.op("pe", lambda e: e.matmul(out, lhsT=lhsT, rhs=rhs, start=start, stop=stop), r, w)

    def TR(self, out, in_, r=(), w=()):
        idb = self.IDB[:]
        self.p.op("pe", lambda e: e.transpose(out=out, in_=in_, identity=idb), list(r) + ["IDB"], w)

    def ACT(self, out, in_, func, r=(), w=(), bias=None, scale=None, accum=None):
        kw = {}
        if bias is not None:
            kw["bias"] = bias
        if scale is not None:
            kw["scale"] = scale
        if accum is not None:
            kw["accum_out"] = accum
        self.p.op("act", lambda e: e.activation(out=out, in_=in_, func=func, **kw), r, w)

    def TS(self, out, in0, s1, s2, op0, op1=None, r=(), w=(), accum=None, eng="dve"):
        kw = {}
        if op1 is not None:
            kw["op1"] = op1
        if accum is not None:
            kw["accum_out"] = accum
        self.p.op(eng, lambda e: e.tensor_scalar(out=out, in0=in0, scalar1=s1, scalar2=s2, op0=op0, **kw), r, w)

    def TT(self, out, in0, in1, op, r=(), w=(), eng="dve"):
        self.p.op(eng, lambda e: e.tensor_tensor(out=out, in0=in0, in1=in1, op=op), r, w)

    def STT(self, out, in0, scalar, in1, op0, op1, r=(), w=()):
        self.p.op("dve", lambda e: e.scalar_tensor_tensor(out=out, in0=in0, scalar=scalar, in1=in1, op0=op0, op1=op1), r, w)

    def CP(self, out, in_, r=(), w=(), eng="dve"):
        if eng == "act":
            self.p.op("act", lambda e: e.copy(out=out, in_=in_), r, w)
        else:
            self.p.op(eng, lambda e: e.tensor_copy(out=out, in_=in_), r, w)

    def MS(self, ap, val, w=(), eng="pool"):
        self.p.op(eng, lambda e: e.memset(ap, val), (), w)

    def DMA(self, out, in_, r=(), w=(), eng="sp", nc_ok=False):
        w = [w] if isinstance(w, str) else list(w)
        if nc_ok:
            self.p.op(eng, lambda e: e.dma_start(out=out, in_=in_, allow_slow_non_contiguous=True), r, w, dma=True)
        else:
            self.p.op(eng, lambda e: e.dma_start(out=out, in_=in_), r, w, dma=True)

    def negreg(self, e):
        if getattr(self, "_negreg", None) is None:
            self._negreg = e.to_reg(NEG)
        return self._negreg

    def arv(self, off, n, dt=BF16):
        ne = n * (2 if dt == F32 else 1)
        assert off + ne <= self.AR_N, (off, ne)
        v = self.AR[:, off:off + ne]
        return v.bitcast(F32) if dt == F32 else v

    def prologue(self):
        p = self.p
        for n, dst in self.wb.items():
            src = self.w[n]
            if len(src.shape) == 3:
                pairs = [(src[l], dst[l]) for l in range(src.shape[0])]
            else:
                pairs = [(src, dst)]
            for s_, d_ in pairs:
                ns = _split(s_.shape[1])
                sv = s_.rearrange("k (s n) -> (k s) n", s=ns)
                dv = d_.rearrange("k (s n) -> (k s) n", s=ns)
                rows = sv.shape[0]
                step = 1024
                for r0 in range(0, rows, step):
                    r1 = min(rows, r0 + step)
                    self.DMA(dv[r0:r1, :], sv[r0:r1, :], w=["wb_" + n], eng="pool")
        self.MS(self.IDB[:], 1.0, w=["IDB"])
        idb = self.IDB[:]
        p.op("pool", lambda e: e.affine_select(out=idb, in_=idb, pattern=[[-1, 128]], compare_op=ALU.is_equal,
                                               fill=0.0, base=0, channel_multiplier=1), ["IDB"], ["IDB"])
        cm = self.CM
        self.MS(cm[:, 0, :], 1.0, w=["CM"])
        self.MS(cm[:, 1, :], 1.0, w=["CM"])
        u2 = cm[:, 1, :]
        p.op("pool", lambda e: e.affine_select(out=u2, in_=u2, pattern=[[1, 128]], compare_op=ALU.is_ge,
                                               fill=0.0, base=0, channel_multiplier=-1), ["CM"], ["CM"])
        self.MS(cm[0:64, 1, 64:128], 0.0, w=["CM"])
        self.MS(cm[:, 2, :], 0.0, w=["CM"])
        self.MS(cm[0:64, 2, 0:64], 1.0, w=["CM"])
        self.MS(cm[64:128, 2, 64:128], 1.0, w=["CM"])
        self.MS(cm[:, 3, :], 0.0, w=["CM"])
        self.MS(cm[0:64, 3, :], 1.0, w=["CM"])
        self.MS(cm[:, 4, :], 0.0, w=["CM"])
        self.MS(cm[64:128, 4, :], 1.0, w=["CM"])
        self.MS(cm[:, 5, :], 0.0, w=["CM"])
        ps_ = cm[:, 5, :]
        p.op("pool", lambda e: e.affine_select(out=ps_, in_=ps_, pattern=[[-1, 128]], compare_op=ALU.is_ge,
                                               fill=PEN, base=-1, channel_multiplier=1), ["CM"], ["CM"])
        self.MS(cm[64:128, 5, 0:64], PEN, w=["CM"])
        self.MS(cm[:, 6, :], 0.0, w=["CM"])
        pi_ = cm[:, 6, :]
        p.op("pool", lambda e: e.affine_select(out=pi_, in_=pi_, pattern=[[1, 128]], compare_op=ALU.is_ge,
                                               fill=PEN, base=0, channel_multiplier=-1), ["CM"], ["CM"])
        self.MS(cm[0:64, 6, 64:128], PEN, w=["CM"])
        self.MS(self.CST[:, 0:1], EPS, w=["CST"])
        self.MS(self.CST[:, 1:2], 1.0, w=["CST"])
        self.MS(self.CST[:, 2:3], 0.0, w=["CST"])
        inv_a = (1.0 / (np.float32(10000.0) ** (np.arange(0, 64, 2, dtype=np.float32) / np.float32(64)))).astype(np.float32)
        inv_i = (1.0 / (np.float32(10000.0) ** (np.arange(0, 32, 2, dtype=np.float32) / np.float32(32)))).astype(np.float32)
        for i, v in enumerate(list(inv_a) + list(inv_i)):
            self.MS(self.INV[:, i:i + 1], float(v), w=["INV"])
        for i in range(NIT):
            self.MS(self.P2[:, 0, i:i + 1], 2.0 ** -(i + 2), w=["P2"])
            self.MS(self.P2[:, 1, i:i + 1], 2.0 ** -(i + 1), w=["P2"])
        w = self.w
        for i, n in enumerate(("ev_q_norm_g", "ev_k_norm_g", "ev_ik_ln_g", "ev_ik_ln_b")):
            self.DMA(self.G64[:, i, :], w[n][0:1, :].to_broadcast([128, 64]), w=["G64"])
        self.DMA(self.ONG[:], w["od_o_norm_g"][0:1, :].to_broadcast([128, 128]), w=["ONG"])
        self.DMA(self.AD[:, 0, :], w["od_a_log"][0:1, :].to_broadcast([128, 8]), w=["AD"])
        self.DMA(self.AD[:, 1, :], w["od_dt_bias"][0:1, :].to_broadcast([128, 8]), w=["AD"])
        for k_ in range(3):
            self.DMA(self.CW0[:, k_, :], w["ev_conv_w"][k_].rearrange("(c p) -> p c", p=128), w=["CW0"], nc_ok=True)
        for k_ in range(4):
            self.DMA(self.CW1[:, k_, :], w["od_conv_w"][k_].rearrange("(c p) -> p c", p=128), w=["CW1"], nc_ok=True)
        self.ACT(self.AD[:, 0, :], self.AD[:, 0, :], AF.Exp, r=["AD"], w=["AD"])
        self.TS(self.AD[:, 0, :], self.AD[:, 0, :], -1.0, None, ALU.mult, r=["AD"], w=["AD"])
        self.p.barrier()

    def load_w(self, dst, name, r0, r1, c0, c1, l=None, wkey=()):
        src = self.wb[name]
        if l is not None:
            src = src[l]
        self.DMA(dst, src[r0:r1, c0:c1].rearrange("(kc p) n -> p kc n", p=128), r=["wb_" + name], w=wkey)

    WBK = ["WB0", "WB1"]

    def rope_tables(self, b):
        S, NT = self.S, self.NT
        ar = 0
        POSI = self.ST[:, 0:NT].bitcast(I32)
        POSF = self.ST[:, 16:16 + NT]
        self.DMA(POSI, self.pos[b].rearrange("(n p) -> p n", p=128), w=["ST"], nc_ok=True)
        self.CP(POSF, POSI, r=["ST"], w=["ST"])
        n = NT * 48
        ANG = self.arv(ar, n, F32).rearrange("p (t f) -> p t f", f=48); ar += 2 * n
        KF = self.arv(ar, n, F32).rearrange("p (t f) -> p t f", f=48); ar += 2 * n
        KI = self.arv(ar, n, F32).bitcast(I32).rearrange("p (t f) -> p t f", f=48); ar += 2 * n
        R2 = self.arv(ar, n, F32).rearrange("p (t f) -> p t f", f=48); ar += 2 * n
        k = ["ropetmp"]
        self.TT(ANG, self.INV[:].unsqueeze(1).to_broadcast([128, NT, 48]),
                POSF.unsqueeze(2).to_broadcast([128, NT, 48]), ALU.mult, r=["INV", "ST"], w=k)
        self.TS(KF, ANG, 1.0 / (2 * math.pi), None, ALU.mult, r=k, w=k)
        self.CP(KI, KF, r=k, w=k)
        self.CP(KF, KI, r=k, w=k)
        C1 = 6.28125
        C2 = 2 * math.pi - C1
        self.STT(ANG, KF, -C1, ANG, ALU.mult, ALU.add, r=k, w=k)
        self.STT(ANG, KF, -C2, ANG, ALU.mult, ALU.add, r=k, w=k)

        def wrap(T):
            self.TS(KF, T, math.pi, 2 * math.pi, ALU.is_gt, ALU.mult, r=k, w=k)
            self.TT(T, T, KF, ALU.subtract, r=k, w=k)
            self.TS(KF, T, -math.pi, 2 * math.pi, ALU.is_lt, ALU.mult, r=k, w=k)
            self.TT(T, T, KF, ALU.add, r=k, w=k)
        wrap(ANG)
        self.TS(R2, ANG, math.pi / 2, None, ALU.add, r=k, w=k)
        wrap(R2)
        self.ACT(self.SIN, ANG, AF.Sin, r=k, w=["SIN"])
        self.ACT(self.COS, R2, AF.Sin, r=k, w=["COS"])

    def norm_T(self, gname, l):
        NT = self.NT
        gslot = self.GREP[:]
        self.DMA(gslot, self.w[gname][l:l + 1, :].to_broadcast([128, D]), w=["GREP"])
        SS = self.ST[:, 32:32 + NT]
        RS = self.ST[:, 48:48 + NT]
        for i in range(NT):
            self.ACT(self.JK[:, 0:D], self.H[:, i, :], AF.Square, r=["H%d" % i], w=["JK", "ST_ss%d" % i], accum=SS[:, i:i + 1])
        self.ACT(RS, SS, AF.Sqrt, r=["ST_ss%d" % i for i in range(NT)], w=["ST_rs"], scale=1.0 / D, bias=self.CST[:, 0:1])
        self.p.op("dve", lambda e: e.reciprocal(out=RS, in_=RS), ["ST_rs"], ["ST_rs"])
        for i in range(NT):
            hn = self.JK[:, D:2 * D] if i % 2 == 0 else self.JK[:, 0:D]
            hk = "JKb" if i % 2 == 0 else "JK"
            self.STT(hn, self.H[:, i, :], RS[:, i:i + 1], gslot, ALU.mult, ALU.mult, r=["H%d" % i, "ST_rs", "GREP"], w=[hk])
            b = self.bank()
            pb = self.ps[b][:].bitcast(BF16)
            for kc in range(KC):
                self.TR(pb[:, kc * 128:(kc + 1) * 128], hn[:, kc * 128:(kc + 1) * 128], r=[hk], w=["ps%d" % b])
            self.CP(self.XT[:, :, i * 128:(i + 1) * 128], pb.rearrange("p (k t) -> p k t", t=128),
                    r=["ps%d" % b], w=["XT%d" % i], eng="act")

    def all_xt(self):
        return ["XT%d" % i for i in range(self.NT)]

    def conv_part(self):
        S, NT, NG = self.S, self.NT, self.NG
        ar = 0
        U = self.arv(ar, S + 2, F32); ar += 2 * (S + 2)
        ZB = self.arv(ar, S, F32); ar += 2 * S
        ACC = self.arv(ar, S, F32); ar += 2 * S
        ZC = self.arv(ar, 1024, F32).rearrange("p (a n) -> p a n", a=2); ar += 2048
        self.MS(U[:, 0:2], 0.0, w=["U"], eng="dve")
        xts = self.all_xt()
        wv = self.WB[:, 0:3 * KC * 128].rearrange("p (j k n) -> p j k n", j=3, k=KC)
        for c in range(4):
            for j, base in enumerate((512, 1024, 0)):
                self.load_w(wv[:, j], "ev_w_in", 0, D, base + c * 128, base + (c + 1) * 128, wkey=self.WBK)
            for g in range(NG):
                tok = slice(g * 512, (g + 1) * 512)
                for j in range(3):
                    b = self.bank()
                    for kc in range(KC):
                        self.MM(self.ps[b][:], wv[:, j, kc, :], self.XT[:, kc, tok], kc == 0, kc == KC - 1,
                                r=self.WBK + xts[g * 4:(g + 1) * 4], w=["ps%d" % b])
                    if j == 0:
                        self.CP(ZC[:, g % 2, :], self.ps[b][:], r=["ps%d" % b], w=["ZC%d" % (g % 2)], eng="act")
                    elif j == 1:
                        self.TT(U[:, 2 + g * 512:2 + (g + 1) * 512], self.ps[b][:], ZC[:, g % 2, :], ALU.mult,
                                r=["ps%d" % b, "ZC%d" % (g % 2)], w=["U"])
                    else:
                        self.CP(ZB[:, tok], self.ps[b][:], r=["ps%d" % b], w=["ZB"], eng="act")
            cw = self.CW0
            self.TS(ACC, U[:, 2:S + 2], cw[:, 2, c:c + 1], None, ALU.mult, r=["U", "CW0"], w=["ACC"])
            self.STT(ACC, U[:, 1:S + 1], cw[:, 1, c:c + 1], ACC, ALU.mult, ALU.add, r=["U", "CW0", "ACC"], w=["ACC"])
            self.STT(ACC, U[:, 0:S], cw[:, 0, c:c + 1], ACC, ALU.mult, ALU.add, r=["U", "CW0", "ACC"], w=["ACC"])
            self.TT(self.YT[:, c, :], ACC, ZB, ALU.mult, r=["ACC", "ZB"], w=["YTa%d" % c])

    def rope(self, dst, src, nh, half, f0, i, tmp, rk, wk):
        c = self.COS[:, i, f0:f0 + half].unsqueeze(1).to_broadcast([128, nh, half])
        s = self.SIN[:, i, f0:f0 + half].unsqueeze(1).to_broadcast([128, nh, half])
        x1 = src[:, :, 0:half]
        x2 = src[:, :, half:2 * half]
        t1 = tmp[:, 0:nh * half].rearrange("p (h d) -> p h d", h=nh)
        t2 = tmp[:, nh * half:2 * nh * half].rearrange("p (h d) -> p h d", h=nh)
        rr = list(rk) + ["SIN", "COS"]
        k1, k2 = "rt1_%d" % self._rp, "rt2_%d" % self._rp
        self.TT(t1, x1, c, ALU.mult, r=rr, w=[k1])
        self.TT(t2, x2, s, ALU.mult, r=rr, w=[k2])
        self.TT(dst[:, :, 0:half], t1, t2, ALU.subtract, r=[k1, k2], w=wk)
        self.TT(t1, x1, s, ALU.mult, r=rr, w=[k1])
        self.TT(t2, x2, c, ALU.mult, r=rr, w=[k2])
        self.TT(dst[:, :, half:2 * half], t1, t2, ALU.add, r=[k1, k2], w=wk)

    def l0_layout(self):
        S, NT = self.S, self.NT
        ar = 0
        L = {}
        L["QT"] = self.YT[:, 4:8, :]
        L["KT"] = self.arv(ar, 4 * S).rearrange("p (h t) -> p h t", h=4); ar += 4 * S
        L["QIT"] = self.arv(ar, 4 * S).rearrange("p (h t) -> p h t", h=4); ar += 4 * S
        L["V"] = self.arv(ar, NT * 8 * 65).rearrange("p (t h d) -> p t h d", t=NT, h=8); ar += NT * 8 * 65
        L["KIT"] = self.arv(ar, S); ar += S
        L["end"] = ar
        return L

    def proj_T(self, L):
        S, NT = self.S, self.NT
        o = NT * 192
        TMPs, QBFs = [], []
        for par in range(2):
            if par == 0 or S >= 2048:
                TMPs.append(self.YA[:, o:o + 2048].bitcast(F32)); o += 2048
                QBFs.append(self.YA[:, o:o + 512]); o += 512
            else:
                e0 = L["end"] + (L["end"] % 2)
                TMPs.append(self.AR[:, e0:e0 + 2048].bitcast(F32))
                QBFs.append(self.AR[:, e0 + 2048:e0 + 2560])
        SQs = [self.JK[:, 0:1024].bitcast(F32), self.JK[:, 1024:2048].bitcast(F32)]
        self.MS(L["V"][:, :, :, 64:65], 1.0, w=["Vones"], eng="dve")
        blocks = (("q", 1536, 512), ("k", 2048, 512), ("v", 2560, 512), ("qi", 3072, 512), ("kw", 3584, 72))
        for bi, (nm, c0, ncol) in enumerate(blocks):
            wk = self.WBK
            wv = self.WB[:, 0:KC * ncol].rearrange("p (k n) -> p k n", k=KC)
            self.load_w(wv, "ev_w_in", 0, D, c0, c0 + ncol, wkey=wk)
            for i in range(NT):
                tk = slice(i * 128, (i + 1) * 128)
                b = self.bank()
                pk = "ps%d" % b
                psv = self.ps[b][:, 0:ncol]
                for kc in range(KC):
                    self.MM(psv, self.XT[:, kc, tk], wv[:, kc, :], kc == 0, kc == KC - 1, r=wk + ["XT%d" % i], w=[pk])
                pp = i % 2
                TMP, QBF, SQ = TMPs[pp], QBFs[pp], SQs[pp]
                QF = TMP[:, 0:512].rearrange("p (h d) -> p h d", h=8)
                RT = TMP[:, 512:1024]
                kJK, kQF, kJQ, kSQ = "JK%d" % pp, "QF%d" % pp, "JKq%d" % pp, "ST_q%d" % pp
                self._rp = pp
                if nm in ("q", "k"):
                    gi = 0 if nm == "q" else 1
                    ps3 = psv.rearrange("p (h d) -> p h d", h=8)
                    self.ACT(SQ, psv, AF.Square, r=[pk], w=[kJK])
                    SSQ = self.ST[:, 64 + 8 * pp:72 + 8 * pp]
                    self.p.op("dve", lambda e, SSQ=SSQ, SQ=SQ: e.tensor_reduce(out=SSQ, in_=SQ.rearrange("p (h d) -> p h d", h=8),
                                                                               axis=AX.X, op=ALU.add), [kJK], [kSQ])
                    self.ACT(SSQ, SSQ, AF.Sqrt, r=[kSQ], w=[kSQ], scale=1.0 / 64, bias=self.CST[:, 0:1])
                    self.p.op("dve", lambda e, SSQ=SSQ: e.reciprocal(out=SSQ, in_=SSQ), [kSQ], [kSQ])
                    self.TT(QF, ps3, SSQ.unsqueeze(2).to_broadcast([128, 8, 64]), ALU.mult, r=[pk, kSQ], w=[kQF])
                    self.TT(QF, QF, self.G64[:, gi, :].unsqueeze(1).to_broadcast([128, 8, 64]), ALU.mult, r=[kQF, "G64"], w=[kQF])
                    QB = QBF.rearrange("p (h d) -> p h d", h=8)
                    self.rope(QB, QF, 8, 32, 0, i, RT, [kQF], [kJQ])
                    b2 = self.bank()
                    pb = self.ps[b2][:].bitcast(BF16)
                    for hp in range(4):
                        self.TR(pb[:, hp * 128:(hp + 1) * 128], QBF[:, hp * 128:(hp + 1) * 128], r=[kJQ], w=["ps%d" % b2])
                    dst = L["QT" if nm == "q" else "KT"]
                    self.CP(dst[:, :, tk], pb[:, 0:512].rearrange("p (h t) -> p h t", h=4), r=["ps%d" % b2],
                            w=["%s%d" % ("QT" if nm == "q" else "KT", i)], eng="act")
                elif nm == "v":
                    self.CP(L["V"][:, i, :, 0:64], psv.rearrange("p (h d) -> p h d", h=8), r=[pk], w=["V%d" % i], eng="act")
                elif nm == "qi":
                    ps3 = psv.rearrange("p (h d) -> p h d", h=8)
                    QB = QBF.rearrange("p (h d) -> p h d", h=8)
                    self.rope(QB, ps3, 8, 16, 32, i, RT, [pk], [kJQ])
                    self.CP(QB[:, :, 32:64], ps3[:, :, 32:64], r=[pk], w=[kJQ], eng="act")
                    b2 = self.bank()
                    pb = self.ps[b2][:].bitcast(BF16)
                    for hp in range(4):
                        self.TR(pb[:, hp * 128:(hp + 1) * 128], QBF[:, hp * 128:(hp + 1) * 128], r=[kJQ], w=["ps%d" % b2])
                    self.CP(L["QIT"][:, :, tk], pb[:, 0:512].rearrange("p (h t) -> p h t", h=4), r=["ps%d" % b2], w=["QIT%d" % i], eng="act")
                else:
                    BN = self.ST[:, 80 + 16 * pp:86 + 16 * pp]
                    MV = self.ST[:, 88 + 16 * pp:90 + 16 * pp]
                    kip = psv[:, 0:64]
                    self.p.op("dve", lambda e, BN=BN, kip=kip: e.bn_stats(out=BN, in_=kip), [pk], ["ST_bn%d" % pp])
                    self.p.op("dve", lambda e, BN=BN, MV=MV: e.bn_aggr(out=MV, in_=BN), ["ST_bn%d" % pp], ["ST_mv%d" % pp])
                    RSD = self.ST[:, 90 + 16 * pp:91 + 16 * pp]
                    self.ACT(RSD, MV[:, 1:2], AF.Sqrt, r=["ST_mv%d" % pp], w=["ST_rsd%d" % pp], bias=self.CST[:, 0:1])
                    self.p.op("dve", lambda e, RSD=RSD: e.reciprocal(out=RSD, in_=RSD), ["ST_rsd%d" % pp], ["ST_rsd%d" % pp])
                    KF_ = TMP[:, 0:64]
                    self.TS(KF_, kip, MV[:, 0:1], RSD, ALU.subtract, ALU.mult, r=[pk, "ST_mv%d" % pp, "ST_rsd%d" % pp], w=[kQF])
                    self.TT(KF_, KF_, self.G64[:, 2, :], ALU.mult, r=[kQF, "G64"], w=[kQF])
                    self.TT(KF_, KF_, self.G64[:, 3, :], ALU.add, r=[kQF, "G64"], w=[kQF])
                    KB_ = QBF[:, 0:128]
                    self.rope(KB_[:, 0:64].unsqueeze(1), KF_.unsqueeze(1), 1, 16, 32, i, RT, [kQF], [kJQ])
                    self.CP(KB_[:, 32:64], KF_[:, 32:64], r=[kQF], w=[kJQ], eng="act")
                    self.CP(KB_[:, 64:128], KB_[:, 0:64], r=[kJQ], w=[kJQ], eng="dve")
                    b2 = self.bank()
                    pb = self.ps[b2][:].bitcast(BF16)
                    self.TR(pb[:, 0:128], KB_, r=[kJQ], w=["ps%d" % b2])
                    self.CP(L["KIT"][:, tk], pb[:, 0:128], r=["ps%d" % b2], w=["KIT%d" % i], eng="act")
                    self.TS(self.WI[:, i, :], psv[:, 64:72], (8 ** -0.5) * (64 ** -0.5), None, ALU.mult, r=[pk], w=["WI%d" % i])

    def attention(self, L):
        S, NT, topk = self.S, self.NT, self.topk
        QT, KT, QIT, V, KIT = L["QT"], L["KT"], L["QIT"], L["V"], L["KIT"]
        xt = self.XT[:].rearrange("p k t -> p (k t)")
        o = 0
        SC, MSK, MT = [], [], []
        for par in range(2):
            SC.append(xt[:, o:o + 2 * S].bitcast(F32)); o += 2 * S
        for par in range(2):
            MSK.append(xt[:, o:o + S]); o += S
        for par in range(2):
            MT.append(xt[:, o:o + S].rearrange("p (k q) -> p k q", q=128)); o += S
        o2 = 0
        RL = self.YA[:, o2:o2 + 2048].bitcast(F32).rearrange("p (a n) -> p a n", a=2); o2 += 2048
        PT = self.YA[:, o2:o2 + 2048].rearrange("p (a n) -> p a n", a=4); o2 += 2048
        YB = self.JK[:, 0:512]
        st = self.ST
        RD = st[:, 150:158]
        OB = (5, 6)
        meng = "pool" if os.environ.get("KDBG_POOLMASK", "1") == "1" else "dve"

        def score(j):
            par = j % 2
            sc, sk = SC[par], "SC%d" % par
            qk = slice(j * 128, (j + 1) * 128)
            nkb = j + 1
            n = nkb * 128
            nch = (nkb + 3) // 4
            MX, MN = st[:, 100 + par:101 + par], st[:, 102 + par:103 + par]
            for ch in range(nch):
                k0 = ch * 512
                kw = min(512, n - k0)
                kkeys = ["KIT%d" % t for t in range(ch * 4, min(nkb, ch * 4 + 4))]
                for h in range(8):
                    hp, hh = h // 2, h % 2
                    pr = slice(hh * 64, (hh + 1) * 64)
                    b = self.bank()
                    pk = "ps%d" % b
                    self.MM(self.ps[b][:, 0:kw], QIT[pr, hp, qk], KIT[pr, k0:k0 + kw], r=["QIT%d" % j] + kkeys, w=[pk])
                    rs = h % 2
                    self.ACT(RL[:, rs, 0:kw], self.ps[b][:, 0:kw], AF.Relu, r=[pk], w=["RL%d" % rs])
                    if h == 0:
                        self.TS(sc[:, k0:k0 + kw], RL[:, rs, 0:kw], self.WI[:, j, 0:1], None, ALU.mult,
                                r=["RL%d" % rs, "WI%d" % j], w=[sk])
                    else:
                        self.STT(sc[:, k0:k0 + kw], RL[:, rs, 0:kw], self.WI[:, j, h:h + 1], sc[:, k0:k0 + kw], ALU.mult, ALU.add,
                                 r=["RL%d" % rs, "WI%d" % j, sk], w=[sk])
                    yield
            if n > topk:
                self.p.op("dve", lambda e, MX=MX, s_=sc[:, 0:n]: e.tensor_reduce(out=MX, in_=s_, axis=AX.X, op=ALU.max), [sk], ["bs_mx%d" % par])
                yield
                self.p.op("dve", lambda e, MN=MN, s_=sc[:, 0:n]: e.tensor_reduce(out=MN, in_=s_, axis=AX.X, op=ALU.min), [sk], ["bs_mn%d" % par])
                yield
            dg = sc[:, j * 128:(j + 1) * 128]
            self.p.op("pool", lambda e, dg=dg: e.affine_select(out=dg, in_=dg, pattern=[[-1, 128]], compare_op=ALU.is_ge,
                                                               fill=self.negreg(e), base=0, channel_multiplier=1), [sk], [sk])
            yield

        def attn(j):
            par = j % 2
            sc, sk = SC[par], "SC%d" % par
            msk, mk = MSK[par], "MSK%d" % par
            mt, mtk = MT[par], "MT%d" % par
            qk = slice(j * 128, (j + 1) * 128)
            nkb = j + 1
            n = nkb * 128
            nch = (nkb + 3) // 4
            MX, MN = st[:, 100 + par:101 + par], st[:, 102 + par:103 + par]
            RG, TH, CNT, DD, THF = (st[:, 104 + i:105 + i] for i in range(5))
            STEP = st[:, 112:112 + NIT]
            STEP2 = st[:, 128:128 + NIT]
            if n > topk:
                self.TT(RG, MX, MN, ALU.subtract, r=["bs_mx%d" % par, "bs_mn%d" % par], w=["bs_rg"])
                self.TS(STEP, self.P2[:, 0, :], RG, None, ALU.mult, r=["P2", "bs_rg"], w=["bs_step"])
                self.TS(STEP2, self.P2[:, 1, :], RG, None, ALU.mult, r=["P2", "bs_rg"], w=["bs_step2"])
                self.STT(TH, MX, 1.0, MN, ALU.mult, ALU.add, r=["bs_mx%d" % par, "bs_mn%d" % par], w=["bs_th"])
                self.TS(TH, TH, 0.5, None, ALU.mult, r=["bs_th"], w=["bs_th"])
                yield
                for it in range(NIT):
                    self.TS(msk[:, 0:n], sc[:, 0:n], TH, None, ALU.is_ge, ALU.add, r=[sk, "bs_th"], w=[mk, "bs_cnt"], accum=CNT)
                    self.TS(DD, CNT, float(topk) - 0.5, STEP2[:, it:it + 1], ALU.is_ge, ALU.mult, r=["bs_cnt", "bs_step2"], w=["bs_dd"])
                    self.STT(TH, DD, STEP[:, it:it + 1], TH, ALU.subtract, ALU.add, r=["bs_dd", "bs_step", "bs_th"], w=["bs_th"])
                    yield
                self.STT(THF, RG, -(2.0 ** -(NIT + 1)), TH, ALU.mult, ALU.add, r=["bs_rg", "bs_th"], w=["bs_thf"])
                self.TS(msk[:, 0:n], sc[:, 0:n], THF, None, ALU.is_ge, r=[sk, "bs_thf"], w=[mk])
            else:
                self.TS(msk[:, 0:n], sc[:, 0:n], 0.5 * NEG, None, ALU.is_ge, r=[sk], w=[mk])
            yield
            for g0 in range(0, nkb, 8):
                g1 = min(nkb, g0 + 8)
                b = self.bank()
                pb = self.ps[b][:].bitcast(BF16)
                for kb in range(g0, g1):
                    self.TR(pb[:, (kb - g0) * 128:(kb - g0 + 1) * 128], msk[:, kb * 128:(kb + 1) * 128], r=[mk], w=["ps%d" % b])
                self.CP(mt[:, g0:g1, :], pb[:, 0:(g1 - g0) * 128].rearrange("p (k q) -> p k q", q=128), r=["ps%d" % b], w=[mtk], eng="act")
                yield

        def attc(j):
            par = j % 2
            mt, mtk = MT[par], "MT%d" % par
            qk = slice(j * 128, (j + 1) * 128)
            nkb = j + 1
            nch = (nkb + 3) // 4
            items = [(h, ch) for h in range(8) for ch in range(nch)]

            def front(k):
                h, ch = items[k]
                hp, hh = h // 2, h % 2
                pr = slice(hh * 64, (hh + 1) * 64)
                kb0 = ch * 4
                kb1 = min(nkb, kb0 + 4)
                kw = (kb1 - kb0) * 128
                b = self.bank()
                pk = "ps%d" % b
                for kb in range(kb0, kb1):
                    self.MM(self.ps[b][:, (kb - kb0) * 128:(kb - kb0 + 1) * 128], KT[pr, hp, kb * 128:(kb + 1) * 128], QT[pr, hp, qk],
                            r=["KT%d" % kb, "QT%d" % j], w=[pk])
                ps_ = k % 4
                self.ACT(PT[:, ps_, 0:kw], self.ps[b][:, 0:kw], AF.Exp, r=[pk], w=["PT%d" % ps_], scale=0.125)
                self.TT(PT[:, ps_, 0:kw], PT[:, ps_, 0:kw], mt[:, kb0:kb1, :].rearrange("p k q -> p (k q)"), ALU.mult,
                        r=["PT%d" % ps_, mtk], w=["PT%d" % ps_], eng=meng)

            def back(k):
                h, ch = items[k]
                ob = OB[h // 4]
                ov = self.ps[ob][:, 0:260].rearrange("p (h d) -> p h d", h=4)[:, h % 4, :]
                kb0 = ch * 4
                kb1 = min(nkb, kb0 + 4)
                ps_ = k % 4
                for kb in range(kb0, kb1):
                    self.MM(ov, PT[:, ps_, (kb - kb0) * 128:(kb - kb0 + 1) * 128], V[:, kb, h, :], kb == 0, kb == nkb - 1,
                            r=["PT%d" % ps_, "V%d" % kb, "Vones"], w=["ps%d" % ob])
            front(0)
            if len(items) > 1:
                front(1)
            yield
            for k in range(len(items)):
                if k + 2 < len(items):
                    front(k + 2)
                back(k)
                yield
            for half in range(2):
                ob = OB[half]
                o3 = self.ps[ob][:, 0:260].rearrange("p (h d) -> p h d", h=4)
                rd = RD[:, half * 4:half * 4 + 4]
                self.p.op("dve", lambda e, rd=rd, o3=o3: e.reciprocal(out=rd, in_=o3[:, :, 64]), ["ps%d" % ob], ["ST_rd%d" % half])
                self.TT(YB[:, half * 256:(half + 1) * 256].rearrange("p (h d) -> p h d", h=4), o3[:, :, 0:64],
                        rd.unsqueeze(2).to_broadcast([128, 4, 64]), ALU.mult, r=["ps%d" % ob, "ST_rd%d" % half], w=["YB"])
            b = self.bank()
            pb = self.ps[b][:].bitcast(BF16)
            for c in range(4):
                self.TR(pb[:, c * 128:(c + 1) * 128], YB[:, c * 128:(c + 1) * 128], r=["YB"], w=["ps%d" % b])
            self.CP(self.YT[:, 4:8, qk], pb[:, 0:512].rearrange("p (c t) -> p c t", c=4), r=["ps%d" % b], w=["YTb%d" % j, "QT%d" % j], eng="act")
            yield

        def merge(gens):
            live = list(gens)
            while live:
                nxt = []
                for g_ in live:
                    try:
                        next(g_)
                        nxt.append(g_)
                    except StopIteration:
                        pass
                live = nxt
        merge([score(0)])
        merge([attn(0)] + ([score(1)] if NT > 1 else []))
        for j in range(NT):
            gl = [attc(j)]
            if j + 1 < NT:
                gl.append(attn(j + 1))
            if j + 2 < NT:
                gl.append(score(j + 2))
            merge(gl)

    def out_proj(self, name, k0, nk, src, rk):
        NT = self.NT
        for cb in range(2):
            wv = self.WB[:, 0:nk * 512].rearrange("p (k n) -> p k n", k=nk)
            self.load_w(wv, name, k0 * 128, (k0 + nk) * 128, cb * 512, (cb + 1) * 512, wkey=self.WBK)
            for i in range(NT):
                b = self.bank()
                for k in range(nk):
                    self.MM(self.ps[b][:], src(k, i), wv[:, k, :], k == 0, k == nk - 1, r=self.WBK + rk(i), w=["ps%d" % b])
                hv = self.H[:, i, cb * 512:(cb + 1) * 512]
                self.TT(hv, self.ps[b][:], hv, ALU.add, r=["ps%d" % b, "H%d" % i], w=["H%d" % i])

    def layer0_mixer(self, b):
        self.norm_T("norm_mix_g", 0)
        self.p.barrier()
        self.conv_part()
        tks = lambda i: slice(i * 128, (i + 1) * 128)
        self.out_proj("ev_w_out", 0, 4, lambda k, i: self.YT[:, k, tks(i)], lambda i: ["YTa%d" % c for c in range(4)])
        self.p.barrier()
        self.rope_tables(b)
        self.p.barrier()
        L = self.l0_layout()
        self.proj_T(L)
        self.p.barrier()
        self.attention(L)
        self.out_proj("ev_w_out", 4, 4, lambda k, i: self.YT[:, 4 + k, tks(i)], lambda i: ["YTb%d" % i])
        self.p.barrier()

    def ffn(self, l):
        S, NT, NG = self.S, self.NT, self.NG
        self.norm_T("norm_ffn_g", l)
        ar = 0
        AT = self.arv(ar, FCN * 512).rearrange("p (f t) -> p f t", f=FCN); ar += FCN * 512
        WD = []
        for s in range(2):
            WD.append(self.arv(ar, FCN * 256).rearrange("p (f n) -> p f n", f=FCN)); ar += FCN * 256
        SG = self.arv(ar, 512, F32); ar += 1024
        nwd = 0
        nfb = 0
        for g in range(NG):
            tok = slice(g * 512, (g + 1) * 512)
            xk = ["XT%d" % i for i in range(g * 4, g * 4 + 4)]
            for f in range(FCN):
                slot = nfb % 2
                nfb += 1
                wk = self.WBK[slot]
                wv = self.WB[:, slot * 2048:(slot + 1) * 2048].rearrange("p (j k n) -> p j k n", j=2, k=KC)
                self.load_w(wv[:, 0], "ffn_w_gate", 0, D, f * 128, (f + 1) * 128, l=l, wkey=[wk])
                self.load_w(wv[:, 1], "ffn_w_up", 0, D, f * 128, (f + 1) * 128, l=l, wkey=[wk])
                bg = self.bank()
                bu = self.bank()
                for kc in range(KC):
                    self.MM(self.ps[bg][:], wv[:, 0, kc, :], self.XT[:, kc, tok], kc == 0, kc == KC - 1, r=[wk] + xk, w=["ps%d" % bg])
                for kc in range(KC):
                    self.MM(self.ps[bu][:], wv[:, 1, kc, :], self.XT[:, kc, tok], kc == 0, kc == KC - 1, r=[wk] + xk, w=["ps%d" % bu])
                self.ACT(SG, self.ps[bg][:], AF.Silu, r=["ps%d" % bg], w=["SG"])
                self.TT(AT[:, f, :], SG, self.ps[bu][:], ALU.mult, r=["SG", "ps%d" % bu], w=["AT%d" % f])
            atk = ["AT%d" % f for f in range(FCN)]
            for cb in range(4):
                slot = nwd % 2
                nwd += 1
                wk = "WD%d" % slot
                src = self.wb["ffn_w_down"][l][:, cb * 256:(cb + 1) * 256].rearrange("(f p) n -> p f n", p=128)
                self.DMA(WD[slot], src, r=["wb_ffn_w_down"], w=[wk])
                for ti in range(4):
                    i = g * 4 + ti
                    b = self.bank()
                    for f in range(FCN):
                        self.MM(self.ps[b][:, 0:256], AT[:, f, ti * 128:(ti + 1) * 128], WD[slot][:, f, :], f == 0, f == FCN - 1,
                                r=[wk] + atk, w=["ps%d" % b])
                    hv = self.H[:, i, cb * 256:(cb + 1) * 256]
                    self.TT(hv, self.ps[b][:, 0:256], hv, ALU.add, r=["ps%d" % b, "H%d" % i], w=["H%d" % i])
        self.p.barrier()

    def ple(self, l, b):
        S, NT = self.S, self.NT
        self.norm_T("ple_norm_g", l)
        ar = 0
        gpost = self.arv(ar, D, F32); ar += 2 * D
        self.DMA(gpost, self.w["ple_post_norm_g"][l:l + 1, :].to_broadcast([128, D]), w=["GPOST"])
        WP = self.arv(ar, 2 * D).rearrange("p (k n) -> p k n", k=2); ar += 2 * D
        self.load_w(WP, "ple_w_proj", 0, PLE, 0, D, l=l, wkey=["WP"])
        PTT = self.arv(ar, 2 * S).rearrange("p (k t) -> p k t", k=2); ar += 2 * S
        PF = []
        PB = []
        for s in range(2):
            PF.append(self.arv(ar, 256, F32)); ar += 512
            PB.append(self.arv(ar, 256)); ar += 256
        SGM = self.arv(ar, 512, F32); ar += 1024
        EE = self.arv(ar, 512, F32); ar += 1024
        RSE = self.ST[:, 176:176 + NT]
        SS2 = self.ST[:, 200:200 + 2 * NT].rearrange("p (t c) -> p t c", c=2)
        eb = (5, 6)
        for i in range(NT):
            s = i % 2
            tk = slice(i * 128, (i + 1) * 128)
            self.DMA(PF[s], self.pin[l, b, i * 128:(i + 1) * 128, :], w=["PF%d" % s])
            self.CP(PB[s], PF[s], r=["PF%d" % s], w=["PB%d" % s], eng="pool")
            bt = self.bank()
            pb = self.ps[bt][:].bitcast(BF16)
            for k in range(2):
                self.TR(pb[:, k * 128:(k + 1) * 128], PB[s][:, k * 128:(k + 1) * 128], r=["PB%d" % s], w=["ps%d" % bt])
            self.CP(PTT[:, :, tk], pb[:, 0:256].rearrange("p (k t) -> p k t", k=2), r=["ps%d" % bt], w=["PTT%d" % i], eng="act")
            for cb in range(2):
                for k in range(2):
                    self.MM(self.ps[eb[cb]][:], PTT[:, k, tk], WP[:, k, cb * 512:(cb + 1) * 512], k == 0, k == 1,
                            r=["PTT%d" % i, "WP"], w=["ps%d" % eb[cb]])
                self.ACT(self.JK[:, 0:1024].bitcast(F32), self.ps[eb[cb]][:], AF.Square, r=["ps%d" % eb[cb]], w=["JK", "ST_sse%d_%d" % (i, cb)],
                         accum=SS2[:, i, cb:cb + 1])
        allss = ["ST_sse%d_%d" % (i, cb) for i in range(NT) for cb in range(2)]
        self.TT(RSE, SS2[:, :, 0], SS2[:, :, 1], ALU.add, r=allss, w=["ST_rse"])
        self.ACT(RSE, RSE, AF.Sqrt, r=["ST_rse"], w=["ST_rse"], scale=1.0 / D, bias=self.CST[:, 0:1])
        self.p.op("dve", lambda e: e.reciprocal(out=RSE, in_=RSE), ["ST_rse"], ["ST_rse"])
        for cb in range(2):
            cs = slice(cb * 512, (cb + 1) * 512)
            wg = self.WB[:].rearrange("p (k n) -> p k n", k=KC)
            self.load_w(wg, "ple_w_gate", 0, D, cb * 512, (cb + 1) * 512, l=l, wkey=self.WBK)
            for i in range(NT):
                tk = slice(i * 128, (i + 1) * 128)
                be = self.bank()
                for k in range(2):
                    self.MM(self.ps[be][:], PTT[:, k, tk], WP[:, k, cs], k == 0, k == 1, r=["PTT%d" % i, "WP"], w=["ps%d" % be])
                self.STT(EE, self.ps[be][:], RSE[:, i:i + 1], gpost[:, cs], ALU.mult, ALU.mult, r=["ps%d" % be, "ST_rse", "GPOST"], w=["EE"])
                bg = self.bank()
                for kc in range(KC):
                    self.MM(self.ps[bg][:], self.XT[:, kc, tk], wg[:, kc, :], kc == 0, kc == KC - 1, r=self.WBK + ["XT%d" % i], w=["ps%d" % bg])
                self.ACT(SGM, self.ps[bg][:], AF.Sigmoid, r=["ps%d" % bg], w=["SGM"])
                self.TT(EE, EE, SGM, ALU.mult, r=["EE", "SGM"], w=["EE"])
                hv = self.H[:, i, cs]
                self.TT(hv, hv, EE, ALU.add, r=["EE", "H%d" % i], w=["H%d" % i])
        self.p.barrier()

    def deltanet(self, b):
        S, NT, NG = self.S, self.NT, self.NG
        CM = self.CM
        ONES, U2, BD, HALF0, HALF1, PEN_S, PEN_IT = (CM[:, i, :] for i in range(7))
        self.norm_T("norm_mix_g", 1)
        ar = 0
        NH = NT * 8

        def f32v(n):
            nonlocal ar
            v = self.arv(ar, n, F32)
            ar += 2 * n
            return v

        def bfv(n):
            nonlocal ar
            v = self.arv(ar, n)
            ar += n + (n % 2)
            return v
        th = lambda v: v.rearrange("p (t h) -> p t h", h=8)
        A, BETA, GT, EG, BEG, EKD, T1, T2 = (th(f32v(NH)) for _ in range(8))
        EGL = f32v(2 * NH).rearrange("p (a t h) -> p a t h", a=2, h=8)
        RAW = f32v(NT * 16).rearrange("p (t n) -> p t n", n=16)
        wab = self.WB[:, 0:KC * 16].rearrange("p (k n) -> p k n", k=KC)
        self.load_w(wab, "od_w_in", 0, D, 4096, 4112, wkey=self.WBK)
        for i in range(NT):
            bk = self.bank()
            for kc in range(KC):
                self.MM(self.ps[bk][:, 0:16], self.XT[:, kc, i * 128:(i + 1) * 128], wab[:, kc, :], kc == 0, kc == KC - 1,
                        r=self.WBK + ["XT%d" % i], w=["ps%d" % bk])
            self.CP(RAW[:, i, :], self.ps[bk][:, 0:16], r=["ps%d" % bk], w=["RAW"], eng="act")
        k = ["dn_small"]
        self.TT(T1, RAW[:, :, 0:8], self.AD[:, 1, :].unsqueeze(1).to_broadcast([128, NT, 8]), ALU.add, r=["RAW", "AD"], w=k)
        self.TS(T2, T1, -1.0, None, ALU.mult, r=k, w=k)
        self.TT(T2, T1, T2, ALU.min, r=k, w=k)
        self.ACT(T2, T2, AF.Exp, r=k, w=k)
        self.ACT(T2, T2, AF.Ln, r=k, w=k, bias=self.CST[:, 1:2])
        self.STT(T1, T1, 0.0, T2, ALU.max, ALU.add, r=k, w=k)
        self.TT(A, T1, self.AD[:, 0, :].unsqueeze(1).to_broadcast([128, NT, 8]), ALU.mult, r=k + ["AD"], w=k)
        self.ACT(BETA, RAW[:, :, 8:16], AF.Sigmoid, r=["RAW"], w=k)
        fl = lambda v: v.rearrange("p t h -> p (t h)")
        Af = fl(A)
        bk = self.bank()
        self.MM(self.ps[bk][:, 0:NH], U2, Af, r=k + ["CM"], w=["ps%d" % bk])
        self.CP(fl(GT), self.ps[bk][:, 0:NH], r=["ps%d" % bk], w=k, eng="act")
        bk = self.bank()
        self.MM(self.ps[bk][:, 0:NH], BD, Af, r=k + ["CM"], w=["ps%d" % bk])
        self.TT(fl(T1), self.ps[bk][:, 0:NH], fl(GT), ALU.subtract, r=["ps%d" % bk] + k, w=k)
        self.ACT(EKD, T1, AF.Exp, r=k, w=k)
        self.ACT(EG, GT, AF.Exp, r=k, w=k)
        self.TT(BEG, BETA, EG, ALU.mult, r=k, w=k)
        for half, HM in enumerate((HALF0, HALF1)):
            bk = self.bank()
            self.MM(self.ps[bk][:, 0:NH], HM, Af, r=k + ["CM"], w=["ps%d" % bk])
            self.ACT(EGL[:, half].rearrange("p t h -> p (t h)"), self.ps[bk][:, 0:NH], AF.Exp, r=["ps%d" % bk], w=k)
        self.nrot = 6
        yt = self.YT[:].rearrange("p k t -> p (k t)")
        yo = 0

        def ytv(n, dt=BF16):
            nonlocal yo
            ne = n * (2 if dt == F32 else 1)
            v = yt[:, yo:yo + ne]
            yo += ne
            assert yo <= 8 * S
            return v.bitcast(F32) if dt == F32 else v
        Sh = S // 2
        NGh = Sh // 512
        QN = [ytv(S), ytv(S)]
        KN = [ytv(S), ytv(S)]
        VT = [ytv(S), ytv(S)]
        ZS = [ytv(S), ytv(S)]
        PRE = [f32v(Sh + 4), f32v(Sh + 4)]
        ACC = [f32v(Sh), f32v(Sh)]
        g4 = lambda v: v.rearrange("p (t c) -> p t c", c=128)
        TA, TB, TC = (f32v(512) for _ in range(3))
        AREP, DM, DMT = TC, TB, TC
        MM_, NN_, MP, NP, RR, VBt, KBGt = (bfv(512) for _ in range(7))
        QG, KD, WT, QKT, UU, YHg = ([bfv(512), bfv(512)] for _ in range(6))
        WO = bfv(1024)
        SF = f32v(128)
        SB_ = bfv(128)
        VN = bfv(128)
        OT = f32v(512)
        OB4 = self.GREP[:, 0:256].bitcast(BF16)
        SILg = self.JK[:, 0:1024].bitcast(F32)
        SQ = self.JK[:, 1024:2048].bitcast(F32)
        wv = self.WB[:].rearrange("p (j k n) -> p j k n", j=4, k=KC)
        xts = self.all_xt()
        IDB4 = self.IDB[:].unsqueeze(1).to_broadcast([128, 4, 128])
        NGRP = NT // 4
        c4 = lambda t: slice(t * 128, (t + 1) * 128)
        TC4 = ["TC%d" % t for t in range(4)]
        TB4 = ["TB%d" % t for t in range(4)]

        def Pgen(h):
            hp = h % 2
            qk_, kk_, vk_, zk_ = "QN%d" % hp, "KN%d" % hp, "VT%d" % hp, "ZS%d" % hp
            for j in range(4):
                self.load_w(wv[:, j], "od_w_in", 0, D, j * 1024 + h * 128, j * 1024 + (h + 1) * 128, wkey=self.WBK)
            for g in range(NG):
                tok = slice(g * 512, (g + 1) * 512)
                bk = self.pbank()
                for kc in range(KC):
                    self.MM(self.ps[bk][:], wv[:, 3, kc, :], self.XT[:, kc, tok], kc == 0, kc == KC - 1, r=self.WBK + xts[g * 4:g * 4 + 4], w=["ps%d" % bk])
                    if kc % 2 == 1 and kc < KC - 1:
                        yield
                self.ACT(ZS[hp][:, tok], self.ps[bk][:], AF.Silu, r=["ps%d" % bk], w=[zk_])
                yield
            unit = 0
            for j, nm in enumerate(("q", "k", "v")):
                for hf in range(2):
                    par = unit % 2
                    unit += 1
                    pre, acc = PRE[par], ACC[par]
                    pk_, ak_ = "PRE%d" % par, "ACC%d" % par
                    if hf == 0:
                        self.MS(pre[:, 0:3], 0.0, w=[pk_], eng="dve")
                    else:
                        self.CP(pre[:, 0:3], PRE[1 - par][:, Sh:Sh + 3], r=["PRE%d" % (1 - par)], w=[pk_], eng="dve")
                    for g in range(NGh):
                        gg = hf * NGh + g
                        tok = slice(gg * 512, (gg + 1) * 512)
                        bk = self.pbank()
                        for kc in range(KC):
                            self.MM(self.ps[bk][:], wv[:, j, kc, :], self.XT[:, kc, tok], kc == 0, kc == KC - 1, r=self.WBK + xts[gg * 4:gg * 4 + 4], w=["ps%d" % bk])
                            if kc % 2 == 1 and kc < KC - 1:
                                yield
                        self.CP(pre[:, 3 + g * 512:3 + (g + 1) * 512], self.ps[bk][:], r=["ps%d" % bk], w=[pk_], eng="act")
                        yield
                    cc = j * 8 + h
                    cw = self.CW1
                    self.TS(acc, pre[:, 3:Sh + 3], cw[:, 3, cc:cc + 1], None, ALU.mult, r=[pk_, "CW1"], w=[ak_])
                    for t in range(3):
                        self.STT(acc, pre[:, t:Sh + t], cw[:, t, cc:cc + 1], acc, ALU.mult, ALU.add, r=[pk_, "CW1", ak_], w=[ak_])
                    yield
                    htok = slice(hf * Sh, (hf + 1) * Sh)
                    if nm == "v":
                        self.ACT(VT[hp][:, htok], acc, AF.Silu, r=[ak_], w=[vk_])
                        yield
                        continue
                    dst, dk_ = (QN[hp], qk_) if nm == "q" else (KN[hp], kk_)
                    for g in range(NGh):
                        gg = hf * NGh + g
                        tok = slice(gg * 512, (gg + 1) * 512)
                        self.ACT(SILg, acc[:, g * 512:(g + 1) * 512], AF.Silu, r=[ak_], w=["JK"])
                        self.ACT(SQ, SILg, AF.Square, r=["JK"], w=["JKb"])
                        bk = self.pbank()
                        self.MM(self.ps[bk][:], ONES, SQ, r=["JKb", "CM"], w=["ps%d" % bk])
                        self.ACT(SQ, self.ps[bk][:], AF.Sqrt, r=["ps%d" % bk], w=["JKb"], bias=self.CST[:, 0:1])
                        self.p.op("dve", lambda e, SQ=SQ: e.reciprocal(out=SQ, in_=SQ), ["JKb"], ["JKb"])
                        if nm == "q":
                            self.STT(dst[:, tok], SILg, 128 ** -0.5, SQ, ALU.mult, ALU.mult, r=["JK", "JKb"], w=[dk_])
                        else:
                            self.TT(dst[:, tok], SILg, SQ, ALU.mult, r=["JK", "JKb"], w=[dk_])
                        yield

        def prep(g, h):
            hp = h % 2
            qn, kn, vt = QN[hp], KN[hp], VT[hp]
            qk_, kk_, vk_ = "QN%d" % hp, "KN%d" % hp, "VT%d" % hp
            i0 = 4 * g
            pg = g % 2
            tl = list(enumerate(range(i0, i0 + 4)))
            tok4 = slice(i0 * 128, (i0 + 4) * 128)
            sm = ["dn_small"]
            bc = lambda v: v[:, i0:i0 + 4, h:h + 1].to_broadcast([128, 4, 128])
            c3 = lambda v: v.unsqueeze(1).to_broadcast([128, 4, 128])
            self.TT(g4(AREP), c3(ONES), bc(A), ALU.mult, r=["CM"] + sm, w=TC4)
            bT = self.bank()
            kT = "ps%d" % bT
            pbT = self.ps[bT][:].bitcast(BF16)
            for t, i in tl:
                self.TR(pbT[:, c4(t)], vt[:, c4(i)], r=[vk_], w=[kT])
                self.TR(pbT[:, 512 + t * 128:512 + (t + 1) * 128], kn[:, c4(i)], r=[kk_], w=[kT])
            yield
            bG = self.bank()
            kG = "ps%d" % bG
            for t, i in tl:
                self.MM(self.ps[bG][:, c4(t)], AREP[:, c4(t)], U2, r=["TC%d" % t, "CM"], w=[kG])
            pv3 = g4(pbT[:, 0:512])
            pk3 = g4(pbT[:, 512:1024])
            self.TT(g4(VBt), pv3, bc(BETA), ALU.mult, r=[kT] + sm, w=["VBt%d" % t for t in range(4)])
            self.TT(g4(KBGt), pk3, bc(BEG), ALU.mult, r=[kT] + sm, w=["KBGt%d" % t for t in range(4)])
            self.TT(g4(KD[pg]), pk3, bc(EKD), ALU.mult, r=[kT] + sm, w=["KD%d_%d" % (pg, t) for t in range(4)])
            yield
            self.ACT(TA, self.ps[bG][:], AF.Exp, r=[kG], w=["TA"])
            self.TT(g4(TB), g4(self.ps[bG][:]), bc(GT), ALU.subtract, r=[kG, "TA"] + sm, w=TB4)
            self.TT(g4(TC), g4(TB), c3(PEN_IT), ALU.subtract, r=TB4 + ["CM"], w=TC4)
            self.TT(g4(TB), g4(TB), c3(PEN_S), ALU.add, r=TB4 + ["CM"], w=TB4)
            yield
            self.TT(QG[pg], qn[:, tok4], TA, ALU.mult, r=[qk_, "TA"], w=["QG%d" % pg])
            self.ACT(DM, TB, AF.Exp, r=TB4, w=TB4, scale=-1.0)
            self.ACT(DMT, TC, AF.Exp, r=TC4, w=TC4)
            bK = self.bank()
            bQ = self.bank()
            for t, i in tl:
                self.MM(self.ps[bK][:, c4(t)], kn[:, c4(i)], kn[:, c4(i)], r=[kk_], w=["ps%d" % bK])
            for t, i in tl:
                self.MM(self.ps[bQ][:, c4(t)], kn[:, c4(i)], qn[:, c4(i)], r=[kk_, qk_], w=["ps%d" % bQ])
            yield
            self.TT(g4(TA), g4(self.ps[bK][:]), bc(BETA), ALU.mult, r=["ps%d" % bK, "QG%d" % pg] + sm, w=["TA"])
            self.TT(MM_, TA, DM, ALU.mult, r=["TA"] + TB4, w=["M%d" % t for t in range(4)])
            self.TT(QKT[pg], self.ps[bQ][:], DMT, ALU.mult, r=["ps%d" % bQ] + TC4, w=["QKT%d" % pg])
            yield
            bN = self.bank()
            pbN = self.ps[bN][:].bitcast(BF16)
            for t, i in tl:
                self.TR(pbN[:, c4(t)], MM_[:, c4(t)], r=["M%d" % t], w=["ps%d" % bN])
            self.CP(NN_, pbN[:, 0:512], r=["ps%d" % bN], w=["N"], eng="act")
            self.TT(g4(RR), IDB4, g4(NN_), ALU.subtract, r=["IDB", "N"], w=["R"])
            yield
            cm_, cn_, km, kn_ = MM_, NN_, "M", "N"
            M4 = ["M%d" % t for t in range(4)]
            for lev in range(1, 6):
                last = lev == 5
                if lev % 2 == 1:
                    nm_, nn_, km2, kn2 = MP, NP, "MP", "NP"
                else:
                    nm_, nn_, km2, kn2 = MM_, NN_, "M", "N"
                b1 = self.bank()
                for t, i in tl:
                    self.MM(self.ps[b1][:, c4(t)], cn_[:, c4(t)], cm_[:, c4(t)], r=[km, kn_] + M4, w=["ps%d" % b1])
                if not last:
                    b2 = self.bank()
                    for t, i in tl:
                        self.MM(self.ps[b2][:, c4(t)], cm_[:, c4(t)], cn_[:, c4(t)], r=[km, kn_] + M4, w=["ps%d" % b2])
                self.CP(nm_, self.ps[b1][:], r=["ps%d" % b1], w=[km2] + (M4 if km2 == "M" else []), eng="act")
                if not last:
                    self.CP(nn_, self.ps[b2][:], r=["ps%d" % b2], w=[kn2], eng="dve")
                yield
                b3 = self.bank()
                for t, i in tl:
                    self.MM(self.ps[b3][:, c4(t)], nm_[:, c4(t)], RR[:, c4(t)], r=[km2, "R"], w=["ps%d" % b3])
                self.TT(RR, self.ps[b3][:], RR, ALU.add, r=["ps%d" % b3, "R"], w=["R"])
                yield
                cm_, cn_, km, kn_ = nm_, nn_, km2, kn2
            bU = self.bank()
            bW = self.bank()
            for t, i in tl:
                self.MM(self.ps[bU][:, c4(t)], RR[:, c4(t)], VBt[:, c4(t)], r=["R", "VBt%d" % t], w=["ps%d" % bU])
            for t, i in tl:
                self.MM(self.ps[bW][:, c4(t)], KBGt[:, c4(t)], RR[:, c4(t)], r=["R", "KBGt%d" % t], w=["ps%d" % bW])
            self.CP(UU[pg], self.ps[bU][:], r=["ps%d" % bU], w=["UU%d" % pg], eng="act")
            self.CP(WT[pg], self.ps[bW][:], r=["ps%d" % bW], w=["WT%d" % pg], eng="dve")
            yield

        def rec(g, h):
            hp = h % 2
            i0 = 4 * g
            pg = g % 2
            for t, i in enumerate(range(i0, i0 + 4)):
                tk = c4(i)
                for half in range(2):
                    pr = slice(half * 64, (half + 1) * 64)
                    b1 = self.bank()
                    self.MM(self.ps[b1][:, 0:128], WT[pg][:, c4(t)], SB_, r=["WT%d" % pg, "SB"], w=["ps%d" % b1])
                    self.TT(VN[pr, :], UU[pg][pr, c4(t)], self.ps[b1][pr, 0:128], ALU.subtract, r=["UU%d" % pg, "ps%d" % b1], w=["VN"])
                    b2 = self.bank()
                    self.MM(self.ps[b2][:, 0:128], QG[pg][:, c4(t)], SB_, True, False, r=["QG%d" % pg, "SB"], w=["ps%d" % b2])
                    self.MM(self.ps[b2][:, 0:128], QKT[pg][pr, c4(t)], VN[pr, :], False, True, r=["QKT%d" % pg, "VN"], w=["ps%d" % b2])
                    self.CP(OT[pr, c4(t)], self.ps[b2][pr, 0:128], r=["ps%d" % b2], w=["OT%d" % t], eng="act")
                    b3 = self.bank()
                    self.MM(self.ps[b3][:, 0:128], KD[pg][pr, c4(t)], VN[pr, :], r=["KD%d_%d" % (pg, t), "VN"], w=["ps%d" % b3])
                    self.STT(SF, SF, EGL[:, half, i, h:h + 1], self.ps[b3][:, 0:128], ALU.mult, ALU.add, r=["SF", "dn_small", "ps%d" % b3], w=["SF"])
                    self.CP(SB_, SF, r=["SF"], w=["SB"], eng="act")
                    yield
            OT4 = ["OT%d" % t for t in range(4)]
            SSO = self.ST[:, 170:174]
            tok4 = slice(i0 * 128, (i0 + 4) * 128)
            self.TT(SQ, OT, OT, ALU.mult, r=OT4, w=["JKb"])
            self.p.op("dve", lambda e, SSO=SSO: e.tensor_reduce(out=SSO, in_=g4(SQ), axis=AX.X, op=ALU.add), ["JKb"], ["ST_sso"])
            self.ACT(SSO, SSO, AF.Sqrt, r=["ST_sso"], w=["ST_sso"], scale=1.0 / 128, bias=self.CST[:, 0:1])
            self.p.op("dve", lambda e, SSO=SSO: e.reciprocal(out=SSO, in_=SSO), ["ST_sso"], ["ST_sso"])
            self.TT(g4(SQ), g4(OT), SSO.unsqueeze(2).to_broadcast([128, 4, 128]), ALU.mult, r=OT4 + ["ST_sso"], w=["JKb"])
            self.TT(g4(OB4), g4(SQ), self.ONG[:].unsqueeze(1).to_broadcast([128, 4, 128]), ALU.mult, r=["JKb", "ONG"], w=["GREP"])
            yield
            bk = self.bank()
            pb = self.ps[bk][:].bitcast(BF16)
            for t in range(4):
                self.TR(pb[:, c4(t)], OB4[:, c4(t)], r=["GREP"], w=["ps%d" % bk])
            self.TT(YHg[pg], pb[:, 0:512], ZS[hp][:, tok4], ALU.mult, r=["ps%d" % bk, "ZS%d" % hp], w=["YH%d" % pg])
            yield
            for t, i in enumerate(range(i0, i0 + 4)):
                for cb in range(2):
                    bo = self.bank()
                    self.MM(self.ps[bo][:], YHg[pg][:, c4(t)], WO[:, cb * 512:(cb + 1) * 512], r=["YH%d" % pg, "WO"], w=["ps%d" % bo])
                    hv = self.H[:, i, cb * 512:(cb + 1) * 512]
                    self.TT(hv, self.ps[bo][:], hv, ALU.add, r=["ps%d" % bo, "H%d" % i], w=["H%d" % i])
                yield

        def alt(ga, gb):
            da = db = False
            while not (da and db):
                if not da:
                    try:
                        next(ga)
                    except StopIteration:
                        da = True
                if not db:
                    try:
                        next(gb)
                    except StopIteration:
                        db = True
                yield

        def Tgen(h):
            self.DMA(WO, self.wb["od_w_out"][h * 128:(h + 1) * 128, :], r=["wb_od_w_out"], w=["WO"])
            self.MS(SF, 0.0, w=["SF"], eng="dve")
            self.MS(SB_, 0.0, w=["SB"], eng="dve")
            for _ in prep(0, h):
                yield
            for g in range(NGRP):
                a_ = prep(g + 1, h) if g + 1 < NGRP else iter(())
                for _ in alt(a_, rec(g, h)):
                    yield

        for _ in Pgen(0):
            pass
        for h in range(8):
            nxt = Pgen(h + 1) if h + 1 < 8 else iter(())
            for _ in alt(Tgen(h), nxt):
                pass
        self.nrot = 5
        self.p.barrier()

    def dump(self, slot, b):
        if self.dbg:
            self.DMA(self.dbg_out[slot, b].rearrange("(n p) d -> p n d", p=128), self.H[:], r=["H%d" % i for i in range(self.NT)], w=["dbg"])

    def build(self):
        NT = self.NT
        self.prologue()
        for b in range(self.NSEQ):
            hk = ["H%d" % i for i in range(NT)]
            step = max(1, NT // 4)
            for i0 in range(0, NT, step):
                self.DMA(self.H[:, i0:i0 + step, :], self.x[b, i0 * 128:(i0 + step) * 128, :].rearrange("(n p) d -> p n d", p=128),
                         w=hk[i0:i0 + step])
            for l in self.layers:
                if l == 0:
                    self.layer0_mixer(b)
                else:
                    self.deltanet(b)
                self.dump(l * 3 + 0, b)
                self.ffn(l)
                self.dump(l * 3 + 1, b)
                self.ple(l, b)
                self.dump(l * 3 + 2, b)
            for i0 in range(0, NT, step):
                self.DMA(self.out[b, i0 * 128:(i0 + step) * 128, :].rearrange("(n p) d -> p n d", p=128), self.H[:, i0:i0 + step, :],
                         r=hk[i0:i0 + step], w=["out"])
            self.p.barrier()
        self.p.emit()
        self.es.close()
        return self.nc


_CACHE = {}


def _get_nc(S, NSEQ, topk):
    key = (S, NSEQ, topk)
    if key not in _CACHE:
        _CACHE[key] = Builder(S, NSEQ, topk).build()
    return _CACHE[key]


def make_in_maps(inputs, ncores):
    x = np.ascontiguousarray(inputs["x"], dtype=np.float32)
    B = x.shape[0]
    nseq = B // ncores
    in_maps = []
    for c in range(ncores):
        sl = slice(c * nseq, (c + 1) * nseq)
        m = {"x": x[sl], "p": np.ascontiguousarray(inputs["p"][:, sl], dtype=np.float32),
             "positions": np.ascontiguousarray(inputs["positions"][sl], dtype=np.int32)}
        for n in ("norm_mix_g", "norm_ffn_g", "ffn_w_gate", "ffn_w_up", "ffn_w_down", "ple_w_proj",
                  "ple_post_norm_g", "ple_norm_g", "ple_w_gate"):
            m[n] = np.ascontiguousarray(inputs[n], dtype=np.float32)
        for n in ("ev_w_in", "ev_w_out", "od_w_in", "od_w_out", "ev_conv_w", "od_conv_w"):
            m[n] = np.ascontiguousarray(inputs[n][0], dtype=np.float32)
        for n in ("ev_q_norm_g", "ev_k_norm_g", "ev_ik_ln_g", "ev_ik_ln_b", "od_a_log", "od_dt_bias", "od_o_norm_g"):
            m[n] = np.ascontiguousarray(inputs[n], dtype=np.float32)
        in_maps.append(m)
    return in_maps


def kernel(**inputs):
    B, S, _ = inputs["x"].shape
    ncores = 8
    nseq = B // ncores
    topk = min(256, S // 4)
    nc = _get_nc(S, nseq, topk)
    in_maps = make_in_maps(inputs, ncores)
    res = run_bass_kernel_spmd(nc, in_maps, core_ids=list(range(ncores)))
    return np.concatenate([np.asarray(r["out"]) for r in res.results], axis=0).astype(np.float32)
```
